# Optimizing a Trainium2 kernel written in Bass

```python
import jax, jax.numpy as jnp
from jax import lax
import numpy as np

D_MODEL = 1024
BATCH = 4
SEQ = 4096
DEPTH = 4

GRID_W = 64
CTX_LEN = 256

MLA_HEADS = 8
Q_LORA = 384
KV_LORA = 256
QK_NOPE = 64
QK_ROPE = 32
V_DIM = 64
ROPE_AXIS_FREQ = QK_ROPE // 4
ROPE_BASE = 10000.0
Q_BLOCK = 128

CONV_WIDTH = 512
CONV_K = 31

FNET_WIDTH = 512
FNET_GROUPS = 4

LRU_WIDTH = 512
LRU_HEADS = 8
LRU_CONV_K = 4
LRU_C = 8.0

N_BRANCH = 4
BRANCH_WIDTH = 512

N_EXPERTS = 16
EXPERT_FF = 1408
EC_CAPACITY = 2

DEEPNORM_ALPHA = (2 * DEPTH) ** 0.25
DEEPNORM_BETA = (8 * DEPTH) ** -0.25
LN_EPS = 1e-6

CTX_STATE_SPLITS = (KV_LORA, QK_ROPE, LRU_WIDTH)
IN_SPLITS = (KV_LORA, QK_ROPE, LRU_WIDTH, Q_LORA, 2 * CONV_WIDTH, FNET_WIDTH, LRU_WIDTH, N_BRANCH * D_MODEL)
CTX_STATE_WIDTH = KV_LORA + QK_ROPE + LRU_WIDTH
IN_WIDTH = CTX_STATE_WIDTH + Q_LORA + 2 * CONV_WIDTH + FNET_WIDTH + LRU_WIDTH + N_BRANCH * D_MODEL

kernel_name = 'hybrid_diffusion_mla_conformer_fnet_rglru_ecmoe'


def split_cols(p, sizes):
    idx, acc = [], 0
    for s in sizes[:-1]:
        acc += s
        idx.append(acc)
    return jnp.split(p, idx, axis=-1)


def layer_norm(x, g=None, b=None):
    xf = x.astype(jnp.float32)
    mu = jnp.mean(xf, axis=-1, keepdims=True)
    var = jnp.mean(jnp.square(xf - mu), axis=-1, keepdims=True)
    y = (xf - mu) * lax.rsqrt(var + LN_EPS)
    if g is not None:
        y = y * g.astype(jnp.float32) + b.astype(jnp.float32)
    return y.astype(x.dtype)


def rms_norm(x, g):
    xf = x.astype(jnp.float32)
    y = xf * lax.rsqrt(jnp.mean(jnp.square(xf), axis=-1, keepdims=True) + LN_EPS) * g.astype(jnp.float32)
    return y.astype(x.dtype)


def modulate(x, shift, scale):
    return layer_norm(x) * (1.0 + scale) + shift


def post_norm(x, out, gate, g, b):
    return layer_norm(DEEPNORM_ALPHA * x + gate * out, g, b)


def rope_tables(rows):
    row = jnp.repeat(jnp.arange(rows), GRID_W)
    col = jnp.tile(jnp.arange(GRID_W), rows)
    inv = ROPE_BASE ** (-jnp.arange(ROPE_AXIS_FREQ, dtype=jnp.float32) / ROPE_AXIS_FREQ)
    ang_r = row.astype(jnp.float32)[:, None, None] * inv
    ang_c = col.astype(jnp.float32)[:, None, None] * inv
    return (jnp.cos(ang_r), jnp.sin(ang_r), jnp.cos(ang_c), jnp.sin(ang_c))


def _rotate(x, cos, sin):
    half = x.shape[-1] // 2
    x1, x2 = x[..., :half], x[..., half:]
    return jnp.concatenate([x1 * cos - x2 * sin, x1 * sin + x2 * cos], axis=-1)


def apply_rope2d(x, tabs):
    cos_r, sin_r, cos_c, sin_c = tabs
    xf = x.astype(jnp.float32)
    r = QK_ROPE // 2
    y = jnp.concatenate([_rotate(xf[..., :r], cos_r, sin_r), _rotate(xf[..., r:], cos_c, sin_c)], axis=-1)
    return y.astype(x.dtype)


def mla_queries(cq_raw, q_norm_g, w_uq, tabs):
    q = jnp.einsum('bsr,rhd->bshd', rms_norm(cq_raw, q_norm_g), w_uq)
    q_nope, q_rope = q[..., :QK_NOPE], q[..., QK_NOPE:]
    if tabs is not None:
        q_rope = apply_rope2d(q_rope, tabs)
    return jnp.concatenate([q_nope, q_rope], axis=-1)


def mla_keys_values(ckv_raw, kr_raw, kv_norm_g, w_ukv, tabs):
    kv = jnp.einsum('bsr,rhd->bshd', rms_norm(ckv_raw, kv_norm_g), w_ukv)
    k_nope, v = kv[..., :QK_NOPE], kv[..., QK_NOPE:]
    k_rope = kr_raw[:, :, None, :]
    if tabs is not None:
        k_rope = apply_rope2d(k_rope, tabs)
    k_rope = jnp.broadcast_to(k_rope, k_nope.shape[:-1] + (QK_ROPE,))
    return jnp.concatenate([k_nope, k_rope], axis=-1), v


def attend(q, k, v):
    scale = (QK_NOPE + QK_ROPE) ** -0.5
    s = jnp.einsum('bqhd,bkhd->bhqk', q, k).astype(jnp.float32) * scale
    p = jax.nn.softmax(s, axis=-1).astype(v.dtype)
    o = jnp.einsum('bhqk,bkhd->bqhd', p, v)
    return o.reshape(o.shape[0], o.shape[1], -1)


def attend_blocked(q, k, v):
    B, S, H, d = q.shape
    nb = S // Q_BLOCK
    qb = jnp.moveaxis(q.reshape(B, nb, Q_BLOCK, H, d), 1, 0)
    o = lax.map(lambda qi: attend(qi, k, v), qb)
    return jnp.moveaxis(o, 0, 1).reshape(B, S, H * V_DIM)


def depthwise_conv(x, w, b, pad):
    y = lax.conv_general_dilated(x, w[:, None, :], window_strides=(1,), padding=[pad],
                                 dimension_numbers=('NWC', 'WIO', 'NWC'),
                                 feature_group_count=x.shape[-1])
    return y + b


def conformer_branch(glu_in, lp):
    a, g = jnp.split(glu_in, 2, axis=-1)
    u = a * jax.nn.sigmoid(g)
    u = depthwise_conv(u, lp['cv_w'], lp['cv_b'], ((CONV_K - 1) // 2, (CONV_K - 1) // 2))
    return jax.nn.silu(layer_norm(u, lp['cv_ln_g'], lp['cv_ln_b']))


def fnet_branch(f):
    B, S, _ = f.shape
    fg = f.astype(jnp.float32).reshape(B, S, FNET_GROUPS, FNET_WIDTH // FNET_GROUPS)
    y = jnp.fft.fft2(fg, axes=(1, 3), norm='ortho').real
    return y.reshape(B, S, FNET_WIDTH).astype(f.dtype)


def _lin_combine(left, right):
    a1, b1 = left
    a2, b2 = right
    return a1 * a2, a2 * b1 + b2


def rglru_scan(xc, wa, ba, wx, bx, lam, h0):
    B, S, _ = xc.shape
    xh = xc.reshape(B, S, LRU_HEADS, LRU_WIDTH // LRU_HEADS)
    r = jax.nn.sigmoid((jnp.einsum('bshi,hij->bshj', xh, wa).reshape(B, S, LRU_WIDTH) + ba).astype(jnp.float32))
    i = jax.nn.sigmoid((jnp.einsum('bshi,hij->bshj', xh, wx).reshape(B, S, LRU_WIDTH) + bx).astype(jnp.float32))
    log_a = -LRU_C * jax.nn.softplus(-lam.astype(jnp.float32)) * r
    a = jnp.exp(log_a)
    u = jnp.sqrt(-jnp.expm1(2.0 * log_a)) * (i * xc.astype(jnp.float32))
    u = u.at[:, 0].add(a[:, 0] * h0)
    _, h = lax.associative_scan(_lin_combine, (a, u), axis=1)
    return h, h[:, -1]


def rglru_bidir(xb, lp, h0_fwd, h0_bwd):
    xc = depthwise_conv(xb, lp['lru_conv_w'], lp['lru_conv_b'], (LRU_CONV_K // 2, LRU_CONV_K - 1 - LRU_CONV_K // 2))
    hf, s_fwd = rglru_scan(xc, lp['lru_wa'][0], lp['lru_ba'][0], lp['lru_wx'][0], lp['lru_bx'][0],
                           lp['lru_lambda'][0], h0_fwd)
    hb, s_bwd = rglru_scan(jnp.flip(xc, 1), lp['lru_wa'][1], lp['lru_ba'][1], lp['lru_wx'][1], lp['lru_bx'][1],
                           lp['lru_lambda'][1], h0_bwd)
    h = (hf + jnp.flip(hb, 1)).astype(xb.dtype)
    return h, s_fwd, s_bwd


def merge_branches(branches, gate_logits, w_branch, w_out):
    stacked = jnp.stack(branches, axis=2)
    proj = jnp.einsum('bskc,kcd->bskd', stacked, w_branch)
    gates = jax.nn.sigmoid(gate_logits.reshape(proj.shape))
    return jnp.sum(gates * proj, axis=2) @ w_out


def mixer_sublayer(x_lat, x_ctx, mod_l, mod_c, lp, tabs, ctx_out):
    shift_l, scale_l, gate_l = mod_l
    B = x_lat.shape[0]
    h_c = modulate(x_ctx, mod_c[0], mod_c[1])
    if ctx_out:
        p_c = split_cols(h_c @ lp['w_in'], IN_SPLITS)
    else:
        p_c = split_cols(h_c @ lp['w_in'][:, :CTX_STATE_WIDTH], CTX_STATE_SPLITS)
    k_c, v_c = mla_keys_values(p_c[0], p_c[1], lp['kv_norm_g'], lp['w_ukv'], None)
    zero_state = jnp.zeros((B, LRU_WIDTH), jnp.float32)
    hrec_c, s_fwd, s_bwd = rglru_bidir(p_c[2], lp, zero_state, zero_state)

    h_l = modulate(x_lat, shift_l, scale_l)
    ckv_l, kr_l, xb_l, cq_l, glu_l, f_l, gb_l, gl_l = split_cols(h_l @ lp['w_in'], IN_SPLITS)
    q_l = mla_queries(cq_l, lp['q_norm_g'], lp['w_uq'], tabs)
    k_l, v_l = mla_keys_values(ckv_l, kr_l, lp['kv_norm_g'], lp['w_ukv'], tabs)
    attn_l = attend_blocked(q_l, jnp.concatenate([k_c, k_l], axis=1), jnp.concatenate([v_c, v_l], axis=1))
    hrec_l, _, _ = rglru_bidir(xb_l, lp, s_fwd, s_bwd)
    branches_l = [attn_l, conformer_branch(glu_l, lp), fnet_branch(f_l), hrec_l * jax.nn.gelu(gb_l)]
    out_l = merge_branches(branches_l, gl_l, lp['w_branch'], lp['w_out'])
    new_lat = post_norm(x_lat, out_l, gate_l, lp['ln1_g'], lp['ln1_b'])
    if not ctx_out:
        return new_lat, None

    cq_c, glu_c, f_c, gb_c, gl_c = p_c[3], p_c[4], p_c[5], p_c[6], p_c[7]
    attn_c = attend(mla_queries(cq_c, lp['q_norm_g'], lp['w_uq'], None), k_c, v_c)
    branches_c = [attn_c, conformer_branch(glu_c, lp), fnet_branch(f_c), hrec_c * jax.nn.gelu(gb_c)]
    out_c = merge_branches(branches_c, gl_c, lp['w_branch'], lp['w_out'])
    new_ctx = post_norm(x_ctx, out_c, mod_c[2], lp['ln1_g'], lp['ln1_b'])
    return new_lat, new_ctx


def ec_moe(h, w_router, w_e_gate, w_e_up, w_e_down):
    B, N, _ = h.shape
    cap = EC_CAPACITY * N // N_EXPERTS
    affinity = jax.nn.softmax((h @ w_router).astype(jnp.float32), axis=-1)
    top_w, top_i = lax.top_k(jnp.swapaxes(affinity, 1, 2), cap)
    b_idx = jnp.arange(B)[:, None, None]
    xe = h[b_idx, top_i]
    he = jax.nn.silu(jnp.einsum('becd,edf->becf', xe, w_e_gate)) * jnp.einsum('becd,edf->becf', xe, w_e_up)
    ye = jnp.einsum('becf,efd->becd', he, w_e_down) * top_w[..., None].astype(h.dtype)
    return jnp.zeros_like(h).at[b_idx, top_i].add(ye)


def moe_sublayer(x, shift, scale, gate, lp):
    h = modulate(x, shift, scale)
    out = ec_moe(h, lp['w_router'], lp['w_e_gate'], lp['w_e_up'], lp['w_e_down'])
    return post_norm(x, out, gate, lp['ln2_g'], lp['ln2_b'])


def setup_inputs(seed: int = 0) -> dict:
    key = jax.random.key(seed)
    ks = jax.random.split(key, 32)
    f32 = jnp.float32
    L = DEPTH
    hd = LRU_WIDTH // LRU_HEADS

    def nrm(k, shape, scale):
        return jax.random.normal(k, shape, f32) * scale

    def gain(k, shape):
        return 1.0 + 0.01 * jax.random.normal(k, shape, f32)

    def bias(k, shape):
        return 0.01 * jax.random.normal(k, shape, f32)

    u = jax.random.uniform(ks[20], (L, 2, LRU_WIDTH), f32, 0.9, 0.999)
    a_base = u ** (1.0 / LRU_C)
    lam = jnp.log(a_base) - jnp.log1p(-a_base)
    return {
        'x': nrm(ks[0], (BATCH, SEQ, D_MODEL), 1.0),
        'c': nrm(ks[1], (BATCH, D_MODEL), 1.0),
        'ctx': nrm(ks[2], (BATCH, CTX_LEN, D_MODEL), 1.0),
        'c_ctx': nrm(ks[3], (D_MODEL,), 1.0),
        'ada_w': nrm(ks[4], (L, D_MODEL, 6 * D_MODEL), 0.5 * D_MODEL ** -0.5),
        'ada_b': bias(ks[5], (L, 6 * D_MODEL)),
        'w_in': nrm(ks[6], (L, D_MODEL, IN_WIDTH), D_MODEL ** -0.5),
        'q_norm_g': gain(ks[7], (L, Q_LORA)),
        'w_uq': nrm(ks[8], (L, Q_LORA, MLA_HEADS, QK_NOPE + QK_ROPE), Q_LORA ** -0.5),
        'kv_norm_g': gain(ks[9], (L, KV_LORA)),
        'w_ukv': nrm(ks[10], (L, KV_LORA, MLA_HEADS, QK_NOPE + V_DIM), KV_LORA ** -0.5),
        'cv_w': nrm(ks[11], (L, CONV_K, CONV_WIDTH), CONV_K ** -0.5),
        'cv_b': bias(ks[12], (L, CONV_WIDTH)),
        'cv_ln_g': gain(ks[13], (L, CONV_WIDTH)),
        'cv_ln_b': bias(ks[14], (L, CONV_WIDTH)),
        'lru_conv_w': nrm(ks[15], (L, LRU_CONV_K, LRU_WIDTH), LRU_CONV_K ** -0.5),
        'lru_conv_b': bias(ks[16], (L, LRU_WIDTH)),
        'lru_wa': nrm(ks[17], (L, 2, LRU_HEADS, hd, hd), hd ** -0.5),
        'lru_ba': bias(ks[18], (L, 2, LRU_WIDTH)),
        'lru_wx': nrm(ks[19], (L, 2, LRU_HEADS, hd, hd), hd ** -0.5),
        'lru_bx': bias(ks[21], (L, 2, LRU_WIDTH)),
        'lru_lambda': lam,
        'w_branch': nrm(ks[22], (L, N_BRANCH, BRANCH_WIDTH, D_MODEL), BRANCH_WIDTH ** -0.5),
        'w_out': nrm(ks[23], (L, D_MODEL, D_MODEL), DEEPNORM_BETA * D_MODEL ** -0.5),
        'ln1_g': gain(ks[24], (L, D_MODEL)),
        'ln1_b': bias(ks[25], (L, D_MODEL)),
        'w_router': nrm(ks[26], (L, D_MODEL, N_EXPERTS), D_MODEL ** -0.5),
        'w_e_gate': nrm(ks[27], (L, N_EXPERTS, D_MODEL, EXPERT_FF), D_MODEL ** -0.5),
        'w_e_up': nrm(ks[28], (L, N_EXPERTS, D_MODEL, EXPERT_FF), D_MODEL ** -0.5),
        'w_e_down': nrm(ks[29], (L, N_EXPERTS, EXPERT_FF, D_MODEL), DEEPNORM_BETA * EXPERT_FF ** -0.5),
        'ln2_g': gain(ks[30], (L, D_MODEL)),
        'ln2_b': bias(ks[31], (L, D_MODEL)),
    }


def reference(x, c, ctx, c_ctx, ada_w, ada_b, w_in, q_norm_g, w_uq, kv_norm_g, w_ukv, cv_w, cv_b, cv_ln_g,
              cv_ln_b, lru_conv_w, lru_conv_b, lru_wa, lru_ba, lru_wx, lru_bx, lru_lambda, w_branch, w_out,
              ln1_g, ln1_b, w_router, w_e_gate, w_e_up, w_e_down, ln2_g, ln2_b):
    ROWS = x.shape[1] // GRID_W
    tabs = rope_tables(ROWS)
    x_lat, x_ctx = x, ctx
    for l in range(DEPTH):
        ctx_out = l < DEPTH - 1
        lp = {
            'w_in': w_in[l], 'q_norm_g': q_norm_g[l], 'w_uq': w_uq[l], 'kv_norm_g': kv_norm_g[l],
            'w_ukv': w_ukv[l], 'cv_w': cv_w[l], 'cv_b': cv_b[l], 'cv_ln_g': cv_ln_g[l], 'cv_ln_b': cv_ln_b[l],
            'lru_conv_w': lru_conv_w[l], 'lru_conv_b': lru_conv_b[l], 'lru_wa': lru_wa[l], 'lru_ba': lru_ba[l],
            'lru_wx': lru_wx[l], 'lru_bx': lru_bx[l], 'lru_lambda': lru_lambda[l], 'w_branch': w_branch[l],
            'w_out': w_out[l], 'ln1_g': ln1_g[l], 'ln1_b': ln1_b[l], 'w_router': w_router[l],
            'w_e_gate': w_e_gate[l], 'w_e_up': w_e_up[l], 'w_e_down': w_e_down[l],
            'ln2_g': ln2_g[l], 'ln2_b': ln2_b[l],
        }
        mod = (jax.nn.silu(c) @ ada_w[l] + ada_b[l])[:, None, :]
        shift1, scale1, gate1, shift2, scale2, gate2 = jnp.split(mod, 6, axis=-1)
        n_mod_c = 6 if ctx_out else 2
        mod_c = jax.nn.silu(c_ctx) @ ada_w[l][:, :n_mod_c * D_MODEL] + ada_b[l][:n_mod_c * D_MODEL]
        mods_c = jnp.split(mod_c, n_mod_c)
        x_lat, x_ctx_new = mixer_sublayer(x_lat, x_ctx, (shift1, scale1, gate1), mods_c, lp, tabs, ctx_out)
        x_lat = moe_sublayer(x_lat, shift2, scale2, gate2, lp)
        if ctx_out:
            x_ctx = moe_sublayer(x_ctx_new, mods_c[3], mods_c[4], mods_c[5], lp)
    return x_lat
```

```python
from contextlib import ExitStack
import math
import numpy as np
import ml_dtypes
import concourse.bass as bass
import concourse.mybir as mybir
from concourse.bass_utils import run_bass_kernel_spmd

F32 = mybir.dt.float32
BF16 = mybir.dt.bfloat16
AF = mybir.ActivationFunctionType
ALU = mybir.AluOpType

DMA_RING = 6

D = 1024
TC = 256
TL = 4096
T = TC + TL
NT = T // 128
DEPTH = 4
IN_W = 7328
NEXP = 16
EFF = 1408
NFC = EFF // 128
ALPHA = (2 * DEPTH) ** 0.25
EPS = 1e-6
GROUPS = [(0, 256)] + [(256 + 512 * i, 512) for i in range(8)]
O_CKV, O_KR, O_XB, O_CQ, O_GLU, O_F, O_GB, O_GL = 0, 256, 288, 800, 1184, 2208, 2720, 3232
NSLOT = 544


class Dep:
    __slots__ = ("w", "r")

    def __init__(self):
        self.w = None
        self.r = {}


class Ctx:
    def __init__(self, nc):
        self.nc = nc
        self.es = ExitStack()
        self.E = {"pe": nc.tensor, "act": nc.scalar, "dve": nc.vector, "pool": nc.gpsimd, "sp": nc.sync}
        self.sem = {}
        self.cnt = {}
        for e in ("pe", "act", "dve", "pool"):
            self.sem[e] = self.es.enter_context(nc.semaphore("s_" + e))
            self.cnt[e] = 0
        self.dma_i = {}
        for q in ("sp", "pool", "act"):
            self.dma_i[q] = 0
            for s in range(DMA_RING):
                k = ("d", q, s)
                self.sem[k] = self.es.enter_context(nc.semaphore("d_%s%d" % (q, s)))
                self.cnt[k] = 0
        self.seen = {e: {} for e in self.E}
        self.n_inst = 0
        self.n_wait = 0
        self.uid = 0

    def close(self):
        self.es.close()

    def _need(self, eng, reads, writes):
        need = {}

        def add(tok):
            if tok is None:
                return
            k, v = tok
            if k == "pe" and eng == "pe":
                return
            if need.get(k, 0) < v:
                need[k] = v

        for d in reads:
            add(d.w)
        for d in writes:
            add(d.w)
            for k, v in d.r.items():
                add((k, v))
        return need

    def _emit_waits(self, eng, need):
        E = self.E[eng]
        seen = self.seen[eng]
        for k, v in need.items():
            if seen.get(k, 0) >= v:
                continue
            E.wait_ge(self.sem[k], v)
            seen[k] = v
            self.n_wait += 1

    def _mark(self, tok, reads, writes):
        k, v = tok
        for d in reads:
            if d.r.get(k, 0) < v:
                d.r[k] = v
        for d in writes:
            d.w = tok
            d.r = {}

    def op(self, eng, fn, reads=(), writes=()):
        need = self._need(eng, reads, writes)
        self._emit_waits(eng, need)
        inst = fn(self.E[eng])
        self.cnt[eng] += 1
        inst.then_inc(self.sem[eng], 1)
        self._mark((eng, self.cnt[eng]), reads, writes)
        self.n_inst += 1
        return inst

    def dma(self, q, out, in_, reads=(), writes=(), **kw):
        E = self.E[q]
        slot = self.dma_i[q] % DMA_RING
        self.dma_i[q] += 1
        k = ("d", q, slot)
        need = self._need(q, reads, writes)
        if self.cnt[k] > 0 and need.get(k, 0) < self.cnt[k]:
            need[k] = self.cnt[k]
        self._emit_waits(q, need)
        inst = E.dma_start(out=out, in_=in_, **kw)
        self.cnt[k] += 16
        inst.then_inc(self.sem[k], 16)
        self._mark((k, self.cnt[k]), reads, writes)
        self.last_tok = (k, self.cnt[k])
        self.n_inst += 1
        return inst

    def idma(self, out, in_, idx_ap, gather, reads=(), writes=(), add=False):
        q = "pool"
        E = self.E[q]
        slot = self.dma_i[q] % DMA_RING
        self.dma_i[q] += 1
        k = ("d", q, slot)
        need = self._need(q, reads, writes)
        if self.cnt[k] > 0 and need.get(k, 0) < self.cnt[k]:
            need[k] = self.cnt[k]
        self._emit_waits(q, need)
        off = bass.IndirectOffsetOnAxis(ap=idx_ap, axis=0)
        kw = {"compute_op": ALU.add} if add else {}
        if gather:
            inst = E.indirect_dma_start(out=out, out_offset=None, in_=in_, in_offset=off, **kw)
        else:
            inst = E.indirect_dma_start(out=out, out_offset=off, in_=in_, in_offset=None, **kw)
        self.cnt[k] += 16
        inst.then_inc(self.sem[k], 16)
        self._mark((k, self.cnt[k]), reads, writes)
        self.last_tok = (k, self.cnt[k])
        self.n_inst += 1
        return inst

    def barrier(self):
        allk = {k: v for k, v in self.cnt.items() if v > 0}
        for e in self.E:
            self._emit_waits(e, dict(allk))


class Buf:
    __slots__ = ("t", "d")

    def __init__(self, t):
        self.t = t
        self.d = Dep()


class Ring:
    def __init__(self, bufs):
        self.bufs = bufs
        self.i = 0

    def next(self):
        b = self.bufs[self.i % len(self.bufs)]
        self.i += 1
        return b


class Phase:
    def __init__(self, c):
        self.c = c
        self.es = ExitStack()

    def sb(self, shape, dt, n=None):
        c = self.c
        out = []
        for _ in range(n or 1):
            c.uid += 1
            out.append(Buf(self.es.enter_context(c.nc.sbuf_tensor("sb%d" % c.uid, list(shape), dt))))
        return out[0] if n is None else Ring(out)

    def ps(self, shape, dt=F32, n=None):
        c = self.c
        out = []
        for _ in range(n or 1):
            c.uid += 1
            out.append(Buf(self.es.enter_context(c.nc.psum_tensor("ps%d" % c.uid, list(shape), dt))))
        return out[0] if n is None else Ring(out)

    def close(self):
        self.c.barrier()
        self.es.close()


def build_program(n_layers=DEPTH, stop_after=None, debug=False):
    nc = bass.Bass("TRN2", target_bir_lowering=False)

    def din(name, shape, dt=F32):
        return nc.dram_tensor(name, list(shape), dt, kind="ExternalInput").ap()

    def dscr(name, shape, dt=F32):
        return nc.dram_tensor(name, list(shape), dt, kind="ExternalOutput" if debug else "Internal").ap()

    L = DEPTH
    x_in = din("x", [TL, D])
    ctx_in = din("ctx", [TC, D])
    cvec_in = din("cvec", [2, D])
    ada_w = din("ada_w", [L, D, 6 * D])
    ada_b = din("ada_b", [L, 6 * D])
    w_in = din("w_in", [L, D, IN_W])
    q_norm_g = din("q_norm_g", [L, 384])
    w_uq = din("w_uq", [L, 384, 768])
    kv_norm_g = din("kv_norm_g", [L, 256])
    w_ukv = din("w_ukv", [L, 256, 1024])
    cv_w = din("cv_w", [L, 31, 512])
    cv_b = din("cv_b", [L, 512])
    cv_ln_g = din("cv_ln_g", [L, 512])
    cv_ln_b = din("cv_ln_b", [L, 512])
    lru_conv_w = din("lru_conv_w", [L, 4, 512])
    lru_conv_b = din("lru_conv_b", [L, 512])
    lru_wa = din("lru_wa", [L, 2, 8, 64, 64])
    lru_ba = din("lru_ba", [L, 2, 512])
    lru_wx = din("lru_wx", [L, 2, 8, 64, 64])
    lru_bx = din("lru_bx", [L, 2, 512])
    lru_lambda = din("lru_lambda", [L, 2, 512])
    w_branch = din("w_branch", [L, 4, 512, D])
    w_out = din("w_out", [L, D, D])
    ln1_g = din("ln1_g", [L, D])
    ln1_b = din("ln1_b", [L, D])
    w_router = din("w_router", [L, D, NEXP])
    w_e_gate = din("w_e_gate", [L, NEXP, D, EFF])
    w_e_up = din("w_e_up", [L, NEXP, D, EFF])
    w_e_down = din("w_e_down", [L, NEXP, EFF, D])
    ln2_g = din("ln2_g", [L, D])
    ln2_b = din("ln2_b", [L, D])
    rope_cos = din("rope_cos", [96, T])
    rope_sin = din("rope_sin", [96, T])
    dft_c = din("dft_c", [TL, TL], BF16)
    dft_s = din("dft_s", [TL, TL], BF16)
    dft_c256 = din("dft_c256", [TC, TC], BF16)
    dft_s256 = din("dft_s256", [TC, TC], BF16)
    ccsc = din("ccsc", [128, 256], BF16)
    out_ap = nc.dram_tensor("out", [TL, D], F32, kind="ExternalOutput").ap()

    X = dscr("X", [T, D])
    MOD = dscr("MOD", [2, 2, 128, 6 * D])
    CKVN = dscr("CKVN", [256, T], BF16)
    CQN = dscr("CQN", [384, T], BF16)
    KT = dscr("KT", [8, 96, T], BF16)
    QT = dscr("QT", [8, 96, T], BF16)
    VV = dscr("VV", [T, 512], BF16)
    XBT = dscr("XBT", [512, T])
    UT = dscr("UT", [512, T])
    FT = dscr("FT", [512, T], BF16)
    GBT = dscr("GBT", [512, T])
    GATES = dscr("GATES", [4 * D, T], BF16)
    BR = dscr("BR", [4, 512, T], BF16)
    H2 = dscr("H2", [T, D], BF16)
    POSD = dscr("POSD", [NT, NEXP * 128])
    YE = dscr("YE", [NEXP, NSLOT, D], BF16)
    YMOE = dscr("YMOE", [T, D])

    c = Ctx(nc)
    c.es.enter_context(nc.allow_non_contiguous_dma(reason="small strided parameter loads"))
    glob = Phase(c)
    idf = glob.sb([128, 128], F32)
    idb = glob.sb([128, 128], BF16)
    ones32 = glob.sb([128, 128], F32)
    onesb = glob.sb([128, 128], BF16)
    c.op("pool", lambda e: e.memset(idf.t[:], 1.0), writes=[idf.d])
    c.op("pool", lambda e: e.affine_select(out=idf.t[:], in_=idf.t[:], pattern=[[-1, 128]], compare_op=ALU.is_equal,
                                           fill=0.0, base=0, channel_multiplier=1), reads=[idf.d], writes=[idf.d])
    c.op("dve", lambda e: e.tensor_copy(out=idb.t[:], in_=idf.t[:]), reads=[idf.d], writes=[idb.d])
    c.op("pool", lambda e: e.memset(ones32.t[:], 1.0), writes=[ones32.d])
    c.op("pool", lambda e: e.memset(onesb.t[:], 1.0), writes=[onesb.d])

    c.dma("sp", X[0:TC, :], ctx_in[:, :])
    for i in range(8):
        c.dma("sp", X[TC + 512 * i:TC + 512 * (i + 1), :], x_in[512 * i:512 * (i + 1), :])
    c.barrier()

    def bcast_rows(ap_row, n=128):
        return ap_row.to_broadcast([n, ap_row.shape[-1]])

    def phase_mod(l, ph):
        cv = ph.sb([128, 2, 8], F32)
        sv = ph.sb([128, 2, 8], F32)
        lh = ph.sb([128, 2, 8, 128], F32)
        brow = ph.sb([1, 6 * D], F32)
        wt = ph.sb([128, 8, 512], F32, n=2)
        pp = ph.ps([128, 512], F32, n=4)
        ob = ph.sb([128, 512], F32, n=3)
        for w in range(2):
            c.dma("pool", cv.t[:, w, :].rearrange("p (k o) -> p k o", o=1),
                  cvec_in[w:w + 1, :].rearrange("o (k p q) -> p (o k) q", p=128, q=1), writes=[cv.d])
        c.dma("pool", brow.t[:], ada_b[l:l + 1, :], writes=[brow.d])
        c.op("act", lambda e: e.activation(out=sv.t[:], in_=cv.t[:], func=AF.Silu), reads=[cv.d], writes=[sv.d])
        for w in range(2):
            for k in range(8):
                c.op("dve", lambda e: e.tensor_copy(out=lh.t[:, w, k, :], in_=sv.t[:, w, k:k + 1].to_broadcast([128, 128])),
                     reads=[sv.d], writes=[lh.d])
        for n in range(12):
            wb = wt.next()
            c.dma("pool", wb.t[:], ada_w[l, :, n * 512:(n + 1) * 512].rearrange("(k p) n -> p k n", p=128),
                  writes=[wb.d])
            for w in range(2):
                p = pp.next()
                for k in range(8):
                    c.op("pe", lambda e: e.matmul(p.t[:], lhsT=lh.t[:, w, k, :], rhs=wb.t[:, k, :], start=(k == 0), stop=False),
                         reads=[lh.d, wb.d], writes=[p.d])
                c.op("pe", lambda e: e.matmul(p.t[:], lhsT=ones32.t[0:1, :], rhs=brow.t[0:1, n * 512:(n + 1) * 512],
                                              start=False, stop=True), reads=[ones32.d, brow.d], writes=[p.d])
                o = ob.next()
                addone = 1.0 if n in (2, 3, 8, 9) else 0.0
                c.op("dve", lambda e: e.tensor_scalar(out=o.t[:], in0=p.t[:], scalar1=addone, scalar2=None, op0=ALU.add),
                     reads=[p.d], writes=[o.d])
                c.dma("sp", MOD[l % 2, w, :, n * 512:(n + 1) * 512], o.t[:], reads=[o.d])
            yield

    def ln_stats(ph, xt, rings):
        st = rings["st"].next()
        mv = rings["mv"].next()
        rs = rings["rs"].next()
        for hh in range(2):
            c.op("dve", lambda e: e.bn_stats(out=st.t[:, hh, :], in_=xt.t[:, hh * 512:(hh + 1) * 512]), reads=[xt.d], writes=[st.d])
        c.op("dve", lambda e: e.bn_aggr(out=mv.t[:], in_=st.t[:].rearrange("p a b -> p (a b)")), reads=[st.d], writes=[mv.d])
        c.op("act", lambda e: e.activation(out=rs.t[:], in_=mv.t[:, 1:2], func=AF.Sqrt, bias=rings["eps"].t[:, 0:1], scale=1.0),
             reads=[mv.d, rings["eps"].d], writes=[rs.d])
        c.op("dve", lambda e: e.reciprocal(out=rs.t[:], in_=rs.t[:]), reads=[rs.d], writes=[rs.d])
        return mv, rs

    def ln_rings(ph):
        r = {"st": ph.sb([128, 2, 6], F32, n=3), "mv": ph.sb([128, 2], F32, n=3), "rs": ph.sb([128, 1], F32, n=3),
             "eps": ph.sb([128, 1], F32)}
        c.op("pool", lambda e: e.memset(r["eps"].t[:], EPS), writes=[r["eps"].d])
        return r

    def phase_lnmod(l, sub, HT=None, AFFT=None, AFFTM=None):
        ph = Phase(c)
        rr = ln_rings(ph)
        xt_r = ph.sb([128, D], F32, n=3)
        hb_r = ph.sb([128, D], BF16, n=2)
        sc = [ph.sb([128, D], F32) for _ in range(2)]
        sh = [ph.sb([128, D], F32) for _ in range(2)]
        o_sh = (0 if sub == 0 else 3) * D
        o_sc = (1 if sub == 0 else 4) * D
        for w in range(2):
            c.dma("pool", sc[w].t[:], MOD[l % 2, w, :, o_sc:o_sc + D], writes=[sc[w].d])
            c.dma("pool", sh[w].t[:], MOD[l % 2, w, :, o_sh:o_sh + D], writes=[sh[w].d])
        if sub == 0:
            ptr = ph.ps([128, 8, 128], BF16, n=3)
        else:
            ptr = ph.ps([128, 8, 128], F32, n=2)
            h32_r = ph.sb([128, 8, 128], F32, n=2)
            wr = ph.sb([128, 8, NEXP], F32)
            c.dma("pool", wr.t[:], w_router[l].rearrange("(k p) e -> p k e", p=128), writes=[wr.d])
            plg = ph.ps([128, NEXP], F32, n=2)
            ex_r = ph.sb([128, NEXP], F32, n=2)
            sm_r = ph.sb([128, 1], F32, n=2)
            pat = ph.ps([NEXP, 128], F32, n=2)
        for i in range(NT):
            w = 1 if i < 2 else 0
            xt = xt_r.next()
            c.dma("pool", xt.t[:], X[i * 128:(i + 1) * 128, :], writes=[xt.d])
            mv, rs = ln_stats(ph, xt, rr)
            c.op("dve", lambda e: e.tensor_scalar(out=xt.t[:], in0=xt.t[:], scalar1=mv.t[:, 0:1], scalar2=rs.t[:, 0:1],
                                                  op0=ALU.subtract, op1=ALU.mult), reads=[xt.d, mv.d, rs.d], writes=[xt.d])
            c.op("dve", lambda e: e.tensor_tensor(out=xt.t[:], in0=xt.t[:], in1=sc[w].t[:], op=ALU.mult),
                 reads=[xt.d, sc[w].d], writes=[xt.d])
            if sub == 0:
                hb = hb_r.next()
                c.op("dve", lambda e: e.tensor_tensor(out=hb.t[:], in0=xt.t[:], in1=sh[w].t[:], op=ALU.add),
                     reads=[xt.d, sh[w].d], writes=[hb.d])
                p = ptr.next()
                for k in range(8):
                    c.op("pe", lambda e: e.transpose(out=p.t[:, k, :], in_=hb.t[:, k * 128:(k + 1) * 128], identity=idb.t[:]),
                         reads=[hb.d, idb.d], writes=[p.d])
                c.op("act", lambda e: e.copy(out=HT.t[:, :, i * 128:(i + 1) * 128], in_=p.t[:]), reads=[p.d], writes=[HT.d])
            else:
                c.op("dve", lambda e: e.tensor_tensor(out=xt.t[:], in0=xt.t[:], in1=sh[w].t[:], op=ALU.add),
                     reads=[xt.d, sh[w].d], writes=[xt.d])
                hb = hb_r.next()
                c.op("act", lambda e: e.copy(out=hb.t[:], in_=xt.t[:]), reads=[xt.d], writes=[hb.d])
                c.dma("sp", H2[i * 128:(i + 1) * 128, :], hb.t[:], reads=[hb.d])
                p = ptr.next()
                for k in range(8):
                    c.op("pe", lambda e: e.transpose(out=p.t[:, k, :], in_=xt.t[:, k * 128:(k + 1) * 128], identity=idf.t[:]),
                         reads=[xt.d, idf.d], writes=[p.d])
                h32 = h32_r.next()
                c.op("act", lambda e: e.copy(out=h32.t[:], in_=p.t[:]), reads=[p.d], writes=[h32.d])
                pl = plg.next()
                for k in range(8):
                    c.op("pe", lambda e: e.matmul(pl.t[:], lhsT=h32.t[:, k, :], rhs=wr.t[:, k, :], start=(k == 0), stop=(k == 7)),
                         reads=[h32.d, wr.d], writes=[pl.d])
                ex = ex_r.next()
                sm = sm_r.next()
                c.op("act", lambda e: e.activation(out=ex.t[:], in_=pl.t[:], func=AF.Exp, accum_out=sm.t[:]),
                     reads=[pl.d], writes=[ex.d, sm.d])
                c.op("dve", lambda e: e.reciprocal(out=sm.t[:], in_=sm.t[:]), reads=[sm.d], writes=[sm.d])
                c.op("dve", lambda e: e.tensor_scalar(out=AFFTM.t[:, i, :], in0=ex.t[:], scalar1=sm.t[:, 0:1], scalar2=None,
                                                      op0=ALU.mult), reads=[ex.d, sm.d], writes=[AFFTM.d])
                pa = pat.next()
                c.op("pe", lambda e: e.transpose(out=pa.t[:], in_=AFFTM.t[:, i, :], identity=idf.t[:]),
                     reads=[AFFTM.d, idf.d], writes=[pa.d])
                c.op("act", lambda e: e.copy(out=AFFT.t[:, i * 128:(i + 1) * 128], in_=pa.t[:]), reads=[pa.d], writes=[AFFT.d])
        ph.close()

    def phase_win(l, HT):
        ph = Phase(c)
        WT = ph.sb([128, 8, 1024], BF16, n=2)
        pj = ph.ps([128, 512], F32, n=6)
        pst = ph.ps([128, 512], F32, n=2)
        t32 = ph.sb([128, 512], F32, n=6)
        tb = ph.sb([128, 512], BF16, n=4)
        cosk = ph.sb([32, T], F32)
        sink = ph.sb([32, T], F32)
        c.dma("pool", cosk.t[:], rope_cos[64:96, :], writes=[cosk.d])
        c.dma("pool", sink.t[:], rope_sin[64:96, :], writes=[sink.d])
        epsb = ph.sb([128, 1], F32)
        c.op("pool", lambda e: e.memset(epsb.t[:], EPS), writes=[epsb.d])

        def load_w(col0, ncols, dst0=0, wb=None):
            if wb is None:
                wb = WT.next()
            c.dma("pool", wb.t[:, :, dst0:dst0 + ncols], w_in[l, :, col0:col0 + ncols].rearrange("(k p) n -> p k n", p=128),
                  writes=[wb.d])
            return wb

        def proj(wb, c0, m, g0, gs):
            p = pj.next()
            for k in range(8):
                c.op("pe", lambda e: e.matmul(p.t[0:m, 0:gs], lhsT=wb.t[:, k, c0:c0 + m], rhs=HT.t[:, k, g0:g0 + gs],
                                              start=(k == 0), stop=(k == 7)), reads=[wb.d, HT.d], writes=[p.d])
            return p

        def rms_section(wb, c0, nch, dst):
            for (g0, gs) in GROUPS:
                ps_ = [proj(wb, c0 + 128 * j, 128, g0, gs) for j in range(nch)]
                pss = pst.next()
                for j in range(nch):
                    sq = t32.next()
                    c.op("act", lambda e: e.activation(out=sq.t[:, 0:gs], in_=ps_[j].t[:, 0:gs], func=AF.Square),
                         reads=[ps_[j].d], writes=[sq.d])
                    c.op("pe", lambda e: e.matmul(pss.t[:, 0:gs], lhsT=ones32.t[:], rhs=sq.t[:, 0:gs], start=(j == 0),
                                                  stop=(j == nch - 1)), reads=[ones32.d, sq.d], writes=[pss.d])
                rs = t32.next()
                c.op("act", lambda e: e.activation(out=rs.t[:, 0:gs], in_=pss.t[:, 0:gs], func=AF.Sqrt, bias=epsb.t[:, 0:1],
                                                   scale=1.0 / (nch * 128)), reads=[pss.d, epsb.d], writes=[rs.d])
                c.op("dve", lambda e: e.reciprocal(out=rs.t[:, 0:gs], in_=rs.t[:, 0:gs]), reads=[rs.d], writes=[rs.d])
                for j in range(nch):
                    o = tb.next()
                    c.op("dve", lambda e: e.tensor_tensor(out=o.t[:, 0:gs], in0=ps_[j].t[:, 0:gs], in1=rs.t[:, 0:gs], op=ALU.mult),
                         reads=[ps_[j].d, rs.d], writes=[o.d])
                    c.dma("sp", dst[j * 128:(j + 1) * 128, g0:g0 + gs], o.t[:, 0:gs], reads=[o.d])

        wa = load_w(O_CKV, 288)
        for (dst, src) in ((0, 8), (8, 0), (16, 24), (24, 16)):
            load_w(O_KR + src, 8, dst0=288 + dst, wb=wa)
        for dst in (0, 16):
            c.op("dve", lambda e: e.tensor_scalar(out=wa.t[:, :, 288 + dst:288 + dst + 8], in0=wa.t[:, :, 288 + dst:288 + dst + 8],
                                                  scalar1=-1.0, scalar2=None, op0=ALU.mult), reads=[wa.d], writes=[wa.d])
        rms_section(wa, 0, 2, CKVN)
        for (g0, gs) in GROUPS:
            pk = proj(wa, 256, 32, g0, gs)
            pr = proj(wa, 288, 32, g0, gs)
            a1 = t32.next()
            a2 = t32.next()
            c.op("dve", lambda e: e.tensor_tensor(out=a1.t[0:32, 0:gs], in0=pk.t[0:32, 0:gs], in1=cosk.t[:, g0:g0 + gs], op=ALU.mult),
                 reads=[pk.d, cosk.d], writes=[a1.d])
            c.op("dve", lambda e: e.tensor_tensor(out=a2.t[0:32, 0:gs], in0=pr.t[0:32, 0:gs], in1=sink.t[:, g0:g0 + gs], op=ALU.mult),
                 reads=[pr.d, sink.d], writes=[a2.d])
            o = tb.next()
            c.op("dve", lambda e: e.tensor_tensor(out=o.t[0:32, 0:gs], in0=a1.t[0:32, 0:gs], in1=a2.t[0:32, 0:gs], op=ALU.add),
                 reads=[a1.d, a2.d], writes=[o.d])
            for h in range(8):
                c.dma("sp", KT[h, 64:96, g0:g0 + gs], o.t[0:32, 0:gs], reads=[o.d])
        wq = load_w(O_CQ, 384)
        rms_section(wq, 0, 3, CQN)
        wx = load_w(O_XB, 512)
        for (g0, gs) in GROUPS:
            for j in range(4):
                p = proj(wx, 128 * j, 128, g0, gs)
                o = t32.next()
                c.op("act", lambda e: e.copy(out=o.t[:, 0:gs], in_=p.t[:, 0:gs]), reads=[p.d], writes=[o.d])
                c.dma("sp", XBT[j * 128:(j + 1) * 128, g0:g0 + gs], o.t[:, 0:gs], reads=[o.d])
        wg = load_w(O_GLU, 1024)
        for (g0, gs) in GROUPS:
            for j in range(4):
                pa = proj(wg, 128 * j, 128, g0, gs)
                pg = proj(wg, 512 + 128 * j, 128, g0, gs)
                sg = t32.next()
                c.op("act", lambda e: e.activation(out=sg.t[:, 0:gs], in_=pg.t[:, 0:gs], func=AF.Sigmoid), reads=[pg.d], writes=[sg.d])
                c.op("dve", lambda e: e.tensor_tensor(out=sg.t[:, 0:gs], in0=pa.t[:, 0:gs], in1=sg.t[:, 0:gs], op=ALU.mult),
                     reads=[pa.d, sg.d], writes=[sg.d])
                c.dma("sp", UT[j * 128:(j + 1) * 128, g0:g0 + gs], sg.t[:, 0:gs], reads=[sg.d])
        wf = load_w(O_F, 512)
        for (g0, gs) in GROUPS:
            for j in range(4):
                p = proj(wf, 128 * j, 128, g0, gs)
                o = tb.next()
                c.op("act", lambda e: e.copy(out=o.t[:, 0:gs], in_=p.t[:, 0:gs]), reads=[p.d], writes=[o.d])
                c.dma("sp", FT[j * 128:(j + 1) * 128, g0:g0 + gs], o.t[:, 0:gs], reads=[o.d])
        wgb = load_w(O_GB, 512)
        for (g0, gs) in GROUPS:
            for j in range(4):
                p = proj(wgb, 128 * j, 128, g0, gs)
                s1 = t32.next()
                c.op("act", lambda e: e.activation(out=s1.t[:, 0:gs], in_=p.t[:, 0:gs], func=AF.Square), reads=[p.d], writes=[s1.d])
                c.op("dve", lambda e: e.tensor_scalar(out=s1.t[:, 0:gs], in0=s1.t[:, 0:gs], scalar1=0.044715, scalar2=1.0,
                                                      op0=ALU.mult, op1=ALU.add), reads=[s1.d], writes=[s1.d])
                c.op("dve", lambda e: e.tensor_tensor(out=s1.t[:, 0:gs], in0=p.t[:, 0:gs], in1=s1.t[:, 0:gs], op=ALU.mult),
                     reads=[p.d, s1.d], writes=[s1.d])
                c.op("act", lambda e: e.activation(out=s1.t[:, 0:gs], in_=s1.t[:, 0:gs], func=AF.Sigmoid, scale=1.5957691216057308),
                     reads=[s1.d], writes=[s1.d])
                c.op("dve", lambda e: e.tensor_tensor(out=s1.t[:, 0:gs], in0=p.t[:, 0:gs], in1=s1.t[:, 0:gs], op=ALU.mult),
                     reads=[p.d, s1.d], writes=[s1.d])
                c.dma("sp", GBT[j * 128:(j + 1) * 128, g0:g0 + gs], s1.t[:, 0:gs], reads=[s1.d])
        for cb in range(4):
            wl = load_w(O_GL + 1024 * cb, 1024)
            for (g0, gs) in GROUPS:
                for j in range(8):
                    p = proj(wl, 128 * j, 128, g0, gs)
                    o = tb.next()
                    c.op("act", lambda e: e.activation(out=o.t[:, 0:gs], in_=p.t[:, 0:gs], func=AF.Sigmoid), reads=[p.d], writes=[o.d])
                    r0 = cb * 1024 + j * 128
                    c.dma("sp", GATES[r0:r0 + 128, g0:g0 + gs], o.t[:, 0:gs], reads=[o.d])
        ph.close()

    def phase_qkv(l):
        ph = Phase(c)
        cqn = ph.sb([128, 3, T], BF16)
        ckn = ph.sb([128, 2, T], BF16)
        cosf = ph.sb([96, T], F32)
        sinf = ph.sb([96, T], F32)
        c.dma("pool", cqn.t[:], CQN.rearrange("(k p) t -> p k t", p=128), writes=[cqn.d])
        c.dma("pool", ckn.t[:], CKVN.rearrange("(k p) t -> p k t", p=128), writes=[ckn.d])
        c.dma("pool", cosf.t[:], rope_cos[:, :], writes=[cosf.d])
        c.dma("pool", sinf.t[:], rope_sin[:, :], writes=[sinf.d])
        wq32 = ph.sb([128, 3, 768], F32)
        gq = ph.sb([128, 3], F32)
        wqb = ph.sb([128, 3, 768], BF16)
        wqr32 = ph.sb([128, 3, 768], F32)
        wqr = ph.sb([128, 3, 768], BF16)
        c.dma("pool", wq32.t[:], w_uq[l].rearrange("(k p) n -> p k n", p=128), writes=[wq32.d])
        c.dma("pool", gq.t[:].rearrange("p (k o) -> p k o", o=1), q_norm_g[l:l + 1, :].rearrange("o (k p q) -> p (o k) q", p=128, q=1),
              writes=[gq.d])
        for k in range(3):
            c.op("dve", lambda e: e.tensor_scalar(out=wq32.t[:, k, :], in0=wq32.t[:, k, :], scalar1=gq.t[:, k:k + 1], scalar2=None,
                                                  op0=ALU.mult), reads=[wq32.d, gq.d], writes=[wq32.d])
        c.op("dve", lambda e: e.tensor_copy(out=wqb.t[:], in_=wq32.t[:]), reads=[wq32.d], writes=[wqb.d])
        c.op("pool", lambda e: e.memset(wqr32.t[:], 0.0), writes=[wqr32.d])
        w4 = wq32.t[:].rearrange("p k (h d) -> p k h d", d=96)
        r4 = wqr32.t[:].rearrange("p k (h d) -> p k h d", d=96)
        for (dst, src, sgn) in ((0, 8, -1.0), (8, 0, 1.0), (16, 24, -1.0), (24, 16, 1.0)):
            for k in range(3):
                c.op("dve", lambda e: e.tensor_scalar(out=r4[:, k, :, 64 + dst:72 + dst], in0=w4[:, k, :, 64 + src:72 + src],
                                                      scalar1=sgn, scalar2=None, op0=ALU.mult), reads=[wq32.d], writes=[wqr32.d])
        c.op("dve", lambda e: e.tensor_copy(out=wqr.t[:], in_=wqr32.t[:]), reads=[wqr32.d], writes=[wqr.d])
        wk32 = ph.sb([128, 2, 1024], F32)
        gk = ph.sb([128, 2], F32)
        wkb = ph.sb([128, 2, 1024], BF16)
        c.dma("pool", wk32.t[:], w_ukv[l].rearrange("(k p) n -> p k n", p=128), writes=[wk32.d])
        c.dma("pool", gk.t[:].rearrange("p (k o) -> p k o", o=1), kv_norm_g[l:l + 1, :].rearrange("o (k p q) -> p (o k) q", p=128, q=1),
              writes=[gk.d])
        for k in range(2):
            c.op("dve", lambda e: e.tensor_scalar(out=wk32.t[:, k, :], in0=wk32.t[:, k, :], scalar1=gk.t[:, k:k + 1], scalar2=None,
                                                  op0=ALU.mult), reads=[wk32.d, gk.d], writes=[wk32.d])
        c.op("dve", lambda e: e.tensor_copy(out=wkb.t[:], in_=wk32.t[:]), reads=[wk32.d], writes=[wkb.d])
        wk4 = wkb.t[:].rearrange("p k (h d) -> p k h d", d=128)
        pq = ph.ps([96, 512], F32, n=4)
        pk = ph.ps([128, 512], F32, n=3)
        t32 = ph.sb([96, 512], F32, n=4)
        ob = ph.sb([128, 512], BF16, n=4)
        for h in range(8):
            for (g0, gs) in GROUPS:
                p1 = pq.next()
                p2 = pq.next()
                for k in range(3):
                    c.op("pe", lambda e: e.matmul(p1.t[:, 0:gs], lhsT=wqb.t[:, k, h * 96:(h + 1) * 96], rhs=cqn.t[:, k, g0:g0 + gs],
                                                  start=(k == 0), stop=(k == 2)), reads=[wqb.d, cqn.d], writes=[p1.d])
                for k in range(3):
                    c.op("pe", lambda e: e.matmul(p2.t[:, 0:gs], lhsT=wqr.t[:, k, h * 96:(h + 1) * 96], rhs=cqn.t[:, k, g0:g0 + gs],
                                                  start=(k == 0), stop=(k == 2)), reads=[wqr.d, cqn.d], writes=[p2.d])
                a1 = t32.next()
                a2 = t32.next()
                c.op("dve", lambda e: e.tensor_tensor(out=a1.t[:, 0:gs], in0=p1.t[:, 0:gs], in1=cosf.t[:, g0:g0 + gs], op=ALU.mult),
                     reads=[p1.d, cosf.d], writes=[a1.d])
                c.op("dve", lambda e: e.tensor_tensor(out=a2.t[:, 0:gs], in0=p2.t[:, 0:gs], in1=sinf.t[:, g0:g0 + gs], op=ALU.mult),
                     reads=[p2.d, sinf.d], writes=[a2.d])
                o = ob.next()
                c.op("dve", lambda e: e.tensor_tensor(out=o.t[0:96, 0:gs], in0=a1.t[:, 0:gs], in1=a2.t[:, 0:gs], op=ALU.add),
                     reads=[a1.d, a2.d], writes=[o.d])
                c.dma("sp", QT[h, :, g0:g0 + gs], o.t[0:96, 0:gs], reads=[o.d])
                p3 = pk.next()
                for k in range(2):
                    c.op("pe", lambda e: e.matmul(p3.t[:, 0:gs], lhsT=wkb.t[:, k, h * 128:h * 128 + 128], rhs=ckn.t[:, k, g0:g0 + gs],
                                                  start=(k == 0), stop=(k == 1)), reads=[wkb.d, ckn.d], writes=[p3.d])
                o2 = ob.next()
                c.op("act", lambda e: e.copy(out=o2.t[0:64, 0:gs], in_=p3.t[0:64, 0:gs]), reads=[p3.d], writes=[o2.d])
                c.dma("sp", KT[h, 0:64, g0:g0 + gs], o2.t[0:64, 0:gs], reads=[o2.d])
        for i in range(NT):
            p3 = pk.next()
            for k in range(2):
                c.op("pe", lambda e: e.matmul(p3.t[:].rearrange("p (h d) -> p h d", d=64), lhsT=ckn.t[:, k, i * 128:(i + 1) * 128],
                                              rhs=wk4[:, k, :, 64:128], start=(k == 0), stop=(k == 1)), reads=[wkb.d, ckn.d], writes=[p3.d])
            o2 = ob.next()
            c.op("act", lambda e: e.copy(out=o2.t[:], in_=p3.t[:]), reads=[p3.d], writes=[o2.d])
            c.dma("sp", VV[i * 128:(i + 1) * 128, :], o2.t[:], reads=[o2.d])
        ph.close()

    def phase_attn(l, ph):
        kt_r = ph.sb([96, T], BF16, n=2)
        qt_r = ph.sb([96, T], BF16, n=2)
        v_r = ph.sb([128, NT, 128], BF16, n=2)
        for vb in v_r.bufs:
            c.op("dve", lambda e: e.memset(vb.t[:], 1.0), writes=[vb.d])
        shf = ph.sb([128, 128], F32)
        c.op("pool", lambda e: e.memset(shf.t[:], 1.0), writes=[shf.d])
        c.op("pool", lambda e: e.affine_select(out=shf.t[:], in_=shf.t[:], pattern=[[-1, 128]], compare_op=ALU.is_equal,
                                               fill=0.0, base=-64, channel_multiplier=1), reads=[shf.d], writes=[shf.d])
        ps_s = ph.ps([128, 512], F32, n=4)
        ps_o = ph.ps([128, 512], F32, n=1)
        ps_z = ph.ps([128, 512], F32, n=1)
        pt_r = ph.sb([128, 512], BF16, n=8)
        oz_r = ph.sb([128, 512], F32, n=2)
        ob_r = ph.sb([64, 512], BF16, n=2)
        scale = 96.0 ** -0.5
        for h in range(8):
            kt = kt_r.next()
            qt = qt_r.next()
            vv = v_r.next()
            c.dma("pool", kt.t[:], KT[h], writes=[kt.d])
            c.dma("pool", qt.t[:], QT[h], writes=[qt.d])
            c.dma("pool", vv.t[:, :, 0:64], VV[:, h * 64:(h + 1) * 64].rearrange("(i p) d -> p i d", p=128), writes=[vv.d])
            for gi, (g0, gs) in enumerate(GROUPS):
                nk = 2 if gi == 0 else NT
                po = ps_o.next()
                LOOK = 3
                pts = {}

                def issue_s(kk):
                    s = ps_s.next()
                    c.op("pe", lambda e: e.matmul(s.t[:, 0:gs], lhsT=kt.t[:, kk * 128:(kk + 1) * 128], rhs=qt.t[:, g0:g0 + gs],
                                                  start=True, stop=True), reads=[kt.d, qt.d], writes=[s.d])
                    pt = pt_r.next()
                    c.op("act", lambda e: e.activation(out=pt.t[:, 0:gs], in_=s.t[:, 0:gs], func=AF.Exp, scale=scale),
                         reads=[s.d], writes=[pt.d])
                    pts[kk] = pt

                for kk in range(min(LOOK, nk)):
                    issue_s(kk)
                for kk in range(nk):
                    if kk + LOOK < nk:
                        issue_s(kk + LOOK)
                    pt = pts.pop(kk)
                    c.op("pe", lambda e: e.matmul(po.t[:, 0:gs], lhsT=vv.t[:, kk, :], rhs=pt.t[:, 0:gs], start=(kk == 0),
                                                  stop=(kk == nk - 1)), reads=[vv.d, pt.d], writes=[po.d])
                    yield
                oz = oz_r.next()
                c.op("dve", lambda e: e.tensor_copy(out=oz.t[0:64, 0:gs], in_=po.t[0:64, 0:gs]), reads=[po.d], writes=[oz.d])
                c.op("dve", lambda e: e.reciprocal(out=oz.t[64:128, 0:gs], in_=po.t[64:128, 0:gs]), reads=[po.d], writes=[oz.d])
                pz = ps_z.next()
                c.op("pe", lambda e: e.matmul(pz.t[:, 0:gs], lhsT=shf.t[:], rhs=oz.t[:, 0:gs], start=True, stop=True),
                     reads=[shf.d, oz.d], writes=[pz.d])
                o = ob_r.next()
                c.op("dve", lambda e: e.tensor_tensor(out=o.t[:, 0:gs], in0=pz.t[0:64, 0:gs], in1=oz.t[0:64, 0:gs], op=ALU.mult),
                     reads=[pz.d, oz.d], writes=[o.d])
                c.dma("sp", BR[0, h * 64:(h + 1) * 64, g0:g0 + gs], o.t[:, 0:gs], reads=[o.d])

    def phase_lru(l, ph):
        NP = T + 3
        xbp = ph.sb([128, T + 6], F32)
        xc = ph.sb([128, NP], F32)
        xcb = ph.sb([128, NP], BF16)
        a_t = ph.sb([128, NP], F32)
        u_t = ph.sb([128, NP], F32)
        hf = ph.sb([128, NP], F32)
        hb = xbp
        gbt = a_t
        cw = ph.sb([128, 4], F32)
        cb = ph.sb([128, 1], F32)
        prm = ph.sb([128, 2, 3], F32)
        cst = ph.sb([128, 2], F32)
        w32 = ph.sb([128, 4, 128], F32)
        wbd = ph.sb([128, 4, 128], BF16)
        pg = ph.ps([128, 512], F32, n=2)
        tmp = ph.sb([128, 512], F32, n=4)
        ob = xcb
        PG = [(i * 512, min(512, NP - i * 512)) for i in range((NP + 511) // 512)]
        for cc in range(4):
            ch = slice(cc * 128, (cc + 1) * 128)
            c.op("pool", lambda e: e.memset(xbp.t[:], 0.0), writes=[xbp.d])
            c.dma("pool", xbp.t[:, 2:2 + TC], XBT[ch, 0:TC], writes=[xbp.d])
            c.dma("pool", xbp.t[:, 261:261 + TL], XBT[ch, TC:T], writes=[xbp.d])
            c.dma("pool", cw.t[:].rearrange("p (j o) -> p j o", o=1), lru_conv_w[l, :, ch].rearrange("j (p o) -> p j o", o=1), writes=[cw.d])
            c.dma("pool", cb.t[:], lru_conv_b[l:l + 1, ch].rearrange("o (p q) -> p (o q)", q=1), writes=[cb.d])
            for d in range(2):
                c.dma("pool", prm.t[:, d, 0:1], lru_ba[l, d:d + 1, ch].rearrange("o (p q) -> p (o q)", q=1), writes=[prm.d])
                c.dma("pool", prm.t[:, d, 1:2], lru_bx[l, d:d + 1, ch].rearrange("o (p q) -> p (o q)", q=1), writes=[prm.d])
                c.dma("pool", prm.t[:, d, 2:3], lru_lambda[l, d:d + 1, ch].rearrange("o (p q) -> p (o q)", q=1), writes=[prm.d])
            c.op("pool", lambda e: e.memset(w32.t[:], 0.0), writes=[w32.d])
            for d in range(2):
                for hh in range(2):
                    c.dma("pool", w32.t[hh * 64:(hh + 1) * 64, d * 2 + 0, hh * 64:(hh + 1) * 64], lru_wa[l, d, cc * 2 + hh], writes=[w32.d])
                    c.dma("pool", w32.t[hh * 64:(hh + 1) * 64, d * 2 + 1, hh * 64:(hh + 1) * 64], lru_wx[l, d, cc * 2 + hh], writes=[w32.d])
            c.op("dve", lambda e: e.tensor_copy(out=wbd.t[:], in_=w32.t[:]), reads=[w32.d], writes=[wbd.d])
            for d in range(2):
                c.op("act", lambda e: e.activation(out=cst.t[:, d:d + 1], in_=prm.t[:, d, 2:3], func=AF.Exp, scale=-1.0),
                     reads=[prm.d], writes=[cst.d])
            c.op("dve", lambda e: e.tensor_scalar(out=cst.t[:], in0=cst.t[:], scalar1=1.0, scalar2=None, op0=ALU.add),
                 reads=[cst.d], writes=[cst.d])
            c.op("act", lambda e: e.activation(out=cst.t[:], in_=cst.t[:], func=AF.Ln), reads=[cst.d], writes=[cst.d])
            c.op("dve", lambda e: e.tensor_scalar(out=cst.t[:], in0=cst.t[:], scalar1=-8.0, scalar2=None, op0=ALU.mult),
                 reads=[cst.d], writes=[cst.d])
            c.op("dve", lambda e: e.tensor_scalar(out=xc.t[:], in0=xbp.t[:, 0:NP], scalar1=cw.t[:, 0:1], scalar2=cb.t[:, 0:1],
                                                  op0=ALU.mult, op1=ALU.add), reads=[xbp.d, cw.d, cb.d], writes=[xc.d])
            for j in range(1, 4):
                c.op("dve", lambda e: e.scalar_tensor_tensor(out=xc.t[:], in0=xbp.t[:, j:j + NP], scalar=cw.t[:, j:j + 1], in1=xc.t[:],
                                                             op0=ALU.mult, op1=ALU.add), reads=[xbp.d, cw.d, xc.d], writes=[xc.d])
            c.op("act", lambda e: e.copy(out=xcb.t[:], in_=xc.t[:]), reads=[xc.d], writes=[xcb.d])
            for d in range(2):
                for (p0, gs) in PG:
                    pr = pg.next()
                    pi = pg.next()
                    c.op("pe", lambda e: e.matmul(pr.t[:, 0:gs], lhsT=wbd.t[:, d * 2, :], rhs=xcb.t[:, p0:p0 + gs], start=True, stop=True),
                         reads=[wbd.d, xcb.d], writes=[pr.d])
                    c.op("pe", lambda e: e.matmul(pi.t[:, 0:gs], lhsT=wbd.t[:, d * 2 + 1, :], rhs=xcb.t[:, p0:p0 + gs], start=True, stop=True),
                         reads=[wbd.d, xcb.d], writes=[pi.d])
                    r = tmp.next()
                    ig = tmp.next()
                    c.op("act", lambda e: e.activation(out=r.t[:, 0:gs], in_=pr.t[:, 0:gs], func=AF.Sigmoid, bias=prm.t[:, d, 0:1], scale=1.0),
                         reads=[pr.d, prm.d], writes=[r.d])
                    c.op("act", lambda e: e.activation(out=ig.t[:, 0:gs], in_=pi.t[:, 0:gs], func=AF.Sigmoid, bias=prm.t[:, d, 1:2], scale=1.0),
                         reads=[pi.d, prm.d], writes=[ig.d])
                    c.op("act", lambda e: e.activation(out=a_t.t[:, p0:p0 + gs], in_=r.t[:, 0:gs], func=AF.Exp, scale=cst.t[:, d:d + 1]),
                         reads=[r.d, cst.d], writes=[a_t.d])
                    c.op("dve", lambda e: e.tensor_tensor(out=r.t[:, 0:gs], in0=a_t.t[:, p0:p0 + gs], in1=a_t.t[:, p0:p0 + gs], op=ALU.mult),
                         reads=[a_t.d], writes=[r.d])
                    c.op("dve", lambda e: e.tensor_scalar(out=r.t[:, 0:gs], in0=r.t[:, 0:gs], scalar1=-1.0, scalar2=1.0, op0=ALU.mult,
                                                          op1=ALU.add), reads=[r.d], writes=[r.d])
                    c.op("act", lambda e: e.activation(out=r.t[:, 0:gs], in_=r.t[:, 0:gs], func=AF.Sqrt), reads=[r.d], writes=[r.d])
                    c.op("dve", lambda e: e.tensor_tensor(out=r.t[:, 0:gs], in0=r.t[:, 0:gs], in1=ig.t[:, 0:gs], op=ALU.mult),
                         reads=[r.d, ig.d], writes=[r.d])
                    c.op("dve", lambda e: e.tensor_tensor(out=u_t.t[:, p0:p0 + gs], in0=r.t[:, 0:gs], in1=xc.t[:, p0:p0 + gs], op=ALU.mult),
                         reads=[r.d, xc.d], writes=[u_t.d])
                    yield
                if d == 0:
                    c.op("dve", lambda e: e.tensor_tensor_scan(out=hf.t[:, 0:TC], data0=a_t.t[:, 0:TC], data1=u_t.t[:, 0:TC], initial=0.0,
                                                               op0=ALU.mult, op1=ALU.add), reads=[a_t.d, u_t.d], writes=[hf.d])
                    c.op("dve", lambda e: e.tensor_tensor_scan(out=hf.t[:, 259:259 + TL], data0=a_t.t[:, 259:259 + TL],
                                                               data1=u_t.t[:, 259:259 + TL], initial=hf.t[:, TC - 1:TC],
                                                               op0=ALU.mult, op1=ALU.add), reads=[a_t.d, u_t.d, hf.d], writes=[hf.d])
                else:
                    c.op("dve", lambda e: e.tensor_tensor_scan(out=hb.t[:, 0:TC][:, ::-1], data0=a_t.t[:, 0:TC][:, ::-1],
                                                               data1=u_t.t[:, 0:TC][:, ::-1], initial=0.0,
                                                               op0=ALU.mult, op1=ALU.add), reads=[a_t.d, u_t.d], writes=[hb.d])
                    c.op("dve", lambda e: e.tensor_tensor_scan(out=hb.t[:, 259:259 + TL][:, ::-1], data0=a_t.t[:, 259:259 + TL][:, ::-1],
                                                               data1=u_t.t[:, 259:259 + TL][:, ::-1], initial=hb.t[:, 0:1],
                                                               op0=ALU.mult, op1=ALU.add), reads=[a_t.d, u_t.d, hb.d], writes=[hb.d])
            c.dma("pool", gbt.t[:, 0:TC], GBT[ch, 0:TC], writes=[gbt.d])
            c.dma("pool", gbt.t[:, 259:259 + TL], GBT[ch, TC:T], writes=[gbt.d])
            for (a0, n) in ((0, TC), (259, TL)):
                c.op("dve", lambda e: e.tensor_tensor(out=hf.t[:, a0:a0 + n], in0=hf.t[:, a0:a0 + n], in1=hb.t[:, a0:a0 + n], op=ALU.add),
                     reads=[hf.d, hb.d], writes=[hf.d])
                c.op("dve", lambda e: e.tensor_tensor(out=ob.t[:, a0:a0 + n], in0=hf.t[:, a0:a0 + n], in1=gbt.t[:, a0:a0 + n], op=ALU.mult),
                     reads=[hf.d, gbt.d], writes=[ob.d])
            c.dma("sp", BR[3, ch, 0:TC], ob.t[:, 0:TC], reads=[ob.d])
            c.dma("sp", BR[3, ch, TC:T], ob.t[:, 259:259 + TL], reads=[ob.d])
            yield

    def phase_conf(l, ph):
        cw = ph.sb([128, 4, 31], F32)
        cbias = ph.sb([128, 4], F32)
        lg = ph.sb([128, 4], F32)
        lb = ph.sb([128, 4], F32)
        for cc in range(4):
            ch = slice(cc * 128, (cc + 1) * 128)
            c.dma("pool", cw.t[:, cc, :].rearrange("p (j o) -> p j o", o=1), cv_w[l, :, ch].rearrange("j (p o) -> p j o", o=1), writes=[cw.d])
            c.dma("pool", cbias.t[:, cc:cc + 1], cv_b[l:l + 1, ch].rearrange("o (p q) -> p (o q)", q=1), writes=[cbias.d])
            c.dma("pool", lg.t[:, cc:cc + 1], cv_ln_g[l:l + 1, ch].rearrange("o (p q) -> p (o q)", q=1), writes=[lg.d])
            c.dma("pool", lb.t[:, cc:cc + 1], cv_ln_b[l:l + 1, ch].rearrange("o (p q) -> p (o q)", q=1), writes=[lb.d])
        epsb = ph.sb([128, 1], F32)
        c.op("pool", lambda e: e.memset(epsb.t[:], EPS), writes=[epsb.d])
        in_r = ph.sb([128, 542], F32, n=8)
        acc_r = ph.sb([128, 512], F32, n=8)
        sq_r = ph.sb([128, 512], F32, n=4)
        ps1 = ph.ps([128, 512], F32, n=1)
        ps2 = ph.ps([128, 512], F32, n=1)
        mean_r = ph.sb([128, 512], F32, n=2)
        rstd_r = ph.sb([128, 512], F32, n=2)
        ob_r = ph.sb([128, 512], BF16, n=4)
        for gi, (g0, gs) in enumerate(GROUPS):
            ins = []
            accs = []
            for cc in range(4):
                ch = slice(cc * 128, (cc + 1) * 128)
                it = in_r.next()
                if gi == 0:
                    c.op("pool", lambda e: e.memset(it.t[:], 0.0), writes=[it.d])
                    c.dma("pool", it.t[:, 15:15 + TC], UT[ch, 0:TC], writes=[it.d])
                else:
                    lo = g0 - 15
                    hi = g0 + gs + 15
                    if lo < TC or hi > T:
                        c.op("pool", lambda e: e.memset(it.t[:], 0.0), writes=[it.d])
                    lo_c = max(lo, TC)
                    hi_c = min(hi, T)
                    c.dma("pool", it.t[:, lo_c - lo:hi_c - lo], UT[ch, lo_c:hi_c], writes=[it.d])
                ins.append(it)
                accs.append(acc_r.next())
            for j in range(31):
                for cc in range(4):
                    it, ac = ins[cc], accs[cc]
                    if j == 0:
                        c.op("dve", lambda e: e.tensor_scalar(out=ac.t[:, 0:gs], in0=it.t[:, 0:gs], scalar1=cw.t[:, cc, 0:1],
                                                              scalar2=cbias.t[:, cc:cc + 1], op0=ALU.mult, op1=ALU.add),
                             reads=[it.d, cw.d, cbias.d], writes=[ac.d])
                    else:
                        c.op("dve", lambda e: e.scalar_tensor_tensor(out=ac.t[:, 0:gs], in0=it.t[:, j:j + gs], scalar=cw.t[:, cc, j:j + 1],
                                                                     in1=ac.t[:, 0:gs], op0=ALU.mult, op1=ALU.add),
                             reads=[it.d, cw.d, ac.d], writes=[ac.d])
                yield
            p1 = ps1.next()
            p2 = ps2.next()
            for cc in range(4):
                sq = sq_r.next()
                c.op("act", lambda e: e.activation(out=sq.t[:, 0:gs], in_=accs[cc].t[:, 0:gs], func=AF.Square), reads=[accs[cc].d], writes=[sq.d])
                c.op("pe", lambda e: e.matmul(p1.t[:, 0:gs], lhsT=ones32.t[:], rhs=accs[cc].t[:, 0:gs], start=(cc == 0), stop=(cc == 3)),
                     reads=[ones32.d, accs[cc].d], writes=[p1.d])
                c.op("pe", lambda e: e.matmul(p2.t[:, 0:gs], lhsT=ones32.t[:], rhs=sq.t[:, 0:gs], start=(cc == 0), stop=(cc == 3)),
                     reads=[ones32.d, sq.d], writes=[p2.d])
            mean = mean_r.next()
            rstd = rstd_r.next()
            c.op("act", lambda e: e.activation(out=mean.t[:, 0:gs], in_=p1.t[:, 0:gs], func=AF.Copy, scale=1.0 / 512), reads=[p1.d], writes=[mean.d])
            c.op("dve", lambda e: e.tensor_tensor(out=rstd.t[:, 0:gs], in0=mean.t[:, 0:gs], in1=mean.t[:, 0:gs], op=ALU.mult),
                 reads=[mean.d], writes=[rstd.d])
            c.op("dve", lambda e: e.scalar_tensor_tensor(out=rstd.t[:, 0:gs], in0=p2.t[:, 0:gs], scalar=1.0 / 512, in1=rstd.t[:, 0:gs],
                                                         op0=ALU.mult, op1=ALU.subtract), reads=[p2.d, rstd.d], writes=[rstd.d])
            c.op("act", lambda e: e.activation(out=rstd.t[:, 0:gs], in_=rstd.t[:, 0:gs], func=AF.Sqrt, bias=epsb.t[:, 0:1], scale=1.0),
                 reads=[rstd.d, epsb.d], writes=[rstd.d])
            c.op("dve", lambda e: e.reciprocal(out=rstd.t[:, 0:gs], in_=rstd.t[:, 0:gs]), reads=[rstd.d], writes=[rstd.d])
            for cc in range(4):
                ac = accs[cc]
                c.op("dve", lambda e: e.tensor_tensor(out=ac.t[:, 0:gs], in0=ac.t[:, 0:gs], in1=mean.t[:, 0:gs], op=ALU.subtract),
                     reads=[ac.d, mean.d], writes=[ac.d])
                c.op("dve", lambda e: e.tensor_tensor(out=ac.t[:, 0:gs], in0=ac.t[:, 0:gs], in1=rstd.t[:, 0:gs], op=ALU.mult),
                     reads=[ac.d, rstd.d], writes=[ac.d])
                o = ob_r.next()
                c.op("act", lambda e: e.activation(out=o.t[:, 0:gs], in_=ac.t[:, 0:gs], func=AF.Silu, bias=lb.t[:, cc:cc + 1],
                                                   scale=lg.t[:, cc:cc + 1]), reads=[ac.d, lb.d, lg.d], writes=[o.d])
                c.dma("sp", BR[1, cc * 128:(cc + 1) * 128, g0:g0 + gs], o.t[:, 0:gs], reads=[o.d])

    def phase_fnet(l, ph):
        AB = ph.sb([128, NT, 4, 256], BF16)
        ft_r = ph.sb([128, T], BF16, n=2)
        cs = ph.sb([128, 256], BF16)
        c.dma("pool", cs.t[:], ccsc[:, :], writes=[cs.d])
        pp = ph.ps([128, 512], F32, n=6)
        for g in range(4):
            ft = ft_r.next()
            c.dma("pool", ft.t[:], FT[g * 128:(g + 1) * 128, :], writes=[ft.d])
            for i in range(NT):
                if i % 4 == 0:
                    yield
                p = pp.next()
                c.op("pe", lambda e: e.matmul(p.t[:, 0:256], lhsT=ft.t[:, i * 128:(i + 1) * 128], rhs=cs.t[:], start=True, stop=True),
                     reads=[ft.d, cs.d], writes=[p.d])
                c.op("act" if g % 2 else "dve",
                     (lambda e: e.copy(out=AB.t[:, i, g, :], in_=p.t[:, 0:256])) if g % 2 else
                     (lambda e: e.tensor_copy(out=AB.t[:, i, g, :], in_=p.t[:, 0:256])), reads=[p.d], writes=[AB.d])
        cblk = ph.sb([128, 512], BF16, n=4)
        sblk = ph.sb([128, 512], BF16, n=4)
        ob_r = ph.sb([128, 512], BF16, n=4)
        for gi, (g0, gs) in enumerate(GROUPS):
            if gi == 0:
                tiles = [0, 1]
                cm, sm, col0, nrm = dft_c256, dft_s256, 0, 1.0 / math.sqrt(TC * 128)
            else:
                tiles = list(range(2, NT))
                cm, sm, col0, nrm = dft_c, dft_s, g0 - TC, 1.0 / math.sqrt(TL * 128)
            ps_ = [pp.next() for _ in range(4)]
            for si, i in enumerate(tiles):
                cb_ = cblk.next()
                sb_ = sblk.next()
                c.dma("pool", cb_.t[:, 0:gs], cm[si * 128:(si + 1) * 128, col0:col0 + gs], writes=[cb_.d])
                c.dma("pool", sb_.t[:, 0:gs], sm[si * 128:(si + 1) * 128, col0:col0 + gs], writes=[sb_.d])
                for g in range(4):
                    c.op("pe", lambda e: e.matmul(ps_[g].t[:, 0:gs], lhsT=AB.t[:, i, g, 0:128], rhs=cb_.t[:, 0:gs], start=(si == 0), stop=False),
                         reads=[AB.d, cb_.d], writes=[ps_[g].d])
                    c.op("pe", lambda e: e.matmul(ps_[g].t[:, 0:gs], lhsT=AB.t[:, i, g, 128:256], rhs=sb_.t[:, 0:gs], start=False,
                                                  stop=(si == len(tiles) - 1)), reads=[AB.d, sb_.d], writes=[ps_[g].d])
                yield
            for g in range(4):
                o = ob_r.next()
                c.op("act", lambda e: e.activation(out=o.t[:, 0:gs], in_=ps_[g].t[:, 0:gs], func=AF.Copy, scale=nrm), reads=[ps_[g].d], writes=[o.d])
                c.dma("sp", BR[2, g * 128:(g + 1) * 128, g0:g0 + gs], o.t[:, 0:gs], reads=[o.d])

    def post_norm_tile(i, src, src_reads, gate, lng, lnb, rr, x_r, t_r, final=False):
        xt = x_r.next()
        c.dma("pool", xt.t[:], X[i * 128:(i + 1) * 128, :], writes=[xt.d])
        tt = t_r.next()
        c.op("dve", lambda e: e.tensor_tensor(out=tt.t[:], in0=src, in1=gate.t[:], op=ALU.mult), reads=list(src_reads) + [gate.d], writes=[tt.d])
        c.op("dve", lambda e: e.scalar_tensor_tensor(out=tt.t[:], in0=xt.t[:], scalar=ALPHA, in1=tt.t[:], op0=ALU.mult, op1=ALU.add),
             reads=[xt.d, tt.d], writes=[tt.d])
        mv, rs = ln_stats(None, tt, rr)
        c.op("dve", lambda e: e.tensor_scalar(out=tt.t[:], in0=tt.t[:], scalar1=mv.t[:, 0:1], scalar2=rs.t[:, 0:1], op0=ALU.subtract,
                                              op1=ALU.mult), reads=[tt.d, mv.d, rs.d], writes=[tt.d])
        c.op("dve", lambda e: e.tensor_tensor(out=tt.t[:], in0=tt.t[:], in1=lng.t[:], op=ALU.mult), reads=[tt.d, lng.d], writes=[tt.d])
        c.op("dve", lambda e: e.tensor_tensor(out=tt.t[:], in0=tt.t[:], in1=lnb.t[:], op=ALU.add), reads=[tt.d, lnb.d], writes=[tt.d])
        c.dma("sp", X[i * 128:(i + 1) * 128, :], tt.t[:], reads=[tt.d])

    def load_pn_consts(ph, l, gate_off, g_ap, b_ap):
        gate = [ph.sb([128, D], F32) for _ in range(2)]
        for w in range(2):
            c.dma("pool", gate[w].t[:], MOD[l % 2, w, :, gate_off:gate_off + D], writes=[gate[w].d])
        lng = ph.sb([128, D], F32)
        lnb = ph.sb([128, D], F32)
        c.dma("pool", lng.t[:], bcast_rows(g_ap[l:l + 1, :]), writes=[lng.d])
        c.dma("pool", lnb.t[:], bcast_rows(b_ap[l:l + 1, :]), writes=[lnb.d])
        return gate, lng, lnb

    def phase_merge(l):
        ph = Phase(c)
        wb = ph.sb([128, 16, D], BF16)
        wo = ph.sb([128, 8, D], BF16)
        for k in range(4):
            c.dma("pool", wb.t[:, k * 4:(k + 1) * 4, :], w_branch[l, k].rearrange("(cc p) n -> p cc n", p=128), writes=[wb.d])
        c.dma("pool", wo.t[:], w_out[l].rearrange("(k p) n -> p k n", p=128), writes=[wo.d])
        gate, lng, lnb = load_pn_consts(ph, l, 2 * D, ln1_g, ln1_b)
        rr = ln_rings(ph)
        x_r = ph.sb([128, D], F32, n=2)
        t_r = ph.sb([128, D], F32, n=2)
        br_r = ph.sb([128, 16, 512], BF16, n=2)
        gt_r = ph.sb([128, 8, 512], BF16, n=4)
        mT_r = ph.sb([128, 8, 512], BF16, n=2)
        tmp_r = ph.sb([128, 512], BF16, n=6)
        pp = ph.ps([128, 512], F32, n=3)
        pacc = ph.ps([128, 512], F32, n=2)
        po = ph.ps([128, D], F32, n=1)
        deferred = []
        for gi, (g0, gs) in enumerate(GROUPS):
            br = br_r.next()
            for k in range(4):
                c.dma("pool", br.t[:, k * 4:(k + 1) * 4, 0:gs], BR[k, :, g0:g0 + gs].rearrange("(cc p) t -> p cc t", p=128), writes=[br.d])
            mT = mT_r.next()
            gt_bufs = {}
            tms = []
            for dch in range(8):
                pm_ = pacc.next()
                for k in range(4):
                    if dch == 0:
                        gt = gt_r.next()
                        c.dma("pool", gt.t[:, :, 0:gs],
                              GATES[k * D:(k + 1) * D, g0:g0 + gs].rearrange("(dc p) t -> p dc t", p=128), writes=[gt.d])
                        gt_bufs[k] = gt
                    gt = gt_bufs[k]
                    p = pp.next()
                    for cc in range(4):
                        c.op("pe", lambda e: e.matmul(p.t[:, 0:gs], lhsT=wb.t[:, k * 4 + cc, dch * 128:(dch + 1) * 128], rhs=br.t[:, k * 4 + cc, 0:gs],
                                                      start=(cc == 0), stop=(cc == 3)), reads=[wb.d, br.d], writes=[p.d])
                    tm = tmp_r.next()
                    c.op("dve", lambda e: e.tensor_tensor(out=tm.t[:, 0:gs], in0=p.t[:, 0:gs], in1=gt.t[:, dch, 0:gs], op=ALU.mult),
                         reads=[p.d, gt.d], writes=[tm.d])
                    tms.append((tm, k, dch, pm_))
                    if len(tms) > 1:
                        tm0, k0, dch0, pm0 = tms.pop(0)
                        c.op("pe", lambda e: e.matmul(pm0.t[:, 0:gs], lhsT=idb.t[:], rhs=tm0.t[:, 0:gs], start=(k0 == 0), stop=(k0 == 3)),
                             reads=[idb.d, tm0.d], writes=[pm0.d])
                        if k0 == 3:
                            c.op("act", lambda e: e.copy(out=mT.t[:, dch0, 0:gs], in_=pm0.t[:, 0:gs]), reads=[pm0.d], writes=[mT.d])
            while tms:
                tm0, k0, dch0, pm0 = tms.pop(0)
                c.op("pe", lambda e: e.matmul(pm0.t[:, 0:gs], lhsT=idb.t[:], rhs=tm0.t[:, 0:gs], start=(k0 == 0), stop=(k0 == 3)),
                     reads=[idb.d, tm0.d], writes=[pm0.d])
                if k0 == 3:
                    c.op("act", lambda e: e.copy(out=mT.t[:, dch0, 0:gs], in_=pm0.t[:, 0:gs]), reads=[pm0.d], writes=[mT.d])
            def finish(mT=mT, g0=g0, gs=gs):
                for sub in range(gs // 128):
                    i = g0 // 128 + sub
                    o = po.next()
                    for half in range(2):
                        for k in range(8):
                            c.op("pe", lambda e: e.matmul(o.t[:, half * 512:(half + 1) * 512], lhsT=mT.t[:, k, sub * 128:(sub + 1) * 128],
                                                          rhs=wo.t[:, k, half * 512:(half + 1) * 512], start=(k == 0), stop=(k == 7)),
                                 reads=[mT.d, wo.d], writes=[o.d])
                    post_norm_tile(i, o.t[:], [o.d], gate[1 if i < 2 else 0], lng, lnb, rr, x_r, t_r)

            if deferred:
                deferred.pop()()
            deferred.append(finish)
        while deferred:
            deferred.pop()()
        ph.close()

    def phase_moe(l):
        pm = Phase(c)
        AFFTM = pm.sb([128, NT, NEXP], F32)
        POS = pm.sb([128, NT, NEXP], F32)
        p12 = Phase(c)
        AFFT = p12.sb([NEXP, T], F32)
        phase_lnmod(l, 1, AFFT=AFFT, AFFTM=AFFTM)
        ph = Phase(c)
        lo = ph.sb([NEXP, 1], F32)
        hi = ph.sb([NEXP, 1], F32)
        mid = ph.sb([NEXP, 1], F32)
        cntb = ph.sb([NEXP, 1], F32)
        ge = ph.sb([NEXP, 1], F32)
        dlt = ph.sb([NEXP, 1], F32)
        scr = ph.sb([NEXP, TL], F32)
        mask = ph.sb([NEXP, T], F32)
        zer = ph.sb([NEXP, TL], F32)
        posm = ph.sb([NEXP, T], F32)
        c.op("pool", lambda e: e.memset(zer.t[:], 0.0), writes=[zer.d])
        for (a0, n, kk, base) in ((0, TC, 32, 512.0), (TC, TL, 512, 0.0)):
            c.op("dve", lambda e: e.memset(lo.t[:], 0.0), writes=[lo.d])
            c.op("dve", lambda e: e.memset(hi.t[:], 1.0), writes=[hi.d])
            for it in range(34):
                c.op("dve", lambda e: e.tensor_tensor(out=mid.t[:], in0=lo.t[:], in1=hi.t[:], op=ALU.add), reads=[lo.d, hi.d], writes=[mid.d])
                c.op("dve", lambda e: e.tensor_scalar(out=mid.t[:], in0=mid.t[:], scalar1=0.5, scalar2=None, op0=ALU.mult),
                     reads=[mid.d], writes=[mid.d])
                c.op("dve", lambda e: e.tensor_scalar(out=scr.t[:, 0:n], in0=AFFT.t[:, a0:a0 + n], scalar1=mid.t[:, 0:1], scalar2=None,
                                                      op0=ALU.is_ge, op1=ALU.add, accum_out=cntb.t[:]), reads=[AFFT.d, mid.d], writes=[scr.d, cntb.d])
                c.op("dve", lambda e: e.tensor_scalar(out=ge.t[:], in0=cntb.t[:], scalar1=float(kk) - 0.5, scalar2=None, op0=ALU.is_ge),
                     reads=[cntb.d], writes=[ge.d])
                c.op("dve", lambda e: e.tensor_tensor(out=dlt.t[:], in0=mid.t[:], in1=lo.t[:], op=ALU.subtract), reads=[mid.d, lo.d], writes=[dlt.d])
                c.op("dve", lambda e: e.tensor_tensor(out=dlt.t[:], in0=dlt.t[:], in1=ge.t[:], op=ALU.mult), reads=[dlt.d, ge.d], writes=[dlt.d])
                c.op("dve", lambda e: e.tensor_tensor(out=lo.t[:], in0=lo.t[:], in1=dlt.t[:], op=ALU.add), reads=[lo.d, dlt.d], writes=[lo.d])
                c.op("dve", lambda e: e.tensor_tensor(out=dlt.t[:], in0=hi.t[:], in1=mid.t[:], op=ALU.subtract), reads=[hi.d, mid.d], writes=[dlt.d])
                c.op("dve", lambda e: e.tensor_tensor(out=dlt.t[:], in0=dlt.t[:], in1=ge.t[:], op=ALU.mult), reads=[dlt.d, ge.d], writes=[dlt.d])
                c.op("dve", lambda e: e.tensor_tensor(out=hi.t[:], in0=mid.t[:], in1=dlt.t[:], op=ALU.add), reads=[mid.d, dlt.d], writes=[hi.d])
            c.op("dve", lambda e: e.tensor_scalar(out=mask.t[:, a0:a0 + n], in0=AFFT.t[:, a0:a0 + n], scalar1=lo.t[:, 0:1], scalar2=None,
                                                  op0=ALU.is_ge), reads=[AFFT.d, lo.d], writes=[mask.d])
            c.op("dve", lambda e: e.tensor_tensor_scan(out=posm.t[:, a0:a0 + n], data0=mask.t[:, a0:a0 + n], data1=zer.t[:, 0:n], initial=0.0,
                                                       op0=ALU.add, op1=ALU.add), reads=[mask.d, zer.d], writes=[posm.d])
            c.op("dve", lambda e: e.tensor_scalar(out=posm.t[:, a0:a0 + n], in0=posm.t[:, a0:a0 + n], scalar1=base, scalar2=None, op0=ALU.add),
                 reads=[posm.d], writes=[posm.d])
            c.op("dve", lambda e: e.tensor_tensor(out=posm.t[:, a0:a0 + n], in0=posm.t[:, a0:a0 + n], in1=mask.t[:, a0:a0 + n], op=ALU.mult),
                 reads=[posm.d, mask.d], writes=[posm.d])
            c.op("dve", lambda e: e.tensor_scalar(out=posm.t[:, a0:a0 + n], in0=posm.t[:, a0:a0 + n], scalar1=-1.0, scalar2=None, op0=ALU.add),
                 reads=[posm.d], writes=[posm.d])
        c.dma("sp", POSD.rearrange("i (e t) -> e i t", e=NEXP), posm.t[:].rearrange("e (i t) -> e i t", t=128), reads=[posm.d])
        ptp = ph.ps([128, NEXP], F32, n=2)
        for i in range(NT):
            p = ptp.next()
            c.op("pe", lambda e: e.transpose(out=p.t[:], in_=posm.t[:, i * 128:(i + 1) * 128], identity=idf.t[0:NEXP, 0:NEXP]),
                 reads=[posm.d, idf.d], writes=[p.d])
            c.op("act", lambda e: e.copy(out=POS.t[:, i, :], in_=p.t[:]), reads=[p.d], writes=[POS.d])
        ph.close()
        p12.close()
        WSL = pm.sb([128, NEXP, 5], F32)
        IDX = pm.sb([128, NEXP, 5], mybir.dt.int32)
        ph = Phase(c)
        ymoe_d = Dep()
        zt = ph.sb([128, D], F32)
        c.op("dve", lambda e: e.memset(zt.t[:], 0.0), writes=[zt.d])
        sc_state = {"prev": {}, "cur": {}, "ex": -1}

        def tok_add(dct):
            k_, v_ = c.last_tok
            if dct.get(k_, 0) < v_:
                dct[k_] = v_

        for i in range(NT):
            c.dma("sp", YMOE[i * 128:(i + 1) * 128, :], zt.t[:], reads=[zt.d])
            tok_add(sc_state["cur"])
        iota = ph.sb([128, 512], F32)
        c.op("pool", lambda e: e.iota(iota.t[:], pattern=[[1, 512]], base=0, channel_multiplier=0, allow_small_or_imprecise_dtypes=True),
             writes=[iota.d])
        iotac = ph.sb([128, 32], F32)
        c.op("pool", lambda e: e.iota(iotac.t[:], pattern=[[1, 32]], base=512, channel_multiplier=0, allow_small_or_imprecise_dtypes=True),
             writes=[iotac.d])
        wsp = ph.sb([128, 5, NT * NEXP], BF16)
        r1 = ph.sb([128, NT * NEXP], F32)
        r2 = ph.sb([128, NT * NEXP], F32)
        aff2 = AFFTM.t[:].rearrange("p i e -> p (i e)")
        c.op("dve", lambda e: e.tensor_copy(out=wsp.t[:, 0, :], in_=aff2), reads=[AFFTM.d], writes=[wsp.d])
        c.op("dve", lambda e: e.tensor_tensor(out=r1.t[:], in0=aff2, in1=wsp.t[:, 0, :], op=ALU.subtract), reads=[AFFTM.d, wsp.d], writes=[r1.d])
        c.op("dve", lambda e: e.tensor_copy(out=wsp.t[:, 1, :], in_=r1.t[:]), reads=[r1.d], writes=[wsp.d])
        c.op("dve", lambda e: e.tensor_tensor(out=r2.t[:], in0=r1.t[:], in1=wsp.t[:, 1, :], op=ALU.subtract), reads=[r1.d, wsp.d], writes=[r2.d])
        c.op("dve", lambda e: e.tensor_copy(out=wsp.t[:, 2, :], in_=r2.t[:]), reads=[r2.d], writes=[wsp.d])
        c.op("pool", lambda e: e.iota(r1.t[:], pattern=[[1, NT], [0, NEXP]], base=0, channel_multiplier=0, allow_small_or_imprecise_dtypes=True),
             reads=[r1.d], writes=[r1.d])
        c.op("dve", lambda e: e.tensor_copy(out=wsp.t[:, 3, :], in_=r1.t[:]), reads=[r1.d], writes=[wsp.d])
        c.op("pool", lambda e: e.iota(r2.t[:], pattern=[[0, NT * NEXP]], base=0, channel_multiplier=1, allow_small_or_imprecise_dtypes=True),
             reads=[r2.d], writes=[r2.d])
        c.op("dve", lambda e: e.tensor_copy(out=wsp.t[:, 4, :], in_=r2.t[:]), reads=[r2.d], writes=[wsp.d])
        sel_r = ph.sb([128, 32 * 512 + 64], BF16, n=2)
        wsl5 = ph.sb([128, NEXP, 5, 5], F32)
        IDXF = ph.sb([128, NEXP, 5], F32)
        pws = ph.ps([128, 8], F32, n=2)
        JC = [(0, 128), (128, 128), (256, 128), (384, 128), (512, 32)]
        c.op("dve", lambda e: e.memset(wsl5.t[:], 0.0), writes=[wsl5.d])
        for ex in range(NEXP):
            sel = sel_r.next()

            def sel_ap(i):
                if i < 2:
                    return sel.t[:, 32 * 512 + i * 32:32 * 512 + (i + 1) * 32]
                return sel.t[:, (i - 2) * 512:(i - 1) * 512]

            for i in range(NT):
                src = iotac if i < 2 else iota
                c.op("dve", lambda e: e.tensor_scalar(out=sel_ap(i), in0=src.t[:], scalar1=POS.t[:, i, ex:ex + 1], scalar2=None, op0=ALU.is_equal),
                     reads=[src.d, POS.d], writes=[sel.d])
            for jc, (j0, jn) in enumerate(JC):
                p = pws.next()
                tl = [0, 1] if jc == 4 else list(range(2, NT))
                for ti, i in enumerate(tl):
                    sa = sel_ap(i)
                    lhs = sa[:, 0:32] if jc == 4 else sa[:, j0:j0 + jn]
                    c.op("pe", lambda e: e.matmul(p.t[0:jn, 0:5], lhsT=lhs, rhs=wsp.t[:, :, i * NEXP + ex], start=(ti == 0), stop=(ti == len(tl) - 1)),
                         reads=[sel.d, wsp.d], writes=[p.d])
                c.op("act", lambda e: e.copy(out=wsl5.t[0:jn, ex, jc, :], in_=p.t[0:jn, 0:5]), reads=[p.d], writes=[wsl5.d])
        c.op("dve", lambda e: e.tensor_tensor(out=WSL.t[:], in0=wsl5.t[:, :, :, 2], in1=wsl5.t[:, :, :, 1], op=ALU.add), reads=[wsl5.d], writes=[WSL.d])
        c.op("dve", lambda e: e.tensor_tensor(out=WSL.t[:], in0=WSL.t[:], in1=wsl5.t[:, :, :, 0], op=ALU.add), reads=[WSL.d, wsl5.d], writes=[WSL.d])
        c.op("dve", lambda e: e.scalar_tensor_tensor(out=IDXF.t[:], in0=wsl5.t[:, :, :, 3], scalar=128.0, in1=wsl5.t[:, :, :, 4], op0=ALU.mult, op1=ALU.add),
             reads=[wsl5.d], writes=[IDXF.d])
        c.op("dve", lambda e: e.tensor_copy(out=IDX.t[:], in_=IDXF.t[:]), reads=[IDXF.d], writes=[IDX.d])
        ph.close()

        ph = Phase(c)
        xg_r = ph.sb([128, 5, D], BF16, n=2)
        xeT = ph.sb([128, 8, NSLOT], BF16)
        heT = ph.sb([128, NFC, NSLOT], BF16)
        wgu_r = ph.sb([128, 2, 8, 512], BF16, n=2)
        wd_r = ph.sb([128, NFC, 512], BF16, n=4)
        ptx = ph.ps([128, NSLOT], BF16, n=2)
        pg_ = ph.ps([128, 512], F32, n=2)
        pgc = ph.ps([128, 32], F32, n=1)
        pup = ph.ps([128, 512], F32, n=2)
        pupc = ph.ps([128, 32], F32, n=1)
        sg_r = ph.sb([128, NSLOT], F32, n=2)
        ye_r = ph.sb([128, D], BF16, n=10)

        def emit_gather(ex):
            xg = xg_r.next()
            for jc, (j0, jn) in enumerate(JC):
                c.idma(xg.t[0:jn, jc, :], H2[:, :], IDX.t[0:jn, ex, jc:jc + 1], gather=True, reads=[IDX.d], writes=[xg.d])
            return xg

        pending = []
        xg_next = emit_gather(0)
        for ex in range(NEXP):
            xg = xg_next
            wds = []
            for half in range(2):
                wd = wd_r.next()
                c.dma("pool", wd.t[:], w_e_down[l, ex, :, half * 512:(half + 1) * 512].rearrange("(f p) n -> p f n", p=128), writes=[wd.d])
                wds.append(wd)
            if ex + 1 < NEXP:
                xg_next = emit_gather(ex + 1)
            for dch in range(8):
                p = ptx.next()
                for jc, (j0, jn) in enumerate(JC):
                    c.op("pe", lambda e: e.transpose(out=p.t[:, j0:j0 + jn], in_=xg.t[0:jn, jc, dch * 128:(dch + 1) * 128], identity=idb.t[0:jn, 0:jn]),
                         reads=[xg.d, idb.d], writes=[p.d])
                c.op("act", lambda e: e.copy(out=xeT.t[:, dch, :], in_=p.t[:]), reads=[p.d], writes=[xeT.d])
            for fc in range(NFC):
                if fc % 4 == 0:
                    wgu = wgu_r.next()
                    nb = min(512, EFF - fc * 128)
                    c.dma("pool", wgu.t[:, 0, :, 0:nb], w_e_gate[l, ex, :, fc * 128:fc * 128 + nb].rearrange("(k p) n -> p k n", p=128), writes=[wgu.d])
                    c.dma("pool", wgu.t[:, 1, :, 0:nb], w_e_up[l, ex, :, fc * 128:fc * 128 + nb].rearrange("(k p) n -> p k n", p=128), writes=[wgu.d])
                if fc == 8:
                    for fn_ in pending:
                        fn_()
                    pending = []
                fo = (fc % 4) * 128
                pga, pgb, pua, pub = pg_.next(), pgc.next(), pup.next(), pupc.next()
                for (pa, pb, wi) in ((pga, pgb, 0), (pua, pub, 1)):
                    for k in range(8):
                        c.op("pe", lambda e: e.matmul(pa.t[:], lhsT=wgu.t[:, wi, k, fo:fo + 128], rhs=xeT.t[:, k, 0:512], start=(k == 0), stop=(k == 7)),
                             reads=[wgu.d, xeT.d], writes=[pa.d])
                    for k in range(8):
                        c.op("pe", lambda e: e.matmul(pb.t[:], lhsT=wgu.t[:, wi, k, fo:fo + 128], rhs=xeT.t[:, k, 512:544], start=(k == 0), stop=(k == 7)),
                             reads=[wgu.d, xeT.d], writes=[pb.d])
                sg = sg_r.next()
                c.op("act", lambda e: e.activation(out=sg.t[:, 0:512], in_=pga.t[:], func=AF.Silu), reads=[pga.d], writes=[sg.d])
                c.op("act", lambda e: e.activation(out=sg.t[:, 512:544], in_=pgb.t[:], func=AF.Silu), reads=[pgb.d], writes=[sg.d])
                c.op("dve", lambda e: e.tensor_tensor(out=heT.t[:, fc, 0:512], in0=pua.t[:], in1=sg.t[:, 0:512], op=ALU.mult),
                     reads=[pua.d, sg.d], writes=[heT.d])
                c.op("dve", lambda e: e.tensor_tensor(out=heT.t[:, fc, 512:544], in0=pub.t[:], in1=sg.t[:, 512:544], op=ALU.mult),
                     reads=[pub.d, sg.d], writes=[heT.d])
            for jc, (j0, jn) in enumerate(JC):
                ye = ye_r.next()
                for half in range(2):
                    wd = wds[half]
                    p = pg_.next() if half == 0 else pup.next()
                    for fc in range(NFC):
                        c.op("pe", lambda e: e.matmul(p.t[0:jn, :], lhsT=heT.t[:, fc, j0:j0 + jn], rhs=wd.t[:, fc, :],
                                                      start=(fc == 0), stop=(fc == NFC - 1)), reads=[heT.d, wd.d], writes=[p.d])
                    c.op("act", lambda e: e.activation(out=ye.t[0:jn, half * 512:(half + 1) * 512], in_=p.t[0:jn, :], func=AF.Copy,
                                                       scale=WSL.t[0:jn, ex, jc:jc + 1]), reads=[p.d, WSL.d], writes=[ye.d])

                def scat(ye=ye, jn=jn, ex=ex, jc=jc):
                    if sc_state["ex"] != ex:
                        sc_state["prev"], sc_state["cur"], sc_state["ex"] = sc_state["cur"], {}, ex
                    dtmp = Dep()
                    dtmp.r = dict(sc_state["prev"])
                    c.idma(YMOE[:, :], ye.t[0:jn, :], IDX.t[0:jn, ex, jc:jc + 1], gather=False, reads=[ye.d, IDX.d], writes=[dtmp], add=True)
                    tok_add(sc_state["cur"])
                pending.append(scat)
        for fn_ in pending:
            fn_()
        ph.close()
        pm.close()
        def phase_m5(l, ph):
            gate, lng, lnb = load_pn_consts(ph, l, 5 * D, ln2_g, ln2_b)
            rr = ln_rings(ph)
            x_r = ph.sb([128, D], F32, n=3)
            t_r = ph.sb([128, D], F32, n=3)
            ym_r = ph.sb([128, D], F32, n=3)
            for i in range(NT):
                ym = ym_r.next()
                c.dma("pool", ym.t[:], YMOE[i * 128:(i + 1) * 128, :], writes=[ym.d])
                post_norm_tile(i, ym.t[:], [ym.d], gate[1 if i < 2 else 0], lng, lnb, rr, x_r, t_r)
                yield

        specs = [(phase_m5, NT, l)]
        if l + 1 < n_layers:
            specs.append((phase_mod, 12, l + 1))
        interleave(specs)

    def interleave(specs):
        phs = [Phase(c) for _ in specs]
        act = [[fn(la, ph), tot, 0] for (fn, tot, la), ph in zip(specs, phs)]
        while act:
            a = min(act, key=lambda a: a[2] / a[1])
            try:
                next(a[0])
                a[2] += 1
            except StopIteration:
                act.remove(a)
        c.barrier()
        for ph in reversed(phs):
            ph.es.close()

    interleave([(phase_mod, 12, 0)])
    for l in range(n_layers):
        if stop_after == "mod":
            break
        ph = Phase(c)
        HT = ph.sb([128, 8, T], BF16)
        phase_lnmod(l, 0, HT=HT)
        phase_win(l, HT)
        ph.close()
        if stop_after == "win":
            break
        phase_qkv(l)
        if stop_after == "qkv":
            break
        interleave([(phase_attn, 8 * (2 + 8 * NT), l), (phase_conf, 9 * 31, l)])
        if stop_after == "lru":
            break
        interleave([(phase_lru, 4 * 19, l), (phase_fnet, NT + 2 + 8 * 32, l)])
        if stop_after == "fnet":
            break
        phase_merge(l)
        if stop_after == "merge":
            break
        phase_moe(l)

    c.barrier()
    for i in range(8):
        c.dma("sp", out_ap[512 * i:512 * (i + 1), :], X[TC + 512 * i:TC + 512 * (i + 1), :])
    c.barrier()
    glob.es.close()
    c.close()
    dbg = {"X": X, "BR": BR, "MOD": MOD, "CKVN": CKVN, "CQN": CQN, "KT": KT, "QT": QT, "VV": VV, "XBT": XBT, "UT": UT, "FT": FT,
           "GBT": GBT, "GATES": GATES, "H2": H2, "POSD": POSD, "YE": YE, "YMOE": YMOE}
    return nc, c, dbg


_CONST = {}


def host_consts():
    if _CONST:
        return _CONST
    bf = ml_dtypes.bfloat16
    inv = (10000.0 ** (-np.arange(8, dtype=np.float32) / 8)).astype(np.float32)
    tt = np.arange(TL)
    row = (tt // 64).astype(np.float32)
    col = (tt % 64).astype(np.float32)
    ang_r = row[None, :] * inv[:, None]
    ang_c = col[None, :] * inv[:, None]
    cosf = np.ones((96, T), np.float32)
    sinf = np.zeros((96, T), np.float32)
    for part, ang in ((0, ang_r), (1, ang_c)):
        for half in range(2):
            r0 = 64 + part * 16 + half * 8
            cosf[r0:r0 + 8, TC:] = np.cos(ang)
            sinf[r0:r0 + 8, TC:] = np.sin(ang)
    _CONST["rope_cos"] = cosf
    _CONST["rope_sin"] = sinf

    def dft(n):
        k = np.arange(n, dtype=np.int64)
        m = (k[:, None] * k[None, :]) % n
        a = 2.0 * np.pi * m.astype(np.float64) / n
        return np.cos(a), np.sin(a)

    cL, sL = dft(TL)
    _CONST["dft_c"] = cL.astype(np.float32).astype(bf)
    _CONST["dft_s"] = (-sL).astype(np.float32).astype(bf)
    c2, s2 = dft(TC)
    _CONST["dft_c256"] = c2.astype(np.float32).astype(bf)
    _CONST["dft_s256"] = (-s2).astype(np.float32).astype(bf)
    cc, sc = dft(128)
    _CONST["ccsc"] = np.concatenate([cc, sc], axis=1).astype(np.float32).astype(bf)
    return _CONST


_PROG = {}


def kernel(**inputs):
    if "nc" not in _PROG:
        _PROG["nc"] = build_program()[0]
    nc = _PROG["nc"]
    cst = host_consts()
    f32 = lambda a: np.ascontiguousarray(np.asarray(a, dtype=np.float32))
    shared = {}
    for k in ("ada_w", "ada_b", "w_in", "q_norm_g", "kv_norm_g", "cv_w", "cv_b", "cv_ln_g", "cv_ln_b", "lru_conv_w", "lru_conv_b",
              "lru_wa", "lru_ba", "lru_wx", "lru_bx", "lru_lambda", "w_branch", "w_out", "ln1_g", "ln1_b", "w_router",
              "w_e_gate", "w_e_up", "w_e_down", "ln2_g", "ln2_b"):
        shared[k] = f32(inputs[k])
    shared["w_uq"] = f32(inputs["w_uq"]).reshape(DEPTH, 384, 768)
    shared["w_ukv"] = f32(inputs["w_ukv"]).reshape(DEPTH, 256, 1024)
    shared.update(cst)
    x = f32(inputs["x"])
    ctx = f32(inputs["ctx"])
    cc = f32(inputs["c"])
    c_ctx = f32(inputs["c_ctx"])
    in_maps = []
    for core in range(8):
        b = core % 4
        m = dict(shared)
        m["x"] = x[b]
        m["ctx"] = ctx[b]
        m["cvec"] = np.ascontiguousarray(np.stack([cc[b], c_ctx], axis=0))
        in_maps.append(m)
    res = run_bass_kernel_spmd(nc, in_maps, core_ids=list(range(8)))
    out = np.stack([np.asarray(res.results[b]["out"], dtype=np.float32) for b in range(4)], axis=0)
    return out
```

```python
from contextlib import ExitStack
import math
import numpy as np
import ml_dtypes
import concourse.bass as bass
import concourse.mybir as mybir
from concourse.bass_utils import run_bass_kernel_spmd

F32 = mybir.dt.float32
BF16 = mybir.dt.bfloat16
AF = mybir.ActivationFunctionType
ALU = mybir.AluOpType

DMA_RING = 6

D = 1024
TC = 256
TL = 4096
T = TC + TL
NT = T // 128
DEPTH = 4
IN_W = 7328
NEXP = 16
EFF = 1408
NFC = EFF // 128
ALPHA = (2 * DEPTH) ** 0.25
EPS = 1e-6
GROUPS = [(0, 256)] + [(256 + 512 * i, 512) for i in range(8)]
O_CKV, O_KR, O_XB, O_CQ, O_GLU, O_F, O_GB, O_GL = 0, 256, 288, 800, 1184, 2208, 2720, 3232
NSLOT = 544


class Dep:
    __slots__ = ("w", "r")

    def __init__(self):
        self.w = None
        self.r = {}


class Ctx:
    def __init__(self, nc):
        self.nc = nc
        self.es = ExitStack()
        self.E = {"pe": nc.tensor, "act": nc.scalar, "dve": nc.vector, "pool": nc.gpsimd, "sp": nc.sync}
        self.sem = {}
        self.cnt = {}
        for e in ("pe", "act", "dve", "pool"):
            self.sem[e] = self.es.enter_context(nc.semaphore("s_" + e))
            self.cnt[e] = 0
        self.dma_i = {}
        for q in ("sp", "pool", "act"):
            self.dma_i[q] = 0
            for s in range(DMA_RING):
                k = ("d", q, s)
                self.sem[k] = self.es.enter_context(nc.semaphore("d_%s%d" % (q, s)))
                self.cnt[k] = 0
        self.seen = {e: {} for e in self.E}
        self.n_inst = 0
        self.n_wait = 0
        self.uid = 0

    def close(self):
        self.es.close()

    def _need(self, eng, reads, writes):
        need = {}

        def add(tok):
            if tok is None:
                return
            k, v = tok
            if k == "pe" and eng == "pe":
                return
            if need.get(k, 0) < v:
                need[k] = v

        for d in reads:
            add(d.w)
        for d in writes:
            add(d.w)
            for k, v in d.r.items():
                add((k, v))
        return need

    def _emit_waits(self, eng, need):
        E = self.E[eng]
        seen = self.seen[eng]
        for k, v in need.items():
            if seen.get(k, 0) >= v:
                continue
            E.wait_ge(self.sem[k], v)
            seen[k] = v
            self.n_wait += 1

    def _mark(self, tok, reads, writes):
        k, v = tok
        for d in reads:
            if d.r.get(k, 0) < v:
                d.r[k] = v
        for d in writes:
            d.w = tok
            d.r = {}

    def op(self, eng, fn, reads=(), writes=()):
        need = self._need(eng, reads, writes)
        self._emit_waits(eng, need)
        inst = fn(self.E[eng])
        self.cnt[eng] += 1
        inst.then_inc(self.sem[eng], 1)
        self._mark((eng, self.cnt[eng]), reads, writes)
        self.n_inst += 1
        return inst

    def dma(self, q, out, in_, reads=(), writes=(), **kw):
        E = self.E[q]
        slot = self.dma_i[q] % DMA_RING
        self.dma_i[q] += 1
        k = ("d", q, slot)
        need = self._need(q, reads, writes)
        if self.cnt[k] > 0 and need.get(k, 0) < self.cnt[k]:
            need[k] = self.cnt[k]
        self._emit_waits(q, need)
        inst = E.dma_start(out=out, in_=in_, **kw)
        self.cnt[k] += 16
        inst.then_inc(self.sem[k], 16)
        self._mark((k, self.cnt[k]), reads, writes)
        self.last_tok = (k, self.cnt[k])
        self.n_inst += 1
        return inst

    def idma(self, out, in_, idx_ap, gather, reads=(), writes=(), add=False):
        q = "pool"
        E = self.E[q]
        slot = self.dma_i[q] % DMA_RING
        self.dma_i[q] += 1
        k = ("d", q, slot)
        need = self._need(q, reads, writes)
        if self.cnt[k] > 0 and need.get(k, 0) < self.cnt[k]:
            need[k] = self.cnt[k]
        self._emit_waits(q, need)
        off = bass.IndirectOffsetOnAxis(ap=idx_ap, axis=0)
        kw = {"compute_op": ALU.add} if add else {}
        if gather:
            inst = E.indirect_dma_start(out=out, out_offset=None, in_=in_, in_offset=off, **kw)
        else:
            inst = E.indirect_dma_start(out=out, out_offset=off, in_=in_, in_offset=None, **kw)
        self.cnt[k] += 16
        inst.then_inc(self.sem[k], 16)
        self._mark((k, self.cnt[k]), reads, writes)
        self.last_tok = (k, self.cnt[k])
        self.n_inst += 1
        return inst

    def barrier(self):
        allk = {k: v for k, v in self.cnt.items() if v > 0}
        for e in self.E:
            self._emit_waits(e, dict(allk))


class Buf:
    __slots__ = ("t", "d")

    def __init__(self, t):
        self.t = t
        self.d = Dep()


class Ring:
    def __init__(self, bufs):
        self.bufs = bufs
        self.i = 0

    def next(self):
        b = self.bufs[self.i % len(self.bufs)]
        self.i += 1
        return b


class Phase:
    def __init__(self, c):
        self.c = c
        self.es = ExitStack()

    def sb(self, shape, dt, n=None):
        c = self.c
        out = []
        for _ in range(n or 1):
            c.uid += 1
            out.append(Buf(self.es.enter_context(c.nc.sbuf_tensor("sb%d" % c.uid, list(shape), dt))))
        return out[0] if n is None else Ring(out)

    def ps(self, shape, dt=F32, n=None):
        c = self.c
        out = []
        for _ in range(n or 1):
            c.uid += 1
            out.append(Buf(self.es.enter_context(c.nc.psum_tensor("ps%d" % c.uid, list(shape), dt))))
        return out[0] if n is None else Ring(out)

    def close(self):
        self.c.barrier()
        self.es.close()


def build_program(n_layers=DEPTH, stop_after=None, debug=False):
    nc = bass.Bass("TRN2", target_bir_lowering=False)

    def din(name, shape, dt=F32):
        return nc.dram_tensor(name, list(shape), dt, kind="ExternalInput").ap()

    def dscr(name, shape, dt=F32):
        return nc.dram_tensor(name, list(shape), dt, kind="ExternalOutput" if debug else "Internal").ap()

    L = DEPTH
    x_in = din("x", [TL, D])
    ctx_in = din("ctx", [TC, D])
    cvec_in = din("cvec", [2, D])
    ada_w = din("ada_w", [L, D, 6 * D])
    ada_b = din("ada_b", [L, 6 * D])
    w_in = din("w_in", [L, D, IN_W])
    q_norm_g = din("q_norm_g", [L, 384])
    w_uq = din("w_uq", [L, 384, 768])
    kv_norm_g = din("kv_norm_g", [L, 256])
    w_ukv = din("w_ukv", [L, 256, 1024])
    cv_w = din("cv_w", [L, 31, 512])
    cv_b = din("cv_b", [L, 512])
    cv_ln_g = din("cv_ln_g", [L, 512])
    cv_ln_b = din("cv_ln_b", [L, 512])
    lru_conv_w = din("lru_conv_w", [L, 4, 512])
    lru_conv_b = din("lru_conv_b", [L, 512])
    lru_wa = din("lru_wa", [L, 2, 8, 64, 64])
    lru_ba = din("lru_ba", [L, 2, 512])
    lru_wx = din("lru_wx", [L, 2, 8, 64, 64])
    lru_bx = din("lru_bx", [L, 2, 512])
    lru_lambda = din("lru_lambda", [L, 2, 512])
    w_branch = din("w_branch", [L, 4, 512, D])
    w_out = din("w_out", [L, D, D])
    ln1_g = din("ln1_g", [L, D])
    ln1_b = din("ln1_b", [L, D])
    w_router = din("w_router", [L, D, NEXP])
    w_e_gate = din("w_e_gate", [L, NEXP, D, EFF])
    w_e_up = din("w_e_up", [L, NEXP, D, EFF])
    w_e_down = din("w_e_down", [L, NEXP, EFF, D])
    ln2_g = din("ln2_g", [L, D])
    ln2_b = din("ln2_b", [L, D])
    rope_cos = din("rope_cos", [96, T])
    rope_sin = din("rope_sin", [96, T])
    dft_c = din("dft_c", [TL, TL], BF16)
    dft_s = din("dft_s", [TL, TL], BF16)
    dft_c256 = din("dft_c256", [TC, TC], BF16)
    dft_s256 = din("dft_s256", [TC, TC], BF16)
    ccsc = din("ccsc", [128, 256], BF16)
    out_ap = nc.dram_tensor("out", [TL, D], F32, kind="ExternalOutput").ap()

    X = dscr("X", [T, D])
    MOD = dscr("MOD", [2, 2, 128, 6 * D])
    CKVN = dscr("CKVN", [256, T], BF16)
    CQN = dscr("CQN", [384, T], BF16)
    KT = dscr("KT", [8, 96, T], BF16)
    QT = dscr("QT", [8, 96, T], BF16)
    VV = dscr("VV", [T, 512], BF16)
    XBT = dscr("XBT", [512, T])
    UT = dscr("UT", [512, T])
    FT = dscr("FT", [512, T], BF16)
    GBT = dscr("GBT", [512, T])
    GATES = dscr("GATES", [4 * D, T], BF16)
    BR = dscr("BR", [4, 512, T], BF16)
    H2 = dscr("H2", [T, D], BF16)
    POSD = dscr("POSD", [NT, NEXP * 128])
    YE = dscr("YE", [NEXP, NSLOT, D], BF16)
    YMOE = dscr("YMOE", [T, D])

    c = Ctx(nc)
    c.es.enter_context(nc.allow_non_contiguous_dma(reason="small strided parameter loads"))
    glob = Phase(c)
    idf = glob.sb([128, 128], F32)
    idb = glob.sb([128, 128], BF16)
    ones32 = glob.sb([128, 128], F32)
    onesb = glob.sb([128, 128], BF16)
    c.op("pool", lambda e: e.memset(idf.t[:], 1.0), writes=[idf.d])
    c.op("pool", lambda e: e.affine_select(out=idf.t[:], in_=idf.t[:], pattern=[[-1, 128]], compare_op=ALU.is_equal,
                                           fill=0.0, base=0, channel_multiplier=1), reads=[idf.d], writes=[idf.d])
    c.op("dve", lambda e: e.tensor_copy(out=idb.t[:], in_=idf.t[:]), reads=[idf.d], writes=[idb.d])
    c.op("pool", lambda e: e.memset(ones32.t[:], 1.0), writes=[ones32.d])
    c.op("pool", lambda e: e.memset(onesb.t[:], 1.0), writes=[onesb.d])

    c.dma("sp", X[0:TC, :], ctx_in[:, :])
    for i in range(8):
        c.dma("sp", X[TC + 512 * i:TC + 512 * (i + 1), :], x_in[512 * i:512 * (i + 1), :])
    c.barrier()

    def bcast_rows(ap_row, n=128):
        return ap_row.to_broadcast([n, ap_row.shape[-1]])

    def phase_mod(l, ph):
        cv = ph.sb([128, 2, 8], F32)
        sv = ph.sb([128, 2, 8], F32)
        lh = ph.sb([128, 2, 8, 128], F32)
        brow = ph.sb([1, 6 * D], F32)
        wt = ph.sb([128, 8, 512], F32, n=2)
        pp = ph.ps([128, 512], F32, n=4)
        ob = ph.sb([128, 512], F32, n=3)
        for w in range(2):
            c.dma("pool", cv.t[:, w, :].rearrange("p (k o) -> p k o", o=1),
                  cvec_in[w:w + 1, :].rearrange("o (k p q) -> p (o k) q", p=128, q=1), writes=[cv.d])
        c.dma("pool", brow.t[:], ada_b[l:l + 1, :], writes=[brow.d])
        c.op("act", lambda e: e.activation(out=sv.t[:], in_=cv.t[:], func=AF.Silu), reads=[cv.d], writes=[sv.d])
        for w in range(2):
            for k in range(8):
                c.op("dve", lambda e: e.tensor_copy(out=lh.t[:, w, k, :], in_=sv.t[:, w, k:k + 1].to_broadcast([128, 128])),
                     reads=[sv.d], writes=[lh.d])
        for n in range(12):
            wb = wt.next()
            c.dma("pool", wb.t[:], ada_w[l, :, n * 512:(n + 1) * 512].rearrange("(k p) n -> p k n", p=128),
                  writes=[wb.d])
            for w in range(2):
                p = pp.next()
                for k in range(8):
                    c.op("pe", lambda e: e.matmul(p.t[:], lhsT=lh.t[:, w, k, :], rhs=wb.t[:, k, :], start=(k == 0), stop=False),
                         reads=[lh.d, wb.d], writes=[p.d])
                c.op("pe", lambda e: e.matmul(p.t[:], lhsT=ones32.t[0:1, :], rhs=brow.t[0:1, n * 512:(n + 1) * 512],
                                              start=False, stop=True), reads=[ones32.d, brow.d], writes=[p.d])
                o = ob.next()
                addone = 1.0 if n in (2, 3, 8, 9) else 0.0
                c.op("dve", lambda e: e.tensor_scalar(out=o.t[:], in0=p.t[:], scalar1=addone, scalar2=None, op0=ALU.add),
                     reads=[p.d], writes=[o.d])
                c.dma("sp", MOD[l % 2, w, :, n * 512:(n + 1) * 512], o.t[:], reads=[o.d])
            yield

    def ln_stats(ph, xt, rings):
        st = rings["st"].next()
        mv = rings["mv"].next()
        rs = rings["rs"].next()
        for hh in range(2):
            c.op("dve", lambda e: e.bn_stats(out=st.t[:, hh, :], in_=xt.t[:, hh * 512:(hh + 1) * 512]), reads=[xt.d], writes=[st.d])
        c.op("dve", lambda e: e.bn_aggr(out=mv.t[:], in_=st.t[:].rearrange("p a b -> p (a b)")), reads=[st.d], writes=[mv.d])
        c.op("act", lambda e: e.activation(out=rs.t[:], in_=mv.t[:, 1:2], func=AF.Sqrt, bias=rings["eps"].t[:, 0:1], scale=1.0),
             reads=[mv.d, rings["eps"].d], writes=[rs.d])
        c.op("dve", lambda e: e.reciprocal(out=rs.t[:], in_=rs.t[:]), reads=[rs.d], writes=[rs.d])
        return mv, rs

    def ln_rings(ph):
        r = {"st": ph.sb([128, 2, 6], F32, n=3), "mv": ph.sb([128, 2], F32, n=3), "rs": ph.sb([128, 1], F32, n=3),
             "eps": ph.sb([128, 1], F32)}
        c.op("pool", lambda e: e.memset(r["eps"].t[:], EPS), writes=[r["eps"].d])
        return r

    def phase_lnmod(l, sub, HT=None, AFFT=None, AFFTM=None):
        ph = Phase(c)
        rr = ln_rings(ph)
        xt_r = ph.sb([128, D], F32, n=3)
        hb_r = ph.sb([128, D], BF16, n=2)
        sc = [ph.sb([128, D], F32) for _ in range(2)]
        sh = [ph.sb([128, D], F32) for _ in range(2)]
        o_sh = (0 if sub == 0 else 3) * D
        o_sc = (1 if sub == 0 else 4) * D
        for w in range(2):
            c.dma("pool", sc[w].t[:], MOD[l % 2, w, :, o_sc:o_sc + D], writes=[sc[w].d])
            c.dma("pool", sh[w].t[:], MOD[l % 2, w, :, o_sh:o_sh + D], writes=[sh[w].d])
        if sub == 0:
            ptr = ph.ps([128, 8, 128], BF16, n=3)
        else:
            ptr = ph.ps([128, 8, 128], F32, n=2)
            h32_r = ph.sb([128, 8, 128], F32, n=2)
            wr = ph.sb([128, 8, NEXP], F32)
            c.dma("pool", wr.t[:], w_router[l].rearrange("(k p) e -> p k e", p=128), writes=[wr.d])
            plg = ph.ps([128, NEXP], F32, n=2)
            ex_r = ph.sb([128, NEXP], F32, n=2)
            sm_r = ph.sb([128, 1], F32, n=2)
            pat = ph.ps([NEXP, 128], F32, n=2)
        for i in range(NT):
            w = 1 if i < 2 else 0
            xt = xt_r.next()
            c.dma("pool", xt.t[:], X[i * 128:(i + 1) * 128, :], writes=[xt.d])
            mv, rs = ln_stats(ph, xt, rr)
            c.op("dve", lambda e: e.tensor_scalar(out=xt.t[:], in0=xt.t[:], scalar1=mv.t[:, 0:1], scalar2=rs.t[:, 0:1],
                                                  op0=ALU.subtract, op1=ALU.mult), reads=[xt.d, mv.d, rs.d], writes=[xt.d])
            c.op("dve", lambda e: e.tensor_tensor(out=xt.t[:], in0=xt.t[:], in1=sc[w].t[:], op=ALU.mult),
                 reads=[xt.d, sc[w].d], writes=[xt.d])
            if sub == 0:
                hb = hb_r.next()
                c.op("dve", lambda e: e.tensor_tensor(out=hb.t[:], in0=xt.t[:], in1=sh[w].t[:], op=ALU.add),
                     reads=[xt.d, sh[w].d], writes=[hb.d])
                p = ptr.next()
                for k in range(8):
                    c.op("pe", lambda e: e.transpose(out=p.t[:, k, :], in_=hb.t[:, k * 128:(k + 1) * 128], identity=idb.t[:]),
                         reads=[hb.d, idb.d], writes=[p.d])
                c.op("act", lambda e: e.copy(out=HT.t[:, :, i * 128:(i + 1) * 128], in_=p.t[:]), reads=[p.d], writes=[HT.d])
            else:
                c.op("dve", lambda e: e.tensor_tensor(out=xt.t[:], in0=xt.t[:], in1=sh[w].t[:], op=ALU.add),
                     reads=[xt.d, sh[w].d], writes=[xt.d])
                hb = hb_r.next()
                c.op("act", lambda e: e.copy(out=hb.t[:], in_=xt.t[:]), reads=[xt.d], writes=[hb.d])
                c.dma("sp", H2[i * 128:(i + 1) * 128, :], hb.t[:], reads=[hb.d])
                p = ptr.next()
                for k in range(8):
                    c.op("pe", lambda e: e.transpose(out=p.t[:, k, :], in_=xt.t[:, k * 128:(k + 1) * 128], identity=idf.t[:]),
                         reads=[xt.d, idf.d], writes=[p.d])
                h32 = h32_r.next()
                c.op("act", lambda e: e.copy(out=h32.t[:], in_=p.t[:]), reads=[p.d], writes=[h32.d])
                pl = plg.next()
                for k in range(8):
                    c.op("pe", lambda e: e.matmul(pl.t[:], lhsT=h32.t[:, k, :], rhs=wr.t[:, k, :], start=(k == 0), stop=(k == 7)),
                         reads=[h32.d, wr.d], writes=[pl.d])
                ex = ex_r.next()
                sm = sm_r.next()
                c.op("act", lambda e: e.activation(out=ex.t[:], in_=pl.t[:], func=AF.Exp, accum_out=sm.t[:]),
                     reads=[pl.d], writes=[ex.d, sm.d])
                c.op("dve", lambda e: e.reciprocal(out=sm.t[:], in_=sm.t[:]), reads=[sm.d], writes=[sm.d])
                c.op("dve", lambda e: e.tensor_scalar(out=AFFTM.t[:, i, :], in0=ex.t[:], scalar1=sm.t[:, 0:1], scalar2=None,
                                                      op0=ALU.mult), reads=[ex.d, sm.d], writes=[AFFTM.d])
                pa = pat.next()
                c.op("pe", lambda e: e.transpose(out=pa.t[:], in_=AFFTM.t[:, i, :], identity=idf.t[:]),
                     reads=[AFFTM.d, idf.d], writes=[pa.d])
                c.op("act", lambda e: e.copy(out=AFFT.t[:, i * 128:(i + 1) * 128], in_=pa.t[:]), reads=[pa.d], writes=[AFFT.d])
        ph.close()

    def phase_win(l, HT):
        ph = Phase(c)
        WT = ph.sb([128, 8, 1024], BF16, n=2)
        pj = ph.ps([128, 512], F32, n=6)
        pst = ph.ps([128, 512], F32, n=2)
        t32 = ph.sb([128, 512], F32, n=6)
        tb = ph.sb([128, 512], BF16, n=4)
        cosk = ph.sb([32, T], F32)
        sink = ph.sb([32, T], F32)
        c.dma("pool", cosk.t[:], rope_cos[64:96, :], writes=[cosk.d])
        c.dma("pool", sink.t[:], rope_sin[64:96, :], writes=[sink.d])
        epsb = ph.sb([128, 1], F32)
        c.op("pool", lambda e: e.memset(epsb.t[:], EPS), writes=[epsb.d])

        def load_w(col0, ncols, dst0=0, wb=None):
            if wb is None:
                wb = WT.next()
            c.dma("pool", wb.t[:, :, dst0:dst0 + ncols], w_in[l, :, col0:col0 + ncols].rearrange("(k p) n -> p k n", p=128),
                  writes=[wb.d])
            return wb

        def proj(wb, c0, m, g0, gs):
            p = pj.next()
            for k in range(8):
                c.op("pe", lambda e: e.matmul(p.t[0:m, 0:gs], lhsT=wb.t[:, k, c0:c0 + m], rhs=HT.t[:, k, g0:g0 + gs],
                                              start=(k == 0), stop=(k == 7)), reads=[wb.d, HT.d], writes=[p.d])
            return p

        def rms_section(wb, c0, nch, dst):
            for (g0, gs) in GROUPS:
                ps_ = [proj(wb, c0 + 128 * j, 128, g0, gs) for j in range(nch)]
                pss = pst.next()
                for j in range(nch):
                    sq = t32.next()
                    c.op("act", lambda e: e.activation(out=sq.t[:, 0:gs], in_=ps_[j].t[:, 0:gs], func=AF.Square),
                         reads=[ps_[j].d], writes=[sq.d])
                    c.op("pe", lambda e: e.matmul(pss.t[:, 0:gs], lhsT=ones32.t[:], rhs=sq.t[:, 0:gs], start=(j == 0),
                                                  stop=(j == nch - 1)), reads=[ones32.d, sq.d], writes=[pss.d])
                rs = t32.next()
                c.op("act", lambda e: e.activation(out=rs.t[:, 0:gs], in_=pss.t[:, 0:gs], func=AF.Sqrt, bias=epsb.t[:, 0:1],
                                                   scale=1.0 / (nch * 128)), reads=[pss.d, epsb.d], writes=[rs.d])
                c.op("dve", lambda e: e.reciprocal(out=rs.t[:, 0:gs], in_=rs.t[:, 0:gs]), reads=[rs.d], writes=[rs.d])
                for j in range(nch):
                    o = tb.next()
                    c.op("dve", lambda e: e.tensor_tensor(out=o.t[:, 0:gs], in0=ps_[j].t[:, 0:gs], in1=rs.t[:, 0:gs], op=ALU.mult),
                         reads=[ps_[j].d, rs.d], writes=[o.d])
                    c.dma("sp", dst[j * 128:(j + 1) * 128, g0:g0 + gs], o.t[:, 0:gs], reads=[o.d])

        wa = load_w(O_CKV, 288)
        for (dst, src) in ((0, 8), (8, 0), (16, 24), (24, 16)):
            load_w(O_KR + src, 8, dst0=288 + dst, wb=wa)
        for dst in (0, 16):
            c.op("dve", lambda e: e.tensor_scalar(out=wa.t[:, :, 288 + dst:288 + dst + 8], in0=wa.t[:, :, 288 + dst:288 + dst + 8],
                                                  scalar1=-1.0, scalar2=None, op0=ALU.mult), reads=[wa.d], writes=[wa.d])
        rms_section(wa, 0, 2, CKVN)
        for (g0, gs) in GROUPS:
            pk = proj(wa, 256, 32, g0, gs)
            pr = proj(wa, 288, 32, g0, gs)
            a1 = t32.next()
            a2 = t32.next()
            c.op("dve", lambda e: e.tensor_tensor(out=a1.t[0:32, 0:gs], in0=pk.t[0:32, 0:gs], in1=cosk.t[:, g0:g0 + gs], op=ALU.mult),
                 reads=[pk.d, cosk.d], writes=[a1.d])
            c.op("dve", lambda e: e.tensor_tensor(out=a2.t[0:32, 0:gs], in0=pr.t[0:32, 0:gs], in1=sink.t[:, g0:g0 + gs], op=ALU.mult),
                 reads=[pr.d, sink.d], writes=[a2.d])
            o = tb.next()
            c.op("dve", lambda e: e.tensor_tensor(out=o.t[0:32, 0:gs], in0=a1.t[0:32, 0:gs], in1=a2.t[0:32, 0:gs], op=ALU.add),
                 reads=[a1.d, a2.d], writes=[o.d])
            for h in range(8):
                c.dma("sp", KT[h, 64:96, g0:g0 + gs], o.t[0:32, 0:gs], reads=[o.d])
        wq = load_w(O_CQ, 384)
        rms_section(wq, 0, 3, CQN)
        wx = load_w(O_XB, 512)
        for (g0, gs) in GROUPS:
            for j in range(4):
                p = proj(wx, 128 * j, 128, g0, gs)
                o = t32.next()
                c.op("act", lambda e: e.copy(out=o.t[:, 0:gs], in_=p.t[:, 0:gs]), reads=[p.d], writes=[o.d])
                c.dma("sp", XBT[j * 128:(j + 1) * 128, g0:g0 + gs], o.t[:, 0:gs], reads=[o.d])
        wg = load_w(O_GLU, 1024)
        for (g0, gs) in GROUPS:
            for j in range(4):
                pa = proj(wg, 128 * j, 128, g0, gs)
                pg = proj(wg, 512 + 128 * j, 128, g0, gs)
                sg = t32.next()
                c.op("act", lambda e: e.activation(out=sg.t[:, 0:gs], in_=pg.t[:, 0:gs], func=AF.Sigmoid), reads=[pg.d], writes=[sg.d])
                c.op("dve", lambda e: e.tensor_tensor(out=sg.t[:, 0:gs], in0=pa.t[:, 0:gs], in1=sg.t[:, 0:gs], op=ALU.mult),
                     reads=[pa.d, sg.d], writes=[sg.d])
                c.dma("sp", UT[j * 128:(j + 1) * 128, g0:g0 + gs], sg.t[:, 0:gs], reads=[sg.d])
        wf = load_w(O_F, 512)
        for (g0, gs) in GROUPS:
            for j in range(4):
                p = proj(wf, 128 * j, 128, g0, gs)
                o = tb.next()
                c.op("act", lambda e: e.copy(out=o.t[:, 0:gs], in_=p.t[:, 0:gs]), reads=[p.d], writes=[o.d])
                c.dma("sp", FT[j * 128:(j + 1) * 128, g0:g0 + gs], o.t[:, 0:gs], reads=[o.d])
        wgb = load_w(O_GB, 512)
        for (g0, gs) in GROUPS:
            for j in range(4):
                p = proj(wgb, 128 * j, 128, g0, gs)
                s1 = t32.next()
                c.op("act", lambda e: e.activation(out=s1.t[:, 0:gs], in_=p.t[:, 0:gs], func=AF.Square), reads=[p.d], writes=[s1.d])
                c.op("dve", lambda e: e.tensor_scalar(out=s1.t[:, 0:gs], in0=s1.t[:, 0:gs], scalar1=0.044715, scalar2=1.0,
                                                      op0=ALU.mult, op1=ALU.add), reads=[s1.d], writes=[s1.d])
                c.op("dve", lambda e: e.tensor_tensor(out=s1.t[:, 0:gs], in0=p.t[:, 0:gs], in1=s1.t[:, 0:gs], op=ALU.mult),
                     reads=[p.d, s1.d], writes=[s1.d])
                c.op("act", lambda e: e.activation(out=s1.t[:, 0:gs], in_=s1.t[:, 0:gs], func=AF.Sigmoid, scale=1.5957691216057308),
                     reads=[s1.d], writes=[s1.d])
                c.op("dve", lambda e: e.tensor_tensor(out=s1.t[:, 0:gs], in0=p.t[:, 0:gs], in1=s1.t[:, 0:gs], op=ALU.mult),
                     reads=[p.d, s1.d], writes=[s1.d])
                c.dma("sp", GBT[j * 128:(j + 1) * 128, g0:g0 + gs], s1.t[:, 0:gs], reads=[s1.d])
        for cb in range(4):
            wl = load_w(O_GL + 1024 * cb, 1024)
            for (g0, gs) in GROUPS:
                for j in range(8):
                    p = proj(wl, 128 * j, 128, g0, gs)
                    o = tb.next()
                    c.op("act", lambda e: e.activation(out=o.t[:, 0:gs], in_=p.t[:, 0:gs], func=AF.Sigmoid), reads=[p.d], writes=[o.d])
                    r0 = cb * 1024 + j * 128
                    c.dma("sp", GATES[r0:r0 + 128, g0:g0 + gs], o.t[:, 0:gs], reads=[o.d])
        ph.close()

    def phase_qkv(l):
        ph = Phase(c)
        cqn = ph.sb([128, 3, T], BF16)
        ckn = ph.sb([128, 2, T], BF16)
        cosf = ph.sb([96, T], F32)
        sinf = ph.sb([96, T], F32)
        c.dma("pool", cqn.t[:], CQN.rearrange("(k p) t -> p k t", p=128), writes=[cqn.d])
        c.dma("pool", ckn.t[:], CKVN.rearrange("(k p) t -> p k t", p=128), writes=[ckn.d])
        c.dma("pool", cosf.t[:], rope_cos[:, :], writes=[cosf.d])
        c.dma("pool", sinf.t[:], rope_sin[:, :], writes=[sinf.d])
        wq32 = ph.sb([128, 3, 768], F32)
        gq = ph.sb([128, 3], F32)
        wqb = ph.sb([128, 3, 768], BF16)
        wqr32 = ph.sb([128, 3, 768], F32)
        wqr = ph.sb([128, 3, 768], BF16)
        c.dma("pool", wq32.t[:], w_uq[l].rearrange("(k p) n -> p k n", p=128), writes=[wq32.d])
        c.dma("pool", gq.t[:].rearrange("p (k o) -> p k o", o=1), q_norm_g[l:l + 1, :].rearrange("o (k p q) -> p (o k) q", p=128, q=1),
              writes=[gq.d])
        for k in range(3):
            c.op("dve", lambda e: e.tensor_scalar(out=wq32.t[:, k, :], in0=wq32.t[:, k, :], scalar1=gq.t[:, k:k + 1], scalar2=None,
                                                  op0=ALU.mult), reads=[wq32.d, gq.d], writes=[wq32.d])
        c.op("dve", lambda e: e.tensor_copy(out=wqb.t[:], in_=wq32.t[:]), reads=[wq32.d], writes=[wqb.d])
        c.op("pool", lambda e: e.memset(wqr32.t[:], 0.0), writes=[wqr32.d])
        w4 = wq32.t[:].rearrange("p k (h d) -> p k h d", d=96)
        r4 = wqr32.t[:].rearrange("p k (h d) -> p k h d", d=96)
        for (dst, src, sgn) in ((0, 8, -1.0), (8, 0, 1.0), (16, 24, -1.0), (24, 16, 1.0)):
            for k in range(3):
                c.op("dve", lambda e: e.tensor_scalar(out=r4[:, k, :, 64 + dst:72 + dst], in0=w4[:, k, :, 64 + src:72 + src],
                                                      scalar1=sgn, scalar2=None, op0=ALU.mult), reads=[wq32.d], writes=[wqr32.d])
        c.op("dve", lambda e: e.tensor_copy(out=wqr.t[:], in_=wqr32.t[:]), reads=[wqr32.d], writes=[wqr.d])
        wk32 = ph.sb([128, 2, 1024], F32)
        gk = ph.sb([128, 2], F32)
        wkb = ph.sb([128, 2, 1024], BF16)
        c.dma("pool", wk32.t[:], w_ukv[l].rearrange("(k p) n -> p k n", p=128), writes=[wk32.d])
        c.dma("pool", gk.t[:].rearrange("p (k o) -> p k o", o=1), kv_norm_g[l:l + 1, :].rearrange("o (k p q) -> p (o k) q", p=128, q=1),
              writes=[gk.d])
        for k in range(2):
            c.op("dve", lambda e: e.tensor_scalar(out=wk32.t[:, k, :], in0=wk32.t[:, k, :], scalar1=gk.t[:, k:k + 1], scalar2=None,
                                                  op0=ALU.mult), reads=[wk32.d, gk.d], writes=[wk32.d])
        c.op("dve", lambda e: e.tensor_copy(out=wkb.t[:], in_=wk32.t[:]), reads=[wk32.d], writes=[wkb.d])
        wk4 = wkb.t[:].rearrange("p k (h d) -> p k h d", d=128)
        pq = ph.ps([96, 512], F32, n=4)
        pk = ph.ps([128, 512], F32, n=3)
        t32 = ph.sb([96, 512], F32, n=4)
        ob = ph.sb([128, 512], BF16, n=4)
        for h in range(8):
            for (g0, gs) in GROUPS:
                p1 = pq.next()
                p2 = pq.next()
                for k in range(3):
                    c.op("pe", lambda e: e.matmul(p1.t[:, 0:gs], lhsT=wqb.t[:, k, h * 96:(h + 1) * 96], rhs=cqn.t[:, k, g0:g0 + gs],
                                                  start=(k == 0), stop=(k == 2)), reads=[wqb.d, cqn.d], writes=[p1.d])
                for k in range(3):
                    c.op("pe", lambda e: e.matmul(p2.t[:, 0:gs], lhsT=wqr.t[:, k, h * 96:(h + 1) * 96], rhs=cqn.t[:, k, g0:g0 + gs],
                                                  start=(k == 0), stop=(k == 2)), reads=[wqr.d, cqn.d], writes=[p2.d])
                a1 = t32.next()
                a2 = t32.next()
                c.op("dve", lambda e: e.tensor_tensor(out=a1.t[:, 0:gs], in0=p1.t[:, 0:gs], in1=cosf.t[:, g0:g0 + gs], op=ALU.mult),
                     reads=[p1.d, cosf.d], writes=[a1.d])
                c.op("dve", lambda e: e.tensor_tensor(out=a2.t[:, 0:gs], in0=p2.t[:, 0:gs], in1=sinf.t[:, g0:g0 + gs], op=ALU.mult),
                     reads=[p2.d, sinf.d], writes=[a2.d])
                o = ob.next()
                c.op("dve", lambda e: e.tensor_tensor(out=o.t[0:96, 0:gs], in0=a1.t[:, 0:gs], in1=a2.t[:, 0:gs], op=ALU.add),
                     reads=[a1.d, a2.d], writes=[o.d])
                c.dma("sp", QT[h, :, g0:g0 + gs], o.t[0:96, 0:gs], reads=[o.d])
                p3 = pk.next()
                for k in range(2):
                    c.op("pe", lambda e: e.matmul(p3.t[:, 0:gs], lhsT=wkb.t[:, k, h * 128:h * 128 + 128], rhs=ckn.t[:, k, g0:g0 + gs],
                                                  start=(k == 0), stop=(k == 1)), reads=[wkb.d, ckn.d], writes=[p3.d])
                o2 = ob.next()
                c.op("act", lambda e: e.copy(out=o2.t[0:64, 0:gs], in_=p3.t[0:64, 0:gs]), reads=[p3.d], writes=[o2.d])
                c.dma("sp", KT[h, 0:64, g0:g0 + gs], o2.t[0:64, 0:gs], reads=[o2.d])
        for i in range(NT):
            p3 = pk.next()
            for k in range(2):
                c.op("pe", lambda e: e.matmul(p3.t[:].rearrange("p (h d) -> p h d", d=64), lhsT=ckn.t[:, k, i * 128:(i + 1) * 128],
                                              rhs=wk4[:, k, :, 64:128], start=(k == 0), stop=(k == 1)), reads=[wkb.d, ckn.d], writes=[p3.d])
            o2 = ob.next()
            c.op("act", lambda e: e.copy(out=o2.t[:], in_=p3.t[:]), reads=[p3.d], writes=[o2.d])
            c.dma("sp", VV[i * 128:(i + 1) * 128, :], o2.t[:], reads=[o2.d])
        ph.close()

    def phase_attn(l, ph):
        kt_r = ph.sb([96, T], BF16, n=2)
        qt_r = ph.sb([96, T], BF16, n=2)
        v_r = ph.sb([128, NT, 128], BF16, n=2)
        for vb in v_r.bufs:
            c.op("dve", lambda e: e.memset(vb.t[:], 1.0), writes=[vb.d])
        shf = ph.sb([128, 128], F32)
        c.op("pool", lambda e: e.memset(shf.t[:], 1.0), writes=[shf.d])
        c.op("pool", lambda e: e.affine_select(out=shf.t[:], in_=shf.t[:], pattern=[[-1, 128]], compare_op=ALU.is_equal,
                                               fill=0.0, base=-64, channel_multiplier=1), reads=[shf.d], writes=[shf.d])
        ps_s = ph.ps([128, 512], F32, n=4)
        ps_o = ph.ps([128, 512], F32, n=1)
        ps_z = ph.ps([128, 512], F32, n=1)
        pt_r = ph.sb([128, 512], BF16, n=8)
        oz_r = ph.sb([128, 512], F32, n=2)
        ob_r = ph.sb([64, 512], BF16, n=2)
        scale = 96.0 ** -0.5
        for h in range(8):
            kt = kt_r.next()
            qt = qt_r.next()
            vv = v_r.next()
            c.dma("pool", kt.t[:], KT[h], writes=[kt.d])
            c.dma("pool", qt.t[:], QT[h], writes=[qt.d])
            c.dma("pool", vv.t[:, :, 0:64], VV[:, h * 64:(h + 1) * 64].rearrange("(i p) d -> p i d", p=128), writes=[vv.d])
            for gi, (g0, gs) in enumerate(GROUPS):
                nk = 2 if gi == 0 else NT
                po = ps_o.next()
                LOOK = 3
                pts = {}

                def issue_s(kk):
                    s = ps_s.next()
                    c.op("pe", lambda e: e.matmul(s.t[:, 0:gs], lhsT=kt.t[:, kk * 128:(kk + 1) * 128], rhs=qt.t[:, g0:g0 + gs],
                                                  start=True, stop=True), reads=[kt.d, qt.d], writes=[s.d])
                    pt = pt_r.next()
                    c.op("act", lambda e: e.activation(out=pt.t[:, 0:gs], in_=s.t[:, 0:gs], func=AF.Exp, scale=scale),
                         reads=[s.d], writes=[pt.d])
                    pts[kk] = pt

                for kk in range(min(LOOK, nk)):
                    issue_s(kk)
                for kk in range(nk):
                    if kk + LOOK < nk:
                        issue_s(kk + LOOK)
                    pt = pts.pop(kk)
                    c.op("pe", lambda e: e.matmul(po.t[:, 0:gs], lhsT=vv.t[:, kk, :], rhs=pt.t[:, 0:gs], start=(kk == 0),
                                                  stop=(kk == nk - 1)), reads=[vv.d, pt.d], writes=[po.d])
                    yield
                oz = oz_r.next()
                c.op("dve", lambda e: e.tensor_copy(out=oz.t[0:64, 0:gs], in_=po.t[0:64, 0:gs]), reads=[po.d], writes=[oz.d])
                c.op("dve", lambda e: e.reciprocal(out=oz.t[64:128, 0:gs], in_=po.t[64:128, 0:gs]), reads=[po.d], writes=[oz.d])
                pz = ps_z.next()
                c.op("pe", lambda e: e.matmul(pz.t[:, 0:gs], lhsT=shf.t[:], rhs=oz.t[:, 0:gs], start=True, stop=True),
                     reads=[shf.d, oz.d], writes=[pz.d])
                o = ob_r.next()
                c.op("dve", lambda e: e.tensor_tensor(out=o.t[:, 0:gs], in0=pz.t[0:64, 0:gs], in1=oz.t[0:64, 0:gs], op=ALU.mult),
                     reads=[pz.d, oz.d], writes=[o.d])
                c.dma("sp", BR[0, h * 64:(h + 1) * 64, g0:g0 + gs], o.t[:, 0:gs], reads=[o.d])

    def phase_lru(l, ph):
        NP = T + 3
        xbp = ph.sb([128, T + 6], F32)
        xc = ph.sb([128, NP], F32)
        xcb = ph.sb([128, NP], BF16)
        a_t = ph.sb([128, NP], F32)
        u_t = ph.sb([128, NP], F32)
        hf = ph.sb([128, NP], F32)
        hb = xbp
        gbt = a_t
        cw = ph.sb([128, 4], F32)
        cb = ph.sb([128, 1], F32)
        prm = ph.sb([128, 2, 3], F32)
        cst = ph.sb([128, 2], F32)
        w32 = ph.sb([128, 4, 128], F32)
        wbd = ph.sb([128, 4, 128], BF16)
        tiny = ph.sb([128, 1], F32)
        c.op("pool", lambda e: e.memset(tiny.t[:], 1e-20), writes=[tiny.d])
        pg = ph.ps([128, 512], F32, n=2)
        tmp = ph.sb([128, 512], F32, n=4)
        ob = xcb
        PG = [(i * 512, min(512, NP - i * 512)) for i in range((NP + 511) // 512)]
        for cc in range(4):
            ch = slice(cc * 128, (cc + 1) * 128)
            c.op("pool", lambda e: e.memset(xbp.t[:], 0.0), writes=[xbp.d])
            c.dma("pool", xbp.t[:, 2:2 + TC], XBT[ch, 0:TC], writes=[xbp.d])
            c.dma("pool", xbp.t[:, 261:261 + TL], XBT[ch, TC:T], writes=[xbp.d])
            c.dma("pool", cw.t[:].rearrange("p (j o) -> p j o", o=1), lru_conv_w[l, :, ch].rearrange("j (p o) -> p j o", o=1), writes=[cw.d])
            c.dma("pool", cb.t[:], lru_conv_b[l:l + 1, ch].rearrange("o (p q) -> p (o q)", q=1), writes=[cb.d])
            for d in range(2):
                c.dma("pool", prm.t[:, d, 0:1], lru_ba[l, d:d + 1, ch].rearrange("o (p q) -> p (o q)", q=1), writes=[prm.d])
                c.dma("pool", prm.t[:, d, 1:2], lru_bx[l, d:d + 1, ch].rearrange("o (p q) -> p (o q)", q=1), writes=[prm.d])
                c.dma("pool", prm.t[:, d, 2:3], lru_lambda[l, d:d + 1, ch].rearrange("o (p q) -> p (o q)", q=1), writes=[prm.d])
            c.op("pool", lambda e: e.memset(w32.t[:], 0.0), writes=[w32.d])
            for d in range(2):
                for hh in range(2):
                    c.dma("pool", w32.t[hh * 64:(hh + 1) * 64, d * 2 + 0, hh * 64:(hh + 1) * 64], lru_wa[l, d, cc * 2 + hh], writes=[w32.d])
                    c.dma("pool", w32.t[hh * 64:(hh + 1) * 64, d * 2 + 1, hh * 64:(hh + 1) * 64], lru_wx[l, d, cc * 2 + hh], writes=[w32.d])
            c.op("dve", lambda e: e.tensor_copy(out=wbd.t[:], in_=w32.t[:]), reads=[w32.d], writes=[wbd.d])
            for d in range(2):
                c.op("act", lambda e: e.activation(out=cst.t[:, d:d + 1], in_=prm.t[:, d, 2:3], func=AF.Exp, scale=-1.0),
                     reads=[prm.d], writes=[cst.d])
            c.op("dve", lambda e: e.tensor_scalar(out=cst.t[:], in0=cst.t[:], scalar1=1.0, scalar2=None, op0=ALU.add),
                 reads=[cst.d], writes=[cst.d])
            c.op("act", lambda e: e.activation(out=cst.t[:], in_=cst.t[:], func=AF.Ln), reads=[cst.d], writes=[cst.d])
            c.op("dve", lambda e: e.tensor_scalar(out=cst.t[:], in0=cst.t[:], scalar1=-8.0, scalar2=None, op0=ALU.mult),
                 reads=[cst.d], writes=[cst.d])
            c.op("dve", lambda e: e.tensor_scalar(out=xc.t[:], in0=xbp.t[:, 0:NP], scalar1=cw.t[:, 0:1], scalar2=cb.t[:, 0:1],
                                                  op0=ALU.mult, op1=ALU.add), reads=[xbp.d, cw.d, cb.d], writes=[xc.d])
            for j in range(1, 4):
                c.op("dve", lambda e: e.scalar_tensor_tensor(out=xc.t[:], in0=xbp.t[:, j:j + NP], scalar=cw.t[:, j:j + 1], in1=xc.t[:],
                                                             op0=ALU.mult, op1=ALU.add), reads=[xbp.d, cw.d, xc.d], writes=[xc.d])
            c.op("act", lambda e: e.copy(out=xcb.t[:], in_=xc.t[:]), reads=[xc.d], writes=[xcb.d])
            for d in range(2):
                for (p0, gs) in PG:
                    pr = pg.next()
                    pi = pg.next()
                    c.op("pe", lambda e: e.matmul(pr.t[:, 0:gs], lhsT=wbd.t[:, d * 2, :], rhs=xcb.t[:, p0:p0 + gs], start=True, stop=True),
                         reads=[wbd.d, xcb.d], writes=[pr.d])
                    c.op("pe", lambda e: e.matmul(pi.t[:, 0:gs], lhsT=wbd.t[:, d * 2 + 1, :], rhs=xcb.t[:, p0:p0 + gs], start=True, stop=True),
                         reads=[wbd.d, xcb.d], writes=[pi.d])
                    r = tmp.next()
                    ig = tmp.next()
                    c.op("act", lambda e: e.activation(out=r.t[:, 0:gs], in_=pr.t[:, 0:gs], func=AF.Sigmoid, bias=prm.t[:, d, 0:1], scale=1.0),
                         reads=[pr.d, prm.d], writes=[r.d])
                    c.op("act", lambda e: e.activation(out=ig.t[:, 0:gs], in_=pi.t[:, 0:gs], func=AF.Sigmoid, bias=prm.t[:, d, 1:2], scale=1.0),
                         reads=[pi.d, prm.d], writes=[ig.d])
                    c.op("act", lambda e: e.activation(out=a_t.t[:, p0:p0 + gs], in_=r.t[:, 0:gs], func=AF.Exp, scale=cst.t[:, d:d + 1]),
                         reads=[r.d, cst.d], writes=[a_t.d])
                    c.op("dve", lambda e: e.tensor_tensor(out=r.t[:, 0:gs], in0=a_t.t[:, p0:p0 + gs], in1=a_t.t[:, p0:p0 + gs], op=ALU.mult),
                         reads=[a_t.d], writes=[r.d])
                    c.op("dve", lambda e: e.tensor_scalar(out=r.t[:, 0:gs], in0=r.t[:, 0:gs], scalar1=-1.0, scalar2=1.0, op0=ALU.mult,
                                                          op1=ALU.add), reads=[r.d], writes=[r.d])
                    c.op("act", lambda e: e.activation(out=r.t[:, 0:gs], in_=r.t[:, 0:gs], func=AF.Sqrt, bias=tiny.t[:, 0:1], scale=1.0),
                         reads=[r.d, tiny.d], writes=[r.d])
                    c.op("dve", lambda e: e.tensor_tensor(out=r.t[:, 0:gs], in0=r.t[:, 0:gs], in1=ig.t[:, 0:gs], op=ALU.mult),
                         reads=[r.d, ig.d], writes=[r.d])
                    c.op("dve", lambda e: e.tensor_tensor(out=u_t.t[:, p0:p0 + gs], in0=r.t[:, 0:gs], in1=xc.t[:, p0:p0 + gs], op=ALU.mult),
                         reads=[r.d, xc.d], writes=[u_t.d])
                    yield
                if d == 0:
                    c.op("dve", lambda e: e.tensor_tensor_scan(out=hf.t[:, 0:TC], data0=a_t.t[:, 0:TC], data1=u_t.t[:, 0:TC], initial=0.0,
                                                               op0=ALU.mult, op1=ALU.add), reads=[a_t.d, u_t.d], writes=[hf.d])
                    c.op("dve", lambda e: e.tensor_tensor_scan(out=hf.t[:, 259:259 + TL], data0=a_t.t[:, 259:259 + TL],
                                                               data1=u_t.t[:, 259:259 + TL], initial=hf.t[:, TC - 1:TC],
                                                               op0=ALU.mult, op1=ALU.add), reads=[a_t.d, u_t.d, hf.d], writes=[hf.d])
                else:
                    c.op("dve", lambda e: e.tensor_tensor_scan(out=hb.t[:, 0:TC][:, ::-1], data0=a_t.t[:, 0:TC][:, ::-1],
                                                               data1=u_t.t[:, 0:TC][:, ::-1], initial=0.0,
                                                               op0=ALU.mult, op1=ALU.add), reads=[a_t.d, u_t.d], writes=[hb.d])
                    c.op("dve", lambda e: e.tensor_tensor_scan(out=hb.t[:, 259:259 + TL][:, ::-1], data0=a_t.t[:, 259:259 + TL][:, ::-1],
                                                               data1=u_t.t[:, 259:259 + TL][:, ::-1], initial=hb.t[:, 0:1],
                                                               op0=ALU.mult, op1=ALU.add), reads=[a_t.d, u_t.d, hb.d], writes=[hb.d])
            c.dma("pool", gbt.t[:, 0:TC], GBT[ch, 0:TC], writes=[gbt.d])
            c.dma("pool", gbt.t[:, 259:259 + TL], GBT[ch, TC:T], writes=[gbt.d])
            for (a0, n) in ((0, TC), (259, TL)):
                c.op("dve", lambda e: e.tensor_tensor(out=hf.t[:, a0:a0 + n], in0=hf.t[:, a0:a0 + n], in1=hb.t[:, a0:a0 + n], op=ALU.add),
                     reads=[hf.d, hb.d], writes=[hf.d])
                c.op("dve", lambda e: e.tensor_tensor(out=ob.t[:, a0:a0 + n], in0=hf.t[:, a0:a0 + n], in1=gbt.t[:, a0:a0 + n], op=ALU.mult),
                     reads=[hf.d, gbt.d], writes=[ob.d])
            c.dma("sp", BR[3, ch, 0:TC], ob.t[:, 0:TC], reads=[ob.d])
            c.dma("sp", BR[3, ch, TC:T], ob.t[:, 259:259 + TL], reads=[ob.d])
            yield

    def phase_conf(l, ph):
        cw = ph.sb([128, 4, 31], F32)
        cbias = ph.sb([128, 4], F32)
        lg = ph.sb([128, 4], F32)
        lb = ph.sb([128, 4], F32)
        for cc in range(4):
            ch = slice(cc * 128, (cc + 1) * 128)
            c.dma("pool", cw.t[:, cc, :].rearrange("p (j o) -> p j o", o=1), cv_w[l, :, ch].rearrange("j (p o) -> p j o", o=1), writes=[cw.d])
            c.dma("pool", cbias.t[:, cc:cc + 1], cv_b[l:l + 1, ch].rearrange("o (p q) -> p (o q)", q=1), writes=[cbias.d])
            c.dma("pool", lg.t[:, cc:cc + 1], cv_ln_g[l:l + 1, ch].rearrange("o (p q) -> p (o q)", q=1), writes=[lg.d])
            c.dma("pool", lb.t[:, cc:cc + 1], cv_ln_b[l:l + 1, ch].rearrange("o (p q) -> p (o q)", q=1), writes=[lb.d])
        epsb = ph.sb([128, 1], F32)
        c.op("pool", lambda e: e.memset(epsb.t[:], EPS), writes=[epsb.d])
        in_r = ph.sb([128, 542], F32, n=8)
        acc_r = ph.sb([128, 512], F32, n=8)
        sq_r = ph.sb([128, 512], F32, n=4)
        ps1 = ph.ps([128, 512], F32, n=1)
        ps2 = ph.ps([128, 512], F32, n=1)
        mean_r = ph.sb([128, 512], F32, n=2)
        rstd_r = ph.sb([128, 512], F32, n=2)
        ob_r = ph.sb([128, 512], BF16, n=4)
        for gi, (g0, gs) in enumerate(GROUPS):
            ins = []
            accs = []
            for cc in range(4):
                ch = slice(cc * 128, (cc + 1) * 128)
                it = in_r.next()
                if gi == 0:
                    c.op("pool", lambda e: e.memset(it.t[:], 0.0), writes=[it.d])
                    c.dma("pool", it.t[:, 15:15 + TC], UT[ch, 0:TC], writes=[it.d])
                else:
                    lo = g0 - 15
                    hi = g0 + gs + 15
                    if lo < TC or hi > T:
                        c.op("pool", lambda e: e.memset(it.t[:], 0.0), writes=[it.d])
                    lo_c = max(lo, TC)
                    hi_c = min(hi, T)
                    c.dma("pool", it.t[:, lo_c - lo:hi_c - lo], UT[ch, lo_c:hi_c], writes=[it.d])
                ins.append(it)
                accs.append(acc_r.next())
            for j in range(31):
                for cc in range(4):
                    it, ac = ins[cc], accs[cc]
                    if j == 0:
                        c.op("dve", lambda e: e.tensor_scalar(out=ac.t[:, 0:gs], in0=it.t[:, 0:gs], scalar1=cw.t[:, cc, 0:1],
                                                              scalar2=cbias.t[:, cc:cc + 1], op0=ALU.mult, op1=ALU.add),
                             reads=[it.d, cw.d, cbias.d], writes=[ac.d])
                    else:
                        c.op("dve", lambda e: e.scalar_tensor_tensor(out=ac.t[:, 0:gs], in0=it.t[:, j:j + gs], scalar=cw.t[:, cc, j:j + 1],
                                                                     in1=ac.t[:, 0:gs], op0=ALU.mult, op1=ALU.add),
                             reads=[it.d, cw.d, ac.d], writes=[ac.d])
                yield
            p1 = ps1.next()
            p2 = ps2.next()
            for cc in range(4):
                sq = sq_r.next()
                c.op("act", lambda e: e.activation(out=sq.t[:, 0:gs], in_=accs[cc].t[:, 0:gs], func=AF.Square), reads=[accs[cc].d], writes=[sq.d])
                c.op("pe", lambda e: e.matmul(p1.t[:, 0:gs], lhsT=ones32.t[:], rhs=accs[cc].t[:, 0:gs], start=(cc == 0), stop=(cc == 3)),
                     reads=[ones32.d, accs[cc].d], writes=[p1.d])
                c.op("pe", lambda e: e.matmul(p2.t[:, 0:gs], lhsT=ones32.t[:], rhs=sq.t[:, 0:gs], start=(cc == 0), stop=(cc == 3)),
                     reads=[ones32.d, sq.d], writes=[p2.d])
            mean = mean_r.next()
            rstd = rstd_r.next()
            c.op("act", lambda e: e.activation(out=mean.t[:, 0:gs], in_=p1.t[:, 0:gs], func=AF.Copy, scale=1.0 / 512), reads=[p1.d], writes=[mean.d])
            c.op("dve", lambda e: e.tensor_tensor(out=rstd.t[:, 0:gs], in0=mean.t[:, 0:gs], in1=mean.t[:, 0:gs], op=ALU.mult),
                 reads=[mean.d], writes=[rstd.d])
            c.op("dve", lambda e: e.scalar_tensor_tensor(out=rstd.t[:, 0:gs], in0=p2.t[:, 0:gs], scalar=1.0 / 512, in1=rstd.t[:, 0:gs],
                                                         op0=ALU.mult, op1=ALU.subtract), reads=[p2.d, rstd.d], writes=[rstd.d])
            c.op("act", lambda e: e.activation(out=rstd.t[:, 0:gs], in_=rstd.t[:, 0:gs], func=AF.Sqrt, bias=epsb.t[:, 0:1], scale=1.0),
                 reads=[rstd.d, epsb.d], writes=[rstd.d])
            c.op("dve", lambda e: e.reciprocal(out=rstd.t[:, 0:gs], in_=rstd.t[:, 0:gs]), reads=[rstd.d], writes=[rstd.d])
            for cc in range(4):
                ac = accs[cc]
                c.op("dve", lambda e: e.tensor_tensor(out=ac.t[:, 0:gs], in0=ac.t[:, 0:gs], in1=mean.t[:, 0:gs], op=ALU.subtract),
                     reads=[ac.d, mean.d], writes=[ac.d])
                c.op("dve", lambda e: e.tensor_tensor(out=ac.t[:, 0:gs], in0=ac.t[:, 0:gs], in1=rstd.t[:, 0:gs], op=ALU.mult),
                     reads=[ac.d, rstd.d], writes=[ac.d])
                o = ob_r.next()
                c.op("act", lambda e: e.activation(out=o.t[:, 0:gs], in_=ac.t[:, 0:gs], func=AF.Silu, bias=lb.t[:, cc:cc + 1],
                                                   scale=lg.t[:, cc:cc + 1]), reads=[ac.d, lb.d, lg.d], writes=[o.d])
                c.dma("sp", BR[1, cc * 128:(cc + 1) * 128, g0:g0 + gs], o.t[:, 0:gs], reads=[o.d])

    def phase_fnet(l, ph):
        AB = ph.sb([128, NT, 4, 256], BF16)
        ft_r = ph.sb([128, T], BF16, n=2)
        cs = ph.sb([128, 256], BF16)
        c.dma("pool", cs.t[:], ccsc[:, :], writes=[cs.d])
        pp = ph.ps([128, 512], F32, n=6)
        for g in range(4):
            ft = ft_r.next()
            c.dma("pool", ft.t[:], FT[g * 128:(g + 1) * 128, :], writes=[ft.d])
            for i in range(NT):
                if i % 4 == 0:
                    yield
                p = pp.next()
                c.op("pe", lambda e: e.matmul(p.t[:, 0:256], lhsT=ft.t[:, i * 128:(i + 1) * 128], rhs=cs.t[:], start=True, stop=True),
                     reads=[ft.d, cs.d], writes=[p.d])
                c.op("act" if g % 2 else "dve",
                     (lambda e: e.copy(out=AB.t[:, i, g, :], in_=p.t[:, 0:256])) if g % 2 else
                     (lambda e: e.tensor_copy(out=AB.t[:, i, g, :], in_=p.t[:, 0:256])), reads=[p.d], writes=[AB.d])
        cblk = ph.sb([128, 512], BF16, n=4)
        sblk = ph.sb([128, 512], BF16, n=4)
        ob_r = ph.sb([128, 512], BF16, n=4)
        for gi, (g0, gs) in enumerate(GROUPS):
            if gi == 0:
                tiles = [0, 1]
                cm, sm, col0, nrm = dft_c256, dft_s256, 0, 1.0 / math.sqrt(TC * 128)
            else:
                tiles = list(range(2, NT))
                cm, sm, col0, nrm = dft_c, dft_s, g0 - TC, 1.0 / math.sqrt(TL * 128)
            ps_ = [pp.next() for _ in range(4)]
            for si, i in enumerate(tiles):
                cb_ = cblk.next()
                sb_ = sblk.next()
                c.dma("pool", cb_.t[:, 0:gs], cm[si * 128:(si + 1) * 128, col0:col0 + gs], writes=[cb_.d])
                c.dma("pool", sb_.t[:, 0:gs], sm[si * 128:(si + 1) * 128, col0:col0 + gs], writes=[sb_.d])
                for g in range(4):
                    c.op("pe", lambda e: e.matmul(ps_[g].t[:, 0:gs], lhsT=AB.t[:, i, g, 0:128], rhs=cb_.t[:, 0:gs], start=(si == 0), stop=False),
                         reads=[AB.d, cb_.d], writes=[ps_[g].d])
                    c.op("pe", lambda e: e.matmul(ps_[g].t[:, 0:gs], lhsT=AB.t[:, i, g, 128:256], rhs=sb_.t[:, 0:gs], start=False,
                                                  stop=(si == len(tiles) - 1)), reads=[AB.d, sb_.d], writes=[ps_[g].d])
                yield
            for g in range(4):
                o = ob_r.next()
                c.op("act", lambda e: e.activation(out=o.t[:, 0:gs], in_=ps_[g].t[:, 0:gs], func=AF.Copy, scale=nrm), reads=[ps_[g].d], writes=[o.d])
                c.dma("sp", BR[2, g * 128:(g + 1) * 128, g0:g0 + gs], o.t[:, 0:gs], reads=[o.d])

    def post_norm_tile(i, src, src_reads, gate, lng, lnb, rr, x_r, t_r, final=False):
        xt = x_r.next()
        c.dma("pool", xt.t[:], X[i * 128:(i + 1) * 128, :], writes=[xt.d])
        tt = t_r.next()
        c.op("dve", lambda e: e.tensor_tensor(out=tt.t[:], in0=src, in1=gate.t[:], op=ALU.mult), reads=list(src_reads) + [gate.d], writes=[tt.d])
        c.op("dve", lambda e: e.scalar_tensor_tensor(out=tt.t[:], in0=xt.t[:], scalar=ALPHA, in1=tt.t[:], op0=ALU.mult, op1=ALU.add),
             reads=[xt.d, tt.d], writes=[tt.d])
        mv, rs = ln_stats(None, tt, rr)
        c.op("dve", lambda e: e.tensor_scalar(out=tt.t[:], in0=tt.t[:], scalar1=mv.t[:, 0:1], scalar2=rs.t[:, 0:1], op0=ALU.subtract,
                                              op1=ALU.mult), reads=[tt.d, mv.d, rs.d], writes=[tt.d])
        c.op("dve", lambda e: e.tensor_tensor(out=tt.t[:], in0=tt.t[:], in1=lng.t[:], op=ALU.mult), reads=[tt.d, lng.d], writes=[tt.d])
        c.op("dve", lambda e: e.tensor_tensor(out=tt.t[:], in0=tt.t[:], in1=lnb.t[:], op=ALU.add), reads=[tt.d, lnb.d], writes=[tt.d])
        c.dma("sp", X[i * 128:(i + 1) * 128, :], tt.t[:], reads=[tt.d])

    def load_pn_consts(ph, l, gate_off, g_ap, b_ap):
        gate = [ph.sb([128, D], F32) for _ in range(2)]
        for w in range(2):
            c.dma("pool", gate[w].t[:], MOD[l % 2, w, :, gate_off:gate_off + D], writes=[gate[w].d])
        lng = ph.sb([128, D], F32)
        lnb = ph.sb([128, D], F32)
        c.dma("pool", lng.t[:], bcast_rows(g_ap[l:l + 1, :]), writes=[lng.d])
        c.dma("pool", lnb.t[:], bcast_rows(b_ap[l:l + 1, :]), writes=[lnb.d])
        return gate, lng, lnb

    def phase_merge(l):
        ph = Phase(c)
        wb = ph.sb([128, 16, D], BF16)
        wo = ph.sb([128, 8, D], BF16)
        for k in range(4):
            c.dma("pool", wb.t[:, k * 4:(k + 1) * 4, :], w_branch[l, k].rearrange("(cc p) n -> p cc n", p=128), writes=[wb.d])
        c.dma("pool", wo.t[:], w_out[l].rearrange("(k p) n -> p k n", p=128), writes=[wo.d])
        gate, lng, lnb = load_pn_consts(ph, l, 2 * D, ln1_g, ln1_b)
        rr = ln_rings(ph)
        x_r = ph.sb([128, D], F32, n=2)
        t_r = ph.sb([128, D], F32, n=2)
        br_r = ph.sb([128, 16, 512], BF16, n=2)
        gt_r = ph.sb([128, 8, 512], BF16, n=4)
        mT_r = ph.sb([128, 8, 512], BF16, n=2)
        tmp_r = ph.sb([128, 512], BF16, n=6)
        pp = ph.ps([128, 512], F32, n=3)
        pacc = ph.ps([128, 512], F32, n=2)
        po = ph.ps([128, D], F32, n=1)
        deferred = []
        for gi, (g0, gs) in enumerate(GROUPS):
            br = br_r.next()
            for k in range(4):
                c.dma("pool", br.t[:, k * 4:(k + 1) * 4, 0:gs], BR[k, :, g0:g0 + gs].rearrange("(cc p) t -> p cc t", p=128), writes=[br.d])
            mT = mT_r.next()
            gt_bufs = {}
            tms = []
            for dch in range(8):
                pm_ = pacc.next()
                for k in range(4):
                    if dch == 0:
                        gt = gt_r.next()
                        c.dma("pool", gt.t[:, :, 0:gs],
                              GATES[k * D:(k + 1) * D, g0:g0 + gs].rearrange("(dc p) t -> p dc t", p=128), writes=[gt.d])
                        gt_bufs[k] = gt
                    gt = gt_bufs[k]
                    p = pp.next()
                    for cc in range(4):
                        c.op("pe", lambda e: e.matmul(p.t[:, 0:gs], lhsT=wb.t[:, k * 4 + cc, dch * 128:(dch + 1) * 128], rhs=br.t[:, k * 4 + cc, 0:gs],
                                                      start=(cc == 0), stop=(cc == 3)), reads=[wb.d, br.d], writes=[p.d])
                    tm = tmp_r.next()
                    c.op("dve", lambda e: e.tensor_tensor(out=tm.t[:, 0:gs], in0=p.t[:, 0:gs], in1=gt.t[:, dch, 0:gs], op=ALU.mult),
                         reads=[p.d, gt.d], writes=[tm.d])
                    tms.append((tm, k, dch, pm_))
                    if len(tms) > 1:
                        tm0, k0, dch0, pm0 = tms.pop(0)
                        c.op("pe", lambda e: e.matmul(pm0.t[:, 0:gs], lhsT=idb.t[:], rhs=tm0.t[:, 0:gs], start=(k0 == 0), stop=(k0 == 3)),
                             reads=[idb.d, tm0.d], writes=[pm0.d])
                        if k0 == 3:
                            c.op("act", lambda e: e.copy(out=mT.t[:, dch0, 0:gs], in_=pm0.t[:, 0:gs]), reads=[pm0.d], writes=[mT.d])
            while tms:
                tm0, k0, dch0, pm0 = tms.pop(0)
                c.op("pe", lambda e: e.matmul(pm0.t[:, 0:gs], lhsT=idb.t[:], rhs=tm0.t[:, 0:gs], start=(k0 == 0), stop=(k0 == 3)),
                     reads=[idb.d, tm0.d], writes=[pm0.d])
                if k0 == 3:
                    c.op("act", lambda e: e.copy(out=mT.t[:, dch0, 0:gs], in_=pm0.t[:, 0:gs]), reads=[pm0.d], writes=[mT.d])
            def finish(mT=mT, g0=g0, gs=gs):
                for sub in range(gs // 128):
                    i = g0 // 128 + sub
                    o = po.next()
                    for half in range(2):
                        for k in range(8):
                            c.op("pe", lambda e: e.matmul(o.t[:, half * 512:(half + 1) * 512], lhsT=mT.t[:, k, sub * 128:(sub + 1) * 128],
                                                          rhs=wo.t[:, k, half * 512:(half + 1) * 512], start=(k == 0), stop=(k == 7)),
                                 reads=[mT.d, wo.d], writes=[o.d])
                    post_norm_tile(i, o.t[:], [o.d], gate[1 if i < 2 else 0], lng, lnb, rr, x_r, t_r)

            if deferred:
                deferred.pop()()
            deferred.append(finish)
        while deferred:
            deferred.pop()()
        ph.close()

    def phase_moe(l):
        pm = Phase(c)
        AFFTM = pm.sb([128, NT, NEXP], F32)
        POS = pm.sb([128, NT, NEXP], F32)
        p12 = Phase(c)
        AFFT = p12.sb([NEXP, T], F32)
        phase_lnmod(l, 1, AFFT=AFFT, AFFTM=AFFTM)
        ph = Phase(c)
        lo = ph.sb([NEXP, 1], F32)
        hi = ph.sb([NEXP, 1], F32)
        mid = ph.sb([NEXP, 1], F32)
        cntb = ph.sb([NEXP, 1], F32)
        ge = ph.sb([NEXP, 1], F32)
        dlt = ph.sb([NEXP, 1], F32)
        scr = ph.sb([NEXP, TL], F32)
        mask = ph.sb([NEXP, T], F32)
        zer = ph.sb([NEXP, TL], F32)
        posm = ph.sb([NEXP, T], F32)
        c.op("pool", lambda e: e.memset(zer.t[:], 0.0), writes=[zer.d])
        for (a0, n, kk, base) in ((0, TC, 32, 512.0), (TC, TL, 512, 0.0)):
            c.op("dve", lambda e: e.memset(lo.t[:], 0.0), writes=[lo.d])
            c.op("dve", lambda e: e.memset(hi.t[:], 1.0), writes=[hi.d])
            for it in range(34):
                c.op("dve", lambda e: e.tensor_tensor(out=mid.t[:], in0=lo.t[:], in1=hi.t[:], op=ALU.add), reads=[lo.d, hi.d], writes=[mid.d])
                c.op("dve", lambda e: e.tensor_scalar(out=mid.t[:], in0=mid.t[:], scalar1=0.5, scalar2=None, op0=ALU.mult),
                     reads=[mid.d], writes=[mid.d])
                c.op("dve", lambda e: e.tensor_scalar(out=scr.t[:, 0:n], in0=AFFT.t[:, a0:a0 + n], scalar1=mid.t[:, 0:1], scalar2=None,
                                                      op0=ALU.is_ge, op1=ALU.add, accum_out=cntb.t[:]), reads=[AFFT.d, mid.d], writes=[scr.d, cntb.d])
                c.op("dve", lambda e: e.tensor_scalar(out=ge.t[:], in0=cntb.t[:], scalar1=float(kk) - 0.5, scalar2=None, op0=ALU.is_ge),
                     reads=[cntb.d], writes=[ge.d])
                c.op("dve", lambda e: e.tensor_tensor(out=dlt.t[:], in0=mid.t[:], in1=lo.t[:], op=ALU.subtract), reads=[mid.d, lo.d], writes=[dlt.d])
                c.op("dve", lambda e: e.tensor_tensor(out=dlt.t[:], in0=dlt.t[:], in1=ge.t[:], op=ALU.mult), reads=[dlt.d, ge.d], writes=[dlt.d])
                c.op("dve", lambda e: e.tensor_tensor(out=lo.t[:], in0=lo.t[:], in1=dlt.t[:], op=ALU.add), reads=[lo.d, dlt.d], writes=[lo.d])
                c.op("dve", lambda e: e.tensor_tensor(out=dlt.t[:], in0=hi.t[:], in1=mid.t[:], op=ALU.subtract), reads=[hi.d, mid.d], writes=[dlt.d])
                c.op("dve", lambda e: e.tensor_tensor(out=dlt.t[:], in0=dlt.t[:], in1=ge.t[:], op=ALU.mult), reads=[dlt.d, ge.d], writes=[dlt.d])
                c.op("dve", lambda e: e.tensor_tensor(out=hi.t[:], in0=mid.t[:], in1=dlt.t[:], op=ALU.add), reads=[mid.d, dlt.d], writes=[hi.d])
            c.op("dve", lambda e: e.tensor_scalar(out=mask.t[:, a0:a0 + n], in0=AFFT.t[:, a0:a0 + n], scalar1=lo.t[:, 0:1], scalar2=None,
                                                  op0=ALU.is_ge), reads=[AFFT.d, lo.d], writes=[mask.d])
            c.op("dve", lambda e: e.tensor_tensor_scan(out=posm.t[:, a0:a0 + n], data0=mask.t[:, a0:a0 + n], data1=zer.t[:, 0:n], initial=0.0,
                                                       op0=ALU.add, op1=ALU.add), reads=[mask.d, zer.d], writes=[posm.d])
            c.op("dve", lambda e: e.tensor_scalar(out=posm.t[:, a0:a0 + n], in0=posm.t[:, a0:a0 + n], scalar1=base, scalar2=None, op0=ALU.add),
                 reads=[posm.d], writes=[posm.d])
            c.op("dve", lambda e: e.tensor_tensor(out=posm.t[:, a0:a0 + n], in0=posm.t[:, a0:a0 + n], in1=mask.t[:, a0:a0 + n], op=ALU.mult),
                 reads=[posm.d, mask.d], writes=[posm.d])
            c.op("dve", lambda e: e.tensor_scalar(out=posm.t[:, a0:a0 + n], in0=posm.t[:, a0:a0 + n], scalar1=-1.0, scalar2=None, op0=ALU.add),
                 reads=[posm.d], writes=[posm.d])
        c.dma("sp", POSD.rearrange("i (e t) -> e i t", e=NEXP), posm.t[:].rearrange("e (i t) -> e i t", t=128), reads=[posm.d])
        ptp = ph.ps([128, NEXP], F32, n=2)
        for i in range(NT):
            p = ptp.next()
            c.op("pe", lambda e: e.transpose(out=p.t[:], in_=posm.t[:, i * 128:(i + 1) * 128], identity=idf.t[0:NEXP, 0:NEXP]),
                 reads=[posm.d, idf.d], writes=[p.d])
            c.op("act", lambda e: e.copy(out=POS.t[:, i, :], in_=p.t[:]), reads=[p.d], writes=[POS.d])
        ph.close()
        p12.close()
        WSL = pm.sb([128, NEXP, 5], F32)
        IDX = pm.sb([128, NEXP, 5], mybir.dt.int32)
        ph = Phase(c)
        ymoe_d = Dep()
        zt = ph.sb([128, D], F32)
        c.op("dve", lambda e: e.memset(zt.t[:], 0.0), writes=[zt.d])
        sc_state = {"prev": {}, "cur": {}, "ex": -1}

        def tok_add(dct):
            k_, v_ = c.last_tok
            if dct.get(k_, 0) < v_:
                dct[k_] = v_

        for i in range(NT):
            c.dma("sp", YMOE[i * 128:(i + 1) * 128, :], zt.t[:], reads=[zt.d])
            tok_add(sc_state["cur"])
        iota = ph.sb([128, 512], F32)
        c.op("pool", lambda e: e.iota(iota.t[:], pattern=[[1, 512]], base=0, channel_multiplier=0, allow_small_or_imprecise_dtypes=True),
             writes=[iota.d])
        iotac = ph.sb([128, 32], F32)
        c.op("pool", lambda e: e.iota(iotac.t[:], pattern=[[1, 32]], base=512, channel_multiplier=0, allow_small_or_imprecise_dtypes=True),
             writes=[iotac.d])
        wsp = ph.sb([128, 5, NT * NEXP], BF16)
        r1 = ph.sb([128, NT * NEXP], F32)
        r2 = ph.sb([128, NT * NEXP], F32)
        aff2 = AFFTM.t[:].rearrange("p i e -> p (i e)")
        c.op("dve", lambda e: e.tensor_copy(out=wsp.t[:, 0, :], in_=aff2), reads=[AFFTM.d], writes=[wsp.d])
        c.op("dve", lambda e: e.tensor_tensor(out=r1.t[:], in0=aff2, in1=wsp.t[:, 0, :], op=ALU.subtract), reads=[AFFTM.d, wsp.d], writes=[r1.d])
        c.op("dve", lambda e: e.tensor_copy(out=wsp.t[:, 1, :], in_=r1.t[:]), reads=[r1.d], writes=[wsp.d])
        c.op("dve", lambda e: e.tensor_tensor(out=r2.t[:], in0=r1.t[:], in1=wsp.t[:, 1, :], op=ALU.subtract), reads=[r1.d, wsp.d], writes=[r2.d])
        c.op("dve", lambda e: e.tensor_copy(out=wsp.t[:, 2, :], in_=r2.t[:]), reads=[r2.d], writes=[wsp.d])
        c.op("pool", lambda e: e.iota(r1.t[:], pattern=[[1, NT], [0, NEXP]], base=0, channel_multiplier=0, allow_small_or_imprecise_dtypes=True),
             reads=[r1.d], writes=[r1.d])
        c.op("dve", lambda e: e.tensor_copy(out=wsp.t[:, 3, :], in_=r1.t[:]), reads=[r1.d], writes=[wsp.d])
        c.op("pool", lambda e: e.iota(r2.t[:], pattern=[[0, NT * NEXP]], base=0, channel_multiplier=1, allow_small_or_imprecise_dtypes=True),
             reads=[r2.d], writes=[r2.d])
        c.op("dve", lambda e: e.tensor_copy(out=wsp.t[:, 4, :], in_=r2.t[:]), reads=[r2.d], writes=[wsp.d])
        sel_r = ph.sb([128, 32 * 512 + 64], BF16, n=2)
        wsl5 = ph.sb([128, NEXP, 5, 5], F32)
        IDXF = ph.sb([128, NEXP, 5], F32)
        pws = ph.ps([128, 8], F32, n=2)
        JC = [(0, 128), (128, 128), (256, 128), (384, 128), (512, 32)]
        c.op("dve", lambda e: e.memset(wsl5.t[:], 0.0), writes=[wsl5.d])
        for ex in range(NEXP):
            sel = sel_r.next()

            def sel_ap(i):
                if i < 2:
                    return sel.t[:, 32 * 512 + i * 32:32 * 512 + (i + 1) * 32]
                return sel.t[:, (i - 2) * 512:(i - 1) * 512]

            for i in range(NT):
                src = iotac if i < 2 else iota
                c.op("dve", lambda e: e.tensor_scalar(out=sel_ap(i), in0=src.t[:], scalar1=POS.t[:, i, ex:ex + 1], scalar2=None, op0=ALU.is_equal),
                     reads=[src.d, POS.d], writes=[sel.d])
            for jc, (j0, jn) in enumerate(JC):
                p = pws.next()
                tl = [0, 1] if jc == 4 else list(range(2, NT))
                for ti, i in enumerate(tl):
                    sa = sel_ap(i)
                    lhs = sa[:, 0:32] if jc == 4 else sa[:, j0:j0 + jn]
                    c.op("pe", lambda e: e.matmul(p.t[0:jn, 0:5], lhsT=lhs, rhs=wsp.t[:, :, i * NEXP + ex], start=(ti == 0), stop=(ti == len(tl) - 1)),
                         reads=[sel.d, wsp.d], writes=[p.d])
                c.op("act", lambda e: e.copy(out=wsl5.t[0:jn, ex, jc, :], in_=p.t[0:jn, 0:5]), reads=[p.d], writes=[wsl5.d])
        c.op("dve", lambda e: e.tensor_tensor(out=WSL.t[:], in0=wsl5.t[:, :, :, 2], in1=wsl5.t[:, :, :, 1], op=ALU.add), reads=[wsl5.d], writes=[WSL.d])
        c.op("dve", lambda e: e.tensor_tensor(out=WSL.t[:], in0=WSL.t[:], in1=wsl5.t[:, :, :, 0], op=ALU.add), reads=[WSL.d, wsl5.d], writes=[WSL.d])
        c.op("dve", lambda e: e.scalar_tensor_tensor(out=IDXF.t[:], in0=wsl5.t[:, :, :, 3], scalar=128.0, in1=wsl5.t[:, :, :, 4], op0=ALU.mult, op1=ALU.add),
             reads=[wsl5.d], writes=[IDXF.d])
        c.op("dve", lambda e: e.tensor_copy(out=IDX.t[:], in_=IDXF.t[:]), reads=[IDXF.d], writes=[IDX.d])
        ph.close()

        ph = Phase(c)
        xg_r = ph.sb([128, 5, D], BF16, n=2)
        xeT = ph.sb([128, 8, NSLOT], BF16)
        heT = ph.sb([128, NFC, NSLOT], BF16)
        wgu_r = ph.sb([128, 2, 8, 512], BF16, n=2)
        wd_r = ph.sb([128, NFC, 512], BF16, n=4)
        ptx = ph.ps([128, NSLOT], BF16, n=2)
        pg_ = ph.ps([128, 512], F32, n=2)
        pgc = ph.ps([128, 32], F32, n=1)
        pup = ph.ps([128, 512], F32, n=2)
        pupc = ph.ps([128, 32], F32, n=1)
        sg_r = ph.sb([128, NSLOT], F32, n=2)
        ye_r = ph.sb([128, D], BF16, n=10)

        def emit_gather(ex):
            xg = xg_r.next()
            for jc, (j0, jn) in enumerate(JC):
                c.idma(xg.t[0:jn, jc, :], H2[:, :], IDX.t[0:jn, ex, jc:jc + 1], gather=True, reads=[IDX.d], writes=[xg.d])
            return xg

        pending = []
        xg_next = emit_gather(0)
        for ex in range(NEXP):
            xg = xg_next
            wds = []
            for half in range(2):
                wd = wd_r.next()
                c.dma("pool", wd.t[:], w_e_down[l, ex, :, half * 512:(half + 1) * 512].rearrange("(f p) n -> p f n", p=128), writes=[wd.d])
                wds.append(wd)
            if ex + 1 < NEXP:
                xg_next = emit_gather(ex + 1)
            for dch in range(8):
                p = ptx.next()
                for jc, (j0, jn) in enumerate(JC):
                    c.op("pe", lambda e: e.transpose(out=p.t[:, j0:j0 + jn], in_=xg.t[0:jn, jc, dch * 128:(dch + 1) * 128], identity=idb.t[0:jn, 0:jn]),
                         reads=[xg.d, idb.d], writes=[p.d])
                c.op("act", lambda e: e.copy(out=xeT.t[:, dch, :], in_=p.t[:]), reads=[p.d], writes=[xeT.d])
            for fc in range(NFC):
                if fc % 4 == 0:
                    wgu = wgu_r.next()
                    nb = min(512, EFF - fc * 128)
                    c.dma("pool", wgu.t[:, 0, :, 0:nb], w_e_gate[l, ex, :, fc * 128:fc * 128 + nb].rearrange("(k p) n -> p k n", p=128), writes=[wgu.d])
                    c.dma("pool", wgu.t[:, 1, :, 0:nb], w_e_up[l, ex, :, fc * 128:fc * 128 + nb].rearrange("(k p) n -> p k n", p=128), writes=[wgu.d])
                if fc == 8:
                    for fn_ in pending:
                        fn_()
                    pending = []
                fo = (fc % 4) * 128
                pga, pgb, pua, pub = pg_.next(), pgc.next(), pup.next(), pupc.next()
                for (pa, pb, wi) in ((pga, pgb, 0), (pua, pub, 1)):
                    for k in range(8):
                        c.op("pe", lambda e: e.matmul(pa.t[:], lhsT=wgu.t[:, wi, k, fo:fo + 128], rhs=xeT.t[:, k, 0:512], start=(k == 0), stop=(k == 7)),
                             reads=[wgu.d, xeT.d], writes=[pa.d])
                    for k in range(8):
                        c.op("pe", lambda e: e.matmul(pb.t[:], lhsT=wgu.t[:, wi, k, fo:fo + 128], rhs=xeT.t[:, k, 512:544], start=(k == 0), stop=(k == 7)),
                             reads=[wgu.d, xeT.d], writes=[pb.d])
                sg = sg_r.next()
                c.op("act", lambda e: e.activation(out=sg.t[:, 0:512], in_=pga.t[:], func=AF.Silu), reads=[pga.d], writes=[sg.d])
                c.op("act", lambda e: e.activation(out=sg.t[:, 512:544], in_=pgb.t[:], func=AF.Silu), reads=[pgb.d], writes=[sg.d])
                c.op("dve", lambda e: e.tensor_tensor(out=heT.t[:, fc, 0:512], in0=pua.t[:], in1=sg.t[:, 0:512], op=ALU.mult),
                     reads=[pua.d, sg.d], writes=[heT.d])
                c.op("dve", lambda e: e.tensor_tensor(out=heT.t[:, fc, 512:544], in0=pub.t[:], in1=sg.t[:, 512:544], op=ALU.mult),
                     reads=[pub.d, sg.d], writes=[heT.d])
            for jc, (j0, jn) in enumerate(JC):
                ye = ye_r.next()
                for half in range(2):
                    wd = wds[half]
                    p = pg_.next() if half == 0 else pup.next()
                    for fc in range(NFC):
                        c.op("pe", lambda e: e.matmul(p.t[0:jn, :], lhsT=heT.t[:, fc, j0:j0 + jn], rhs=wd.t[:, fc, :],
                                                      start=(fc == 0), stop=(fc == NFC - 1)), reads=[heT.d, wd.d], writes=[p.d])
                    c.op("act", lambda e: e.activation(out=ye.t[0:jn, half * 512:(half + 1) * 512], in_=p.t[0:jn, :], func=AF.Copy,
                                                       scale=WSL.t[0:jn, ex, jc:jc + 1]), reads=[p.d, WSL.d], writes=[ye.d])

                def scat(ye=ye, jn=jn, ex=ex, jc=jc):
                    if sc_state["ex"] != ex:
                        sc_state["prev"], sc_state["cur"], sc_state["ex"] = sc_state["cur"], {}, ex
                    dtmp = Dep()
                    dtmp.r = dict(sc_state["prev"])
                    c.idma(YMOE[:, :], ye.t[0:jn, :], IDX.t[0:jn, ex, jc:jc + 1], gather=False, reads=[ye.d, IDX.d], writes=[dtmp], add=True)
                    tok_add(sc_state["cur"])
                pending.append(scat)
        for fn_ in pending:
            fn_()
        ph.close()
        pm.close()
        def phase_m5(l, ph):
            gate, lng, lnb = load_pn_consts(ph, l, 5 * D, ln2_g, ln2_b)
            rr = ln_rings(ph)
            x_r = ph.sb([128, D], F32, n=3)
            t_r = ph.sb([128, D], F32, n=3)
            ym_r = ph.sb([128, D], F32, n=3)
            for i in range(NT):
                ym = ym_r.next()
                c.dma("pool", ym.t[:], YMOE[i * 128:(i + 1) * 128, :], writes=[ym.d])
                post_norm_tile(i, ym.t[:], [ym.d], gate[1 if i < 2 else 0], lng, lnb, rr, x_r, t_r)
                yield

        specs = [(phase_m5, NT, l)]
        if l + 1 < n_layers:
            specs.append((phase_mod, 12, l + 1))
        interleave(specs)

    def interleave(specs):
        phs = [Phase(c) for _ in specs]
        act = [[fn(la, ph), tot, 0] for (fn, tot, la), ph in zip(specs, phs)]
        while act:
            a = min(act, key=lambda a: a[2] / a[1])
            try:
                next(a[0])
                a[2] += 1
            except StopIteration:
                act.remove(a)
        c.barrier()
        for ph in reversed(phs):
            ph.es.close()

    interleave([(phase_mod, 12, 0)])
    for l in range(n_layers):
        if stop_after == "mod":
            break
        ph = Phase(c)
        HT = ph.sb([128, 8, T], BF16)
        phase_lnmod(l, 0, HT=HT)
        phase_win(l, HT)
        ph.close()
        if stop_after == "win":
            break
        phase_qkv(l)
        if stop_after == "qkv":
            break
        interleave([(phase_attn, 8 * (2 + 8 * NT), l), (phase_conf, 9 * 31, l)])
        if stop_after == "lru":
            break
        interleave([(phase_lru, 4 * 19, l), (phase_fnet, NT + 2 + 8 * 32, l)])
        if stop_after == "fnet":
            break
        phase_merge(l)
        if stop_after == "merge":
            break
        phase_moe(l)

    c.barrier()
    for i in range(8):
        c.dma("sp", out_ap[512 * i:512 * (i + 1), :], X[TC + 512 * i:TC + 512 * (i + 1), :])
    c.barrier()
    glob.es.close()
    c.close()
    dbg = {"X": X, "BR": BR, "MOD": MOD, "CKVN": CKVN, "CQN": CQN, "KT": KT, "QT": QT, "VV": VV, "XBT": XBT, "UT": UT, "FT": FT,
           "GBT": GBT, "GATES": GATES, "H2": H2, "POSD": POSD, "YE": YE, "YMOE": YMOE}
    return nc, c, dbg


_CONST = {}


def host_consts():
    if _CONST:
        return _CONST
    bf = ml_dtypes.bfloat16
    inv = (10000.0 ** (-np.arange(8, dtype=np.float32) / 8)).astype(np.float32)
    tt = np.arange(TL)
    row = (tt // 64).astype(np.float32)
    col = (tt % 64).astype(np.float32)
    ang_r = row[None, :] * inv[:, None]
    ang_c = col[None, :] * inv[:, None]
    cosf = np.ones((96, T), np.float32)
    sinf = np.zeros((96, T), np.float32)
    for part, ang in ((0, ang_r), (1, ang_c)):
        for half in range(2):
            r0 = 64 + part * 16 + half * 8
            cosf[r0:r0 + 8, TC:] = np.cos(ang)
            sinf[r0:r0 + 8, TC:] = np.sin(ang)
    _CONST["rope_cos"] = cosf
    _CONST["rope_sin"] = sinf

    def dft(n):
        k = np.arange(n, dtype=np.int64)
        m = (k[:, None] * k[None, :]) % n
        a = 2.0 * np.pi * m.astype(np.float64) / n
        return np.cos(a), np.sin(a)

    cL, sL = dft(TL)
    _CONST["dft_c"] = cL.astype(np.float32).astype(bf)
    _CONST["dft_s"] = (-sL).astype(np.float32).astype(bf)
    c2, s2 = dft(TC)
    _CONST["dft_c256"] = c2.astype(np.float32).astype(bf)
    _CONST["dft_s256"] = (-s2).astype(np.float32).astype(bf)
    cc, sc = dft(128)
    _CONST["ccsc"] = np.concatenate([cc, sc], axis=1).astype(np.float32).astype(bf)
    return _CONST


_PROG = {}


def kernel(**inputs):
    if "nc" not in _PROG:
        _PROG["nc"] = build_program()[0]
    nc = _PROG["nc"]
    cst = host_consts()
    f32 = lambda a: np.ascontiguousarray(np.asarray(a, dtype=np.float32))
    shared = {}
    for k in ("ada_w", "ada_b", "w_in", "q_norm_g", "kv_norm_g", "cv_w", "cv_b", "cv_ln_g", "cv_ln_b", "lru_conv_w", "lru_conv_b",
              "lru_wa", "lru_ba", "lru_wx", "lru_bx", "lru_lambda", "w_branch", "w_out", "ln1_g", "ln1_b", "w_router",
              "w_e_gate", "w_e_up", "w_e_down", "ln2_g", "ln2_b"):
        shared[k] = f32(inputs[k])
    shared["w_uq"] = f32(inputs["w_uq"]).reshape(DEPTH, 384, 768)
    shared["w_ukv"] = f32(inputs["w_ukv"]).reshape(DEPTH, 256, 1024)
    shared.update(cst)
    x = f32(inputs["x"])
    ctx = f32(inputs["ctx"])
    cc = f32(inputs["c"])
    c_ctx = f32(inputs["c_ctx"])
    in_maps = []
    for core in range(8):
        b = core % 4
        m = dict(shared)
        m["x"] = x[b]
        m["ctx"] = ctx[b]
        m["cvec"] = np.ascontiguousarray(np.stack([cc[b], c_ctx], axis=0))
        in_maps.append(m)
    res = run_bass_kernel_spmd(nc, in_maps, core_ids=list(range(8)))
    out = np.stack([np.asarray(res.results[b]["out"], dtype=np.float32) for b in range(4)], axis=0)
    return out
```

```python
from contextlib import ExitStack
import math
import numpy as np
import ml_dtypes
import concourse.bass as bass
import concourse.mybir as mybir
from concourse.bass_utils import run_bass_kernel_spmd

F32 = mybir.dt.float32
BF16 = mybir.dt.bfloat16
AF = mybir.ActivationFunctionType
ALU = mybir.AluOpType

DMA_RING = 6

D = 1024
TC = 256
TL = 4096
T = TC + TL
NT = T // 128
DEPTH = 4
IN_W = 7328
NEXP = 16
EFF = 1408
NFC = EFF // 128
ALPHA = (2 * DEPTH) ** 0.25
EPS = 1e-6
GROUPS = [(0, 256)] + [(256 + 512 * i, 512) for i in range(8)]
O_CKV, O_KR, O_XB, O_CQ, O_GLU, O_F, O_GB, O_GL = 0, 256, 288, 800, 1184, 2208, 2720, 3232
NSLOT = 544


class Dep:
    __slots__ = ("w", "r")

    def __init__(self):
        self.w = None
        self.r = {}


class Ctx:
    def __init__(self, nc):
        self.nc = nc
        self.es = ExitStack()
        self.E = {"pe": nc.tensor, "act": nc.scalar, "dve": nc.vector, "pool": nc.gpsimd, "sp": nc.sync}
        self.sem = {}
        self.cnt = {}
        for e in ("pe", "act", "dve", "pool"):
            self.sem[e] = self.es.enter_context(nc.semaphore("s_" + e))
            self.cnt[e] = 0
        self.dma_i = {}
        for q in ("sp", "pool", "act"):
            self.dma_i[q] = 0
            for s in range(DMA_RING):
                k = ("d", q, s)
                self.sem[k] = self.es.enter_context(nc.semaphore("d_%s%d" % (q, s)))
                self.cnt[k] = 0
        self.seen = {e: {} for e in self.E}
        self.n_inst = 0
        self.n_wait = 0
        self.uid = 0

    def close(self):
        self.es.close()

    def _need(self, eng, reads, writes):
        need = {}

        def add(tok):
            if tok is None:
                return
            k, v = tok
            if k == "pe" and eng == "pe":
                return
            if need.get(k, 0) < v:
                need[k] = v

        for d in reads:
            add(d.w)
        for d in writes:
            add(d.w)
            for k, v in d.r.items():
                add((k, v))
        return need

    def _emit_waits(self, eng, need):
        E = self.E[eng]
        seen = self.seen[eng]
        for k, v in need.items():
            if seen.get(k, 0) >= v:
                continue
            E.wait_ge(self.sem[k], v)
            seen[k] = v
            self.n_wait += 1

    def _mark(self, tok, reads, writes):
        k, v = tok
        for d in reads:
            if d.r.get(k, 0) < v:
                d.r[k] = v
        for d in writes:
            d.w = tok
            d.r = {}

    def op(self, eng, fn, reads=(), writes=()):
        need = self._need(eng, reads, writes)
        self._emit_waits(eng, need)
        inst = fn(self.E[eng])
        self.cnt[eng] += 1
        inst.then_inc(self.sem[eng], 1)
        self._mark((eng, self.cnt[eng]), reads, writes)
        self.n_inst += 1
        return inst

    def dma(self, q, out, in_, reads=(), writes=(), **kw):
        E = self.E[q]
        slot = self.dma_i[q] % DMA_RING
        self.dma_i[q] += 1
        k = ("d", q, slot)
        need = self._need(q, reads, writes)
        if self.cnt[k] > 0 and need.get(k, 0) < self.cnt[k]:
            need[k] = self.cnt[k]
        self._emit_waits(q, need)
        inst = E.dma_start(out=out, in_=in_, **kw)
        self.cnt[k] += 16
        inst.then_inc(self.sem[k], 16)
        self._mark((k, self.cnt[k]), reads, writes)
        self.last_tok = (k, self.cnt[k])
        self.n_inst += 1
        return inst

    def idma(self, out, in_, idx_ap, gather, reads=(), writes=(), add=False):
        q = "pool"
        E = self.E[q]
        slot = self.dma_i[q] % DMA_RING
        self.dma_i[q] += 1
        k = ("d", q, slot)
        need = self._need(q, reads, writes)
        if self.cnt[k] > 0 and need.get(k, 0) < self.cnt[k]:
            need[k] = self.cnt[k]
        self._emit_waits(q, need)
        off = bass.IndirectOffsetOnAxis(ap=idx_ap, axis=0)
        kw = {"compute_op": ALU.add} if add else {}
        if gather:
            inst = E.indirect_dma_start(out=out, out_offset=None, in_=in_, in_offset=off, **kw)
        else:
            inst = E.indirect_dma_start(out=out, out_offset=off, in_=in_, in_offset=None, **kw)
        self.cnt[k] += 16
        inst.then_inc(self.sem[k], 16)
        self._mark((k, self.cnt[k]), reads, writes)
        self.last_tok = (k, self.cnt[k])
        self.n_inst += 1
        return inst

    def barrier(self):
        allk = {k: v for k, v in self.cnt.items() if v > 0}
        for e in self.E:
            self._emit_waits(e, dict(allk))


class Buf:
    __slots__ = ("t", "d")

    def __init__(self, t):
        self.t = t
        self.d = Dep()


class Ring:
    def __init__(self, bufs):
        self.bufs = bufs
        self.i = 0

    def next(self):
        b = self.bufs[self.i % len(self.bufs)]
        self.i += 1
        return b


class Phase:
    def __init__(self, c):
        self.c = c
        self.es = ExitStack()

    def sb(self, shape, dt, n=None):
        c = self.c
        out = []
        for _ in range(n or 1):
            c.uid += 1
            out.append(Buf(self.es.enter_context(c.nc.sbuf_tensor("sb%d" % c.uid, list(shape), dt))))
        return out[0] if n is None else Ring(out)

    def ps(self, shape, dt=F32, n=None):
        c = self.c
        out = []
        for _ in range(n or 1):
            c.uid += 1
            out.append(Buf(self.es.enter_context(c.nc.psum_tensor("ps%d" % c.uid, list(shape), dt))))
        return out[0] if n is None else Ring(out)

    def close(self):
        self.c.barrier()
        self.es.close()


def build_program(n_layers=DEPTH, stop_after=None, debug=False):
    nc = bass.Bass("TRN2", target_bir_lowering=False)

    def din(name, shape, dt=F32):
        return nc.dram_tensor(name, list(shape), dt, kind="ExternalInput").ap()

    def dscr(name, shape, dt=F32):
        return nc.dram_tensor(name, list(shape), dt, kind="ExternalOutput" if debug else "Internal").ap()

    L = DEPTH
    x_in = din("x", [TL, D])
    ctx_in = din("ctx", [TC, D])
    cvec_in = din("cvec", [2, D])
    ada_w = din("ada_w", [L, D, 6 * D])
    ada_b = din("ada_b", [L, 6 * D])
    w_in = din("w_in", [L, D, IN_W])
    q_norm_g = din("q_norm_g", [L, 384])
    w_uq = din("w_uq", [L, 384, 768])
    kv_norm_g = din("kv_norm_g", [L, 256])
    w_ukv = din("w_ukv", [L, 256, 1024])
    cv_w = din("cv_w", [L, 31, 512])
    cv_b = din("cv_b", [L, 512])
    cv_ln_g = din("cv_ln_g", [L, 512])
    cv_ln_b = din("cv_ln_b", [L, 512])
    lru_conv_w = din("lru_conv_w", [L, 4, 512])
    lru_conv_b = din("lru_conv_b", [L, 512])
    lru_wa = din("lru_wa", [L, 2, 8, 64, 64])
    lru_ba = din("lru_ba", [L, 2, 512])
    lru_wx = din("lru_wx", [L, 2, 8, 64, 64])
    lru_bx = din("lru_bx", [L, 2, 512])
    lru_lambda = din("lru_lambda", [L, 2, 512])
    w_branch = din("w_branch", [L, 4, 512, D])
    w_out = din("w_out", [L, D, D])
    ln1_g = din("ln1_g", [L, D])
    ln1_b = din("ln1_b", [L, D])
    w_router = din("w_router", [L, D, NEXP])
    w_e_gate = din("w_e_gate", [L, NEXP, D, EFF])
    w_e_up = din("w_e_up", [L, NEXP, D, EFF])
    w_e_down = din("w_e_down", [L, NEXP, EFF, D])
    ln2_g = din("ln2_g", [L, D])
    ln2_b = din("ln2_b", [L, D])
    rope_cos = din("rope_cos", [96, T])
    rope_sin = din("rope_sin", [96, T])
    dft_c = din("dft_c", [TL, TL], BF16)
    dft_s = din("dft_s", [TL, TL], BF16)
    dft_c256 = din("dft_c256", [TC, TC], BF16)
    dft_s256 = din("dft_s256", [TC, TC], BF16)
    ccsc = din("ccsc", [128, 256], BF16)
    out_ap = nc.dram_tensor("out", [TL, D], F32, kind="ExternalOutput").ap()

    X = dscr("X", [T, D])
    MOD = dscr("MOD", [2, 2, 128, 6 * D])
    CKVN = dscr("CKVN", [256, T], BF16)
    CQN = dscr("CQN", [384, T], BF16)
    KT = dscr("KT", [8, 96, T], BF16)
    QT = dscr("QT", [8, 96, T], BF16)
    VV = dscr("VV", [T, 512], BF16)
    XBT = dscr("XBT", [512, T])
    UT = dscr("UT", [512, T])
    FT = dscr("FT", [512, T], BF16)
    GBT = dscr("GBT", [512, T])
    GATES = dscr("GATES", [4 * D, T], BF16)
    BR = dscr("BR", [4, 512, T], BF16)
    H2 = dscr("H2", [T, D], BF16)
    POSD = dscr("POSD", [NT, NEXP * 128])
    YE = dscr("YE", [NEXP, NSLOT, D], BF16)
    YMOE = dscr("YMOE", [T, D])

    c = Ctx(nc)
    c.es.enter_context(nc.allow_non_contiguous_dma(reason="small strided parameter loads"))
    glob = Phase(c)
    idf = glob.sb([128, 128], F32)
    idb = glob.sb([128, 128], BF16)
    ones32 = glob.sb([128, 128], F32)
    onesb = glob.sb([128, 128], BF16)
    c.op("pool", lambda e: e.memset(idf.t[:], 1.0), writes=[idf.d])
    c.op("pool", lambda e: e.affine_select(out=idf.t[:], in_=idf.t[:], pattern=[[-1, 128]], compare_op=ALU.is_equal,
                                           fill=0.0, base=0, channel_multiplier=1), reads=[idf.d], writes=[idf.d])
    c.op("dve", lambda e: e.tensor_copy(out=idb.t[:], in_=idf.t[:]), reads=[idf.d], writes=[idb.d])
    c.op("pool", lambda e: e.memset(ones32.t[:], 1.0), writes=[ones32.d])
    c.op("pool", lambda e: e.memset(onesb.t[:], 1.0), writes=[onesb.d])

    c.dma("sp", X[0:TC, :], ctx_in[:, :])
    for i in range(8):
        c.dma("sp", X[TC + 512 * i:TC + 512 * (i + 1), :], x_in[512 * i:512 * (i + 1), :])
    c.barrier()

    def bcast_rows(ap_row, n=128):
        return ap_row.to_broadcast([n, ap_row.shape[-1]])

    def phase_mod(l, ph):
        cv = ph.sb([128, 2, 8], F32)
        sv = ph.sb([128, 2, 8], F32)
        lh = ph.sb([128, 2, 8, 128], F32)
        brow = ph.sb([1, 6 * D], F32)
        wt = ph.sb([128, 8, 512], F32, n=2)
        pp = ph.ps([128, 512], F32, n=4)
        ob = ph.sb([128, 512], F32, n=3)
        for w in range(2):
            c.dma("pool", cv.t[:, w, :].rearrange("p (k o) -> p k o", o=1),
                  cvec_in[w:w + 1, :].rearrange("o (k p q) -> p (o k) q", p=128, q=1), writes=[cv.d])
        c.dma("pool", brow.t[:], ada_b[l:l + 1, :], writes=[brow.d])
        c.op("act", lambda e: e.activation(out=sv.t[:], in_=cv.t[:], func=AF.Silu), reads=[cv.d], writes=[sv.d])
        for w in range(2):
            for k in range(8):
                c.op("dve", lambda e: e.tensor_copy(out=lh.t[:, w, k, :], in_=sv.t[:, w, k:k + 1].to_broadcast([128, 128])),
                     reads=[sv.d], writes=[lh.d])
        for n in range(12):
            wb = wt.next()
            c.dma("pool", wb.t[:], ada_w[l, :, n * 512:(n + 1) * 512].rearrange("(k p) n -> p k n", p=128),
                  writes=[wb.d])
            for w in range(2):
                p = pp.next()
                for k in range(8):
                    c.op("pe", lambda e: e.matmul(p.t[:], lhsT=lh.t[:, w, k, :], rhs=wb.t[:, k, :], start=(k == 0), stop=False),
                         reads=[lh.d, wb.d], writes=[p.d])
                c.op("pe", lambda e: e.matmul(p.t[:], lhsT=ones32.t[0:1, :], rhs=brow.t[0:1, n * 512:(n + 1) * 512],
                                              start=False, stop=True), reads=[ones32.d, brow.d], writes=[p.d])
                o = ob.next()
                addone = 1.0 if n in (2, 3, 8, 9) else 0.0
                c.op("dve", lambda e: e.tensor_scalar(out=o.t[:], in0=p.t[:], scalar1=addone, scalar2=None, op0=ALU.add),
                     reads=[p.d], writes=[o.d])
                c.dma("sp", MOD[l % 2, w, :, n * 512:(n + 1) * 512], o.t[:], reads=[o.d])
            yield

    def ln_stats(ph, xt, rings):
        st = rings["st"].next()
        mv = rings["mv"].next()
        rs = rings["rs"].next()
        for hh in range(2):
            c.op("dve", lambda e: e.bn_stats(out=st.t[:, hh, :], in_=xt.t[:, hh * 512:(hh + 1) * 512]), reads=[xt.d], writes=[st.d])
        c.op("dve", lambda e: e.bn_aggr(out=mv.t[:], in_=st.t[:].rearrange("p a b -> p (a b)")), reads=[st.d], writes=[mv.d])
        c.op("act", lambda e: e.activation(out=rs.t[:], in_=mv.t[:, 1:2], func=AF.Sqrt, bias=rings["eps"].t[:, 0:1], scale=1.0),
             reads=[mv.d, rings["eps"].d], writes=[rs.d])
        c.op("dve", lambda e: e.reciprocal(out=rs.t[:], in_=rs.t[:]), reads=[rs.d], writes=[rs.d])
        return mv, rs

    def ln_rings(ph):
        r = {"st": ph.sb([128, 2, 6], F32, n=3), "mv": ph.sb([128, 2], F32, n=3), "rs": ph.sb([128, 1], F32, n=3),
             "eps": ph.sb([128, 1], F32)}
        c.op("pool", lambda e: e.memset(r["eps"].t[:], EPS), writes=[r["eps"].d])
        return r

    def phase_lnmod(l, sub, HT=None, AFFT=None, AFFTM=None):
        ph = Phase(c)
        rr = ln_rings(ph)
        xt_r = ph.sb([128, D], F32, n=3)
        hb_r = ph.sb([128, D], BF16, n=2)
        sc = [ph.sb([128, D], F32) for _ in range(2)]
        sh = [ph.sb([128, D], F32) for _ in range(2)]
        o_sh = (0 if sub == 0 else 3) * D
        o_sc = (1 if sub == 0 else 4) * D
        for w in range(2):
            c.dma("pool", sc[w].t[:], MOD[l % 2, w, :, o_sc:o_sc + D], writes=[sc[w].d])
            c.dma("pool", sh[w].t[:], MOD[l % 2, w, :, o_sh:o_sh + D], writes=[sh[w].d])
        if sub == 0:
            ptr = ph.ps([128, 8, 128], BF16, n=3)
        else:
            ptr = ph.ps([128, 8, 128], F32, n=2)
            h32_r = ph.sb([128, 8, 128], F32, n=2)
            wr = ph.sb([128, 8, NEXP], F32)
            c.dma("pool", wr.t[:], w_router[l].rearrange("(k p) e -> p k e", p=128), writes=[wr.d])
            plg = ph.ps([128, NEXP], F32, n=2)
            ex_r = ph.sb([128, NEXP], F32, n=2)
            sm_r = ph.sb([128, 1], F32, n=2)
            pat = ph.ps([NEXP, 128], F32, n=2)
        for i in range(NT):
            w = 1 if i < 2 else 0
            xt = xt_r.next()
            c.dma("pool", xt.t[:], X[i * 128:(i + 1) * 128, :], writes=[xt.d])
            mv, rs = ln_stats(ph, xt, rr)
            c.op("dve", lambda e: e.tensor_scalar(out=xt.t[:], in0=xt.t[:], scalar1=mv.t[:, 0:1], scalar2=rs.t[:, 0:1],
                                                  op0=ALU.subtract, op1=ALU.mult), reads=[xt.d, mv.d, rs.d], writes=[xt.d])
            c.op("dve", lambda e: e.tensor_tensor(out=xt.t[:], in0=xt.t[:], in1=sc[w].t[:], op=ALU.mult),
                 reads=[xt.d, sc[w].d], writes=[xt.d])
            if sub == 0:
                hb = hb_r.next()
                c.op("dve", lambda e: e.tensor_tensor(out=hb.t[:], in0=xt.t[:], in1=sh[w].t[:], op=ALU.add),
                     reads=[xt.d, sh[w].d], writes=[hb.d])
                p = ptr.next()
                for k in range(8):
                    c.op("pe", lambda e: e.transpose(out=p.t[:, k, :], in_=hb.t[:, k * 128:(k + 1) * 128], identity=idb.t[:]),
                         reads=[hb.d, idb.d], writes=[p.d])
                c.op("act", lambda e: e.copy(out=HT.t[:, :, i * 128:(i + 1) * 128], in_=p.t[:]), reads=[p.d], writes=[HT.d])
            else:
                c.op("dve", lambda e: e.tensor_tensor(out=xt.t[:], in0=xt.t[:], in1=sh[w].t[:], op=ALU.add),
                     reads=[xt.d, sh[w].d], writes=[xt.d])
                hb = hb_r.next()
                c.op("act", lambda e: e.copy(out=hb.t[:], in_=xt.t[:]), reads=[xt.d], writes=[hb.d])
                c.dma("sp", H2[i * 128:(i + 1) * 128, :], hb.t[:], reads=[hb.d])
                p = ptr.next()
                for k in range(8):
                    c.op("pe", lambda e: e.transpose(out=p.t[:, k, :], in_=xt.t[:, k * 128:(k + 1) * 128], identity=idf.t[:]),
                         reads=[xt.d, idf.d], writes=[p.d])
                h32 = h32_r.next()
                c.op("act", lambda e: e.copy(out=h32.t[:], in_=p.t[:]), reads=[p.d], writes=[h32.d])
                pl = plg.next()
                for k in range(8):
                    c.op("pe", lambda e: e.matmul(pl.t[:], lhsT=h32.t[:, k, :], rhs=wr.t[:, k, :], start=(k == 0), stop=(k == 7)),
                         reads=[h32.d, wr.d], writes=[pl.d])
                ex = ex_r.next()
                sm = sm_r.next()
                c.op("act", lambda e: e.activation(out=ex.t[:], in_=pl.t[:], func=AF.Exp, accum_out=sm.t[:]),
                     reads=[pl.d], writes=[ex.d, sm.d])
                c.op("dve", lambda e: e.reciprocal(out=sm.t[:], in_=sm.t[:]), reads=[sm.d], writes=[sm.d])
                c.op("dve", lambda e: e.tensor_scalar(out=AFFTM.t[:, i, :], in0=ex.t[:], scalar1=sm.t[:, 0:1], scalar2=None,
                                                      op0=ALU.mult), reads=[ex.d, sm.d], writes=[AFFTM.d])
                pa = pat.next()
                c.op("pe", lambda e: e.transpose(out=pa.t[:], in_=AFFTM.t[:, i, :], identity=idf.t[:]),
                     reads=[AFFTM.d, idf.d], writes=[pa.d])
                c.op("act", lambda e: e.copy(out=AFFT.t[:, i * 128:(i + 1) * 128], in_=pa.t[:]), reads=[pa.d], writes=[AFFT.d])
        ph.close()

    def phase_win(l, HT):
        ph = Phase(c)
        WT = ph.sb([128, 8, 1024], BF16, n=2)
        pj = ph.ps([128, 512], F32, n=6)
        pst = ph.ps([128, 512], F32, n=2)
        t32 = ph.sb([128, 512], F32, n=6)
        tb = ph.sb([128, 512], BF16, n=4)
        cosk = ph.sb([32, T], F32)
        sink = ph.sb([32, T], F32)
        c.dma("pool", cosk.t[:], rope_cos[64:96, :], writes=[cosk.d])
        c.dma("pool", sink.t[:], rope_sin[64:96, :], writes=[sink.d])
        epsb = ph.sb([128, 1], F32)
        c.op("pool", lambda e: e.memset(epsb.t[:], EPS), writes=[epsb.d])

        def load_w(col0, ncols, dst0=0, wb=None):
            if wb is None:
                wb = WT.next()
            c.dma("pool", wb.t[:, :, dst0:dst0 + ncols], w_in[l, :, col0:col0 + ncols].rearrange("(k p) n -> p k n", p=128),
                  writes=[wb.d])
            return wb

        def proj(wb, c0, m, g0, gs):
            p = pj.next()
            for k in range(8):
                c.op("pe", lambda e: e.matmul(p.t[0:m, 0:gs], lhsT=wb.t[:, k, c0:c0 + m], rhs=HT.t[:, k, g0:g0 + gs],
                                              start=(k == 0), stop=(k == 7)), reads=[wb.d, HT.d], writes=[p.d])
            return p

        def rms_section(wb, c0, nch, dst):
            for (g0, gs) in GROUPS:
                ps_ = [proj(wb, c0 + 128 * j, 128, g0, gs) for j in range(nch)]
                pss = pst.next()
                for j in range(nch):
                    sq = t32.next()
                    c.op("act", lambda e: e.activation(out=sq.t[:, 0:gs], in_=ps_[j].t[:, 0:gs], func=AF.Square),
                         reads=[ps_[j].d], writes=[sq.d])
                    c.op("pe", lambda e: e.matmul(pss.t[:, 0:gs], lhsT=ones32.t[:], rhs=sq.t[:, 0:gs], start=(j == 0),
                                                  stop=(j == nch - 1)), reads=[ones32.d, sq.d], writes=[pss.d])
                rs = t32.next()
                c.op("act", lambda e: e.activation(out=rs.t[:, 0:gs], in_=pss.t[:, 0:gs], func=AF.Sqrt, bias=epsb.t[:, 0:1],
                                                   scale=1.0 / (nch * 128)), reads=[pss.d, epsb.d], writes=[rs.d])
                c.op("dve", lambda e: e.reciprocal(out=rs.t[:, 0:gs], in_=rs.t[:, 0:gs]), reads=[rs.d], writes=[rs.d])
                for j in range(nch):
                    o = tb.next()
                    c.op("dve", lambda e: e.tensor_tensor(out=o.t[:, 0:gs], in0=ps_[j].t[:, 0:gs], in1=rs.t[:, 0:gs], op=ALU.mult),
                         reads=[ps_[j].d, rs.d], writes=[o.d])
                    c.dma("sp", dst[j * 128:(j + 1) * 128, g0:g0 + gs], o.t[:, 0:gs], reads=[o.d])

        wa = load_w(O_CKV, 288)
        for (dst, src) in ((0, 8), (8, 0), (16, 24), (24, 16)):
            load_w(O_KR + src, 8, dst0=288 + dst, wb=wa)
        for dst in (0, 16):
            c.op("dve", lambda e: e.tensor_scalar(out=wa.t[:, :, 288 + dst:288 + dst + 8], in0=wa.t[:, :, 288 + dst:288 + dst + 8],
                                                  scalar1=-1.0, scalar2=None, op0=ALU.mult), reads=[wa.d], writes=[wa.d])
        rms_section(wa, 0, 2, CKVN)
        for (g0, gs) in GROUPS:
            pk = proj(wa, 256, 32, g0, gs)
            pr = proj(wa, 288, 32, g0, gs)
            a1 = t32.next()
            a2 = t32.next()
            c.op("dve", lambda e: e.tensor_tensor(out=a1.t[0:32, 0:gs], in0=pk.t[0:32, 0:gs], in1=cosk.t[:, g0:g0 + gs], op=ALU.mult),
                 reads=[pk.d, cosk.d], writes=[a1.d])
            c.op("dve", lambda e: e.tensor_tensor(out=a2.t[0:32, 0:gs], in0=pr.t[0:32, 0:gs], in1=sink.t[:, g0:g0 + gs], op=ALU.mult),
                 reads=[pr.d, sink.d], writes=[a2.d])
            o = tb.next()
            c.op("dve", lambda e: e.tensor_tensor(out=o.t[0:32, 0:gs], in0=a1.t[0:32, 0:gs], in1=a2.t[0:32, 0:gs], op=ALU.add),
                 reads=[a1.d, a2.d], writes=[o.d])
            for h in range(8):
                c.dma("sp", KT[h, 64:96, g0:g0 + gs], o.t[0:32, 0:gs], reads=[o.d])
        wq = load_w(O_CQ, 384)
        rms_section(wq, 0, 3, CQN)
        wx = load_w(O_XB, 512)
        for (g0, gs) in GROUPS:
            for j in range(4):
                p = proj(wx, 128 * j, 128, g0, gs)
                o = t32.next()
                c.op("act", lambda e: e.copy(out=o.t[:, 0:gs], in_=p.t[:, 0:gs]), reads=[p.d], writes=[o.d])
                c.dma("sp", XBT[j * 128:(j + 1) * 128, g0:g0 + gs], o.t[:, 0:gs], reads=[o.d])
        wg = load_w(O_GLU, 1024)
        for (g0, gs) in GROUPS:
            for j in range(4):
                pa = proj(wg, 128 * j, 128, g0, gs)
                pg = proj(wg, 512 + 128 * j, 128, g0, gs)
                sg = t32.next()
                c.op("act", lambda e: e.activation(out=sg.t[:, 0:gs], in_=pg.t[:, 0:gs], func=AF.Sigmoid), reads=[pg.d], writes=[sg.d])
                c.op("dve", lambda e: e.tensor_tensor(out=sg.t[:, 0:gs], in0=pa.t[:, 0:gs], in1=sg.t[:, 0:gs], op=ALU.mult),
                     reads=[pa.d, sg.d], writes=[sg.d])
                c.dma("sp", UT[j * 128:(j + 1) * 128, g0:g0 + gs], sg.t[:, 0:gs], reads=[sg.d])
        wf = load_w(O_F, 512)
        for (g0, gs) in GROUPS:
            for j in range(4):
                p = proj(wf, 128 * j, 128, g0, gs)
                o = tb.next()
                c.op("act", lambda e: e.copy(out=o.t[:, 0:gs], in_=p.t[:, 0:gs]), reads=[p.d], writes=[o.d])
                c.dma("sp", FT[j * 128:(j + 1) * 128, g0:g0 + gs], o.t[:, 0:gs], reads=[o.d])
        wgb = load_w(O_GB, 512)
        for (g0, gs) in GROUPS:
            for j in range(4):
                p = proj(wgb, 128 * j, 128, g0, gs)
                s1 = t32.next()
                c.op("act", lambda e: e.activation(out=s1.t[:, 0:gs], in_=p.t[:, 0:gs], func=AF.Square), reads=[p.d], writes=[s1.d])
                c.op("dve", lambda e: e.tensor_scalar(out=s1.t[:, 0:gs], in0=s1.t[:, 0:gs], scalar1=0.044715, scalar2=1.0,
                                                      op0=ALU.mult, op1=ALU.add), reads=[s1.d], writes=[s1.d])
                c.op("dve", lambda e: e.tensor_tensor(out=s1.t[:, 0:gs], in0=p.t[:, 0:gs], in1=s1.t[:, 0:gs], op=ALU.mult),
                     reads=[p.d, s1.d], writes=[s1.d])
                c.op("act", lambda e: e.activation(out=s1.t[:, 0:gs], in_=s1.t[:, 0:gs], func=AF.Sigmoid, scale=1.5957691216057308),
                     reads=[s1.d], writes=[s1.d])
                c.op("dve", lambda e: e.tensor_tensor(out=s1.t[:, 0:gs], in0=p.t[:, 0:gs], in1=s1.t[:, 0:gs], op=ALU.mult),
                     reads=[p.d, s1.d], writes=[s1.d])
                c.dma("sp", GBT[j * 128:(j + 1) * 128, g0:g0 + gs], s1.t[:, 0:gs], reads=[s1.d])
        for cb in range(4):
            wl = load_w(O_GL + 1024 * cb, 1024)
            for (g0, gs) in GROUPS:
                for j in range(8):
                    p = proj(wl, 128 * j, 128, g0, gs)
                    o = tb.next()
                    c.op("act", lambda e: e.activation(out=o.t[:, 0:gs], in_=p.t[:, 0:gs], func=AF.Sigmoid), reads=[p.d], writes=[o.d])
                    r0 = cb * 1024 + j * 128
                    c.dma("sp", GATES[r0:r0 + 128, g0:g0 + gs], o.t[:, 0:gs], reads=[o.d])
        ph.close()

    def phase_qkv(l):
        ph = Phase(c)
        cqn = ph.sb([128, 3, T], BF16)
        ckn = ph.sb([128, 2, T], BF16)
        cosf = ph.sb([96, T], F32)
        sinf = ph.sb([96, T], F32)
        c.dma("pool", cqn.t[:], CQN.rearrange("(k p) t -> p k t", p=128), writes=[cqn.d])
        c.dma("pool", ckn.t[:], CKVN.rearrange("(k p) t -> p k t", p=128), writes=[ckn.d])
        c.dma("pool", cosf.t[:], rope_cos[:, :], writes=[cosf.d])
        c.dma("pool", sinf.t[:], rope_sin[:, :], writes=[sinf.d])
        wq32 = ph.sb([128, 3, 768], F32)
        gq = ph.sb([128, 3], F32)
        wqb = ph.sb([128, 3, 768], BF16)
        wqr32 = ph.sb([128, 3, 768], F32)
        wqr = ph.sb([128, 3, 768], BF16)
        c.dma("pool", wq32.t[:], w_uq[l].rearrange("(k p) n -> p k n", p=128), writes=[wq32.d])
        c.dma("pool", gq.t[:].rearrange("p (k o) -> p k o", o=1), q_norm_g[l:l + 1, :].rearrange("o (k p q) -> p (o k) q", p=128, q=1),
              writes=[gq.d])
        for k in range(3):
            c.op("dve", lambda e: e.tensor_scalar(out=wq32.t[:, k, :], in0=wq32.t[:, k, :], scalar1=gq.t[:, k:k + 1], scalar2=None,
                                                  op0=ALU.mult), reads=[wq32.d, gq.d], writes=[wq32.d])
        c.op("dve", lambda e: e.tensor_copy(out=wqb.t[:], in_=wq32.t[:]), reads=[wq32.d], writes=[wqb.d])
        c.op("pool", lambda e: e.memset(wqr32.t[:], 0.0), writes=[wqr32.d])
        w4 = wq32.t[:].rearrange("p k (h d) -> p k h d", d=96)
        r4 = wqr32.t[:].rearrange("p k (h d) -> p k h d", d=96)
        for (dst, src, sgn) in ((0, 8, -1.0), (8, 0, 1.0), (16, 24, -1.0), (24, 16, 1.0)):
            for k in range(3):
                c.op("dve", lambda e: e.tensor_scalar(out=r4[:, k, :, 64 + dst:72 + dst], in0=w4[:, k, :, 64 + src:72 + src],
                                                      scalar1=sgn, scalar2=None, op0=ALU.mult), reads=[wq32.d], writes=[wqr32.d])
        c.op("dve", lambda e: e.tensor_copy(out=wqr.t[:], in_=wqr32.t[:]), reads=[wqr32.d], writes=[wqr.d])
        wk32 = ph.sb([128, 2, 1024], F32)
        gk = ph.sb([128, 2], F32)
        wkb = ph.sb([128, 2, 1024], BF16)
        c.dma("pool", wk32.t[:], w_ukv[l].rearrange("(k p) n -> p k n", p=128), writes=[wk32.d])
        c.dma("pool", gk.t[:].rearrange("p (k o) -> p k o", o=1), kv_norm_g[l:l + 1, :].rearrange("o (k p q) -> p (o k) q", p=128, q=1),
              writes=[gk.d])
        for k in range(2):
            c.op("dve", lambda e: e.tensor_scalar(out=wk32.t[:, k, :], in0=wk32.t[:, k, :], scalar1=gk.t[:, k:k + 1], scalar2=None,
                                                  op0=ALU.mult), reads=[wk32.d, gk.d], writes=[wk32.d])
        c.op("dve", lambda e: e.tensor_copy(out=wkb.t[:], in_=wk32.t[:]), reads=[wk32.d], writes=[wkb.d])
        wk4 = wkb.t[:].rearrange("p k (h d) -> p k h d", d=128)
        pq = ph.ps([96, 512], F32, n=4)
        pk = ph.ps([128, 512], F32, n=3)
        t32 = ph.sb([96, 512], F32, n=4)
        ob = ph.sb([128, 512], BF16, n=4)
        for h in range(8):
            for (g0, gs) in GROUPS:
                p1 = pq.next()
                p2 = pq.next()
                for k in range(3):
                    c.op("pe", lambda e: e.matmul(p1.t[:, 0:gs], lhsT=wqb.t[:, k, h * 96:(h + 1) * 96], rhs=cqn.t[:, k, g0:g0 + gs],
                                                  start=(k == 0), stop=(k == 2)), reads=[wqb.d, cqn.d], writes=[p1.d])
                for k in range(3):
                    c.op("pe", lambda e: e.matmul(p2.t[:, 0:gs], lhsT=wqr.t[:, k, h * 96:(h + 1) * 96], rhs=cqn.t[:, k, g0:g0 + gs],
                                                  start=(k == 0), stop=(k == 2)), reads=[wqr.d, cqn.d], writes=[p2.d])
                a1 = t32.next()
                a2 = t32.next()
                c.op("dve", lambda e: e.tensor_tensor(out=a1.t[:, 0:gs], in0=p1.t[:, 0:gs], in1=cosf.t[:, g0:g0 + gs], op=ALU.mult),
                     reads=[p1.d, cosf.d], writes=[a1.d])
                c.op("dve", lambda e: e.tensor_tensor(out=a2.t[:, 0:gs], in0=p2.t[:, 0:gs], in1=sinf.t[:, g0:g0 + gs], op=ALU.mult),
                     reads=[p2.d, sinf.d], writes=[a2.d])
                o = ob.next()
                c.op("dve", lambda e: e.tensor_tensor(out=o.t[0:96, 0:gs], in0=a1.t[:, 0:gs], in1=a2.t[:, 0:gs], op=ALU.add),
                     reads=[a1.d, a2.d], writes=[o.d])
                c.dma("sp", QT[h, :, g0:g0 + gs], o.t[0:96, 0:gs], reads=[o.d])
                p3 = pk.next()
                for k in range(2):
                    c.op("pe", lambda e: e.matmul(p3.t[:, 0:gs], lhsT=wkb.t[:, k, h * 128:h * 128 + 128], rhs=ckn.t[:, k, g0:g0 + gs],
                                                  start=(k == 0), stop=(k == 1)), reads=[wkb.d, ckn.d], writes=[p3.d])
                o2 = ob.next()
                c.op("act", lambda e: e.copy(out=o2.t[0:64, 0:gs], in_=p3.t[0:64, 0:gs]), reads=[p3.d], writes=[o2.d])
                c.dma("sp", KT[h, 0:64, g0:g0 + gs], o2.t[0:64, 0:gs], reads=[o2.d])
        for i in range(NT):
            p3 = pk.next()
            for k in range(2):
                c.op("pe", lambda e: e.matmul(p3.t[:].rearrange("p (h d) -> p h d", d=64), lhsT=ckn.t[:, k, i * 128:(i + 1) * 128],
                                              rhs=wk4[:, k, :, 64:128], start=(k == 0), stop=(k == 1)), reads=[wkb.d, ckn.d], writes=[p3.d])
            o2 = ob.next()
            c.op("act", lambda e: e.copy(out=o2.t[:], in_=p3.t[:]), reads=[p3.d], writes=[o2.d])
            c.dma("sp", VV[i * 128:(i + 1) * 128, :], o2.t[:], reads=[o2.d])
        ph.close()

    def phase_attn(l, ph):
        kt_r = ph.sb([96, T], BF16, n=2)
        qt_r = ph.sb([96, T], BF16, n=2)
        v_r = ph.sb([128, NT, 128], BF16, n=2)
        for vb in v_r.bufs:
            c.op("dve", lambda e: e.memset(vb.t[:], 1.0), writes=[vb.d])
        shf = ph.sb([128, 128], F32)
        c.op("pool", lambda e: e.memset(shf.t[:], 1.0), writes=[shf.d])
        c.op("pool", lambda e: e.affine_select(out=shf.t[:], in_=shf.t[:], pattern=[[-1, 128]], compare_op=ALU.is_equal,
                                               fill=0.0, base=-64, channel_multiplier=1), reads=[shf.d], writes=[shf.d])
        ps_s = ph.ps([128, 512], F32, n=4)
        ps_o = ph.ps([128, 512], F32, n=1)
        ps_z = ph.ps([128, 512], F32, n=1)
        pt_r = ph.sb([128, 512], BF16, n=8)
        oz_r = ph.sb([128, 512], F32, n=2)
        ob_r = ph.sb([64, 512], BF16, n=2)
        scale = 96.0 ** -0.5
        for h in range(8):
            kt = kt_r.next()
            qt = qt_r.next()
            vv = v_r.next()
            c.dma("pool", kt.t[:], KT[h], writes=[kt.d])
            c.dma("pool", qt.t[:], QT[h], writes=[qt.d])
            c.dma("pool", vv.t[:, :, 0:64], VV[:, h * 64:(h + 1) * 64].rearrange("(i p) d -> p i d", p=128), writes=[vv.d])
            for gi, (g0, gs) in enumerate(GROUPS):
                nk = 2 if gi == 0 else NT
                po = ps_o.next()
                LOOK = 3
                pts = {}

                def issue_s(kk):
                    s = ps_s.next()
                    c.op("pe", lambda e: e.matmul(s.t[:, 0:gs], lhsT=kt.t[:, kk * 128:(kk + 1) * 128], rhs=qt.t[:, g0:g0 + gs],
                                                  start=True, stop=True), reads=[kt.d, qt.d], writes=[s.d])
                    pt = pt_r.next()
                    c.op("act", lambda e: e.activation(out=pt.t[:, 0:gs], in_=s.t[:, 0:gs], func=AF.Exp, scale=scale),
                         reads=[s.d], writes=[pt.d])
                    pts[kk] = pt

                for kk in range(min(LOOK, nk)):
                    issue_s(kk)
                for kk in range(nk):
                    if kk + LOOK < nk:
                        issue_s(kk + LOOK)
                    pt = pts.pop(kk)
                    c.op("pe", lambda e: e.matmul(po.t[:, 0:gs], lhsT=vv.t[:, kk, :], rhs=pt.t[:, 0:gs], start=(kk == 0),
                                                  stop=(kk == nk - 1)), reads=[vv.d, pt.d], writes=[po.d])
                    yield
                oz = oz_r.next()
                c.op("dve", lambda e: e.tensor_copy(out=oz.t[0:64, 0:gs], in_=po.t[0:64, 0:gs]), reads=[po.d], writes=[oz.d])
                c.op("dve", lambda e: e.reciprocal(out=oz.t[64:128, 0:gs], in_=po.t[64:128, 0:gs]), reads=[po.d], writes=[oz.d])
                pz = ps_z.next()
                c.op("pe", lambda e: e.matmul(pz.t[:, 0:gs], lhsT=shf.t[:], rhs=oz.t[:, 0:gs], start=True, stop=True),
                     reads=[shf.d, oz.d], writes=[pz.d])
                o = ob_r.next()
                c.op("dve", lambda e: e.tensor_tensor(out=o.t[:, 0:gs], in0=pz.t[0:64, 0:gs], in1=oz.t[0:64, 0:gs], op=ALU.mult),
                     reads=[pz.d, oz.d], writes=[o.d])
                c.dma("sp", BR[0, h * 64:(h + 1) * 64, g0:g0 + gs], o.t[:, 0:gs], reads=[o.d])

    def phase_lru(l, ph):
        NP = T + 3
        xbp = ph.sb([128, T + 6], F32)
        xc = ph.sb([128, NP], F32)
        xcb = ph.sb([128, NP], BF16)
        a_t = ph.sb([128, NP], F32)
        u_t = ph.sb([128, NP], F32)
        hf = ph.sb([128, NP], F32)
        hb = xbp
        gbt = a_t
        cw = ph.sb([128, 4], F32)
        cb = ph.sb([128, 1], F32)
        prm = ph.sb([128, 2, 3], F32)
        cst = ph.sb([128, 2], F32)
        w32 = ph.sb([128, 4, 128], F32)
        wbd = ph.sb([128, 4, 128], BF16)
        tiny = ph.sb([128, 1], F32)
        c.op("pool", lambda e: e.memset(tiny.t[:], 1e-20), writes=[tiny.d])
        pg = ph.ps([128, 512], F32, n=2)
        tmp = ph.sb([128, 512], F32, n=4)
        ob = xcb
        PG = [(i * 512, min(512, NP - i * 512)) for i in range((NP + 511) // 512)]
        for cc in range(4):
            ch = slice(cc * 128, (cc + 1) * 128)
            c.op("pool", lambda e: e.memset(xbp.t[:], 0.0), writes=[xbp.d])
            c.dma("pool", xbp.t[:, 2:2 + TC], XBT[ch, 0:TC], writes=[xbp.d])
            c.dma("pool", xbp.t[:, 261:261 + TL], XBT[ch, TC:T], writes=[xbp.d])
            c.dma("pool", cw.t[:].rearrange("p (j o) -> p j o", o=1), lru_conv_w[l, :, ch].rearrange("j (p o) -> p j o", o=1), writes=[cw.d])
            c.dma("pool", cb.t[:], lru_conv_b[l:l + 1, ch].rearrange("o (p q) -> p (o q)", q=1), writes=[cb.d])
            for d in range(2):
                c.dma("pool", prm.t[:, d, 0:1], lru_ba[l, d:d + 1, ch].rearrange("o (p q) -> p (o q)", q=1), writes=[prm.d])
                c.dma("pool", prm.t[:, d, 1:2], lru_bx[l, d:d + 1, ch].rearrange("o (p q) -> p (o q)", q=1), writes=[prm.d])
                c.dma("pool", prm.t[:, d, 2:3], lru_lambda[l, d:d + 1, ch].rearrange("o (p q) -> p (o q)", q=1), writes=[prm.d])
            c.op("pool", lambda e: e.memset(w32.t[:], 0.0), writes=[w32.d])
            for d in range(2):
                for hh in range(2):
                    c.dma("pool", w32.t[hh * 64:(hh + 1) * 64, d * 2 + 0, hh * 64:(hh + 1) * 64], lru_wa[l, d, cc * 2 + hh], writes=[w32.d])
                    c.dma("pool", w32.t[hh * 64:(hh + 1) * 64, d * 2 + 1, hh * 64:(hh + 1) * 64], lru_wx[l, d, cc * 2 + hh], writes=[w32.d])
            c.op("dve", lambda e: e.tensor_copy(out=wbd.t[:], in_=w32.t[:]), reads=[w32.d], writes=[wbd.d])
            for d in range(2):
                c.op("act", lambda e: e.activation(out=cst.t[:, d:d + 1], in_=prm.t[:, d, 2:3], func=AF.Exp, scale=-1.0),
                     reads=[prm.d], writes=[cst.d])
            c.op("dve", lambda e: e.tensor_scalar(out=cst.t[:], in0=cst.t[:], scalar1=1.0, scalar2=None, op0=ALU.add),
                 reads=[cst.d], writes=[cst.d])
            c.op("act", lambda e: e.activation(out=cst.t[:], in_=cst.t[:], func=AF.Ln), reads=[cst.d], writes=[cst.d])
            c.op("dve", lambda e: e.tensor_scalar(out=cst.t[:], in0=cst.t[:], scalar1=-8.0, scalar2=None, op0=ALU.mult),
                 reads=[cst.d], writes=[cst.d])
            c.op("dve", lambda e: e.tensor_scalar(out=xc.t[:], in0=xbp.t[:, 0:NP], scalar1=cw.t[:, 0:1], scalar2=cb.t[:, 0:1],
                                                  op0=ALU.mult, op1=ALU.add), reads=[xbp.d, cw.d, cb.d], writes=[xc.d])
            for j in range(1, 4):
                c.op("dve", lambda e: e.scalar_tensor_tensor(out=xc.t[:], in0=xbp.t[:, j:j + NP], scalar=cw.t[:, j:j + 1], in1=xc.t[:],
                                                             op0=ALU.mult, op1=ALU.add), reads=[xbp.d, cw.d, xc.d], writes=[xc.d])
            c.op("act", lambda e: e.copy(out=xcb.t[:], in_=xc.t[:]), reads=[xc.d], writes=[xcb.d])
            for d in range(2):
                for (p0, gs) in PG:
                    pr = pg.next()
                    pi = pg.next()
                    c.op("pe", lambda e: e.matmul(pr.t[:, 0:gs], lhsT=wbd.t[:, d * 2, :], rhs=xcb.t[:, p0:p0 + gs], start=True, stop=True),
                         reads=[wbd.d, xcb.d], writes=[pr.d])
                    c.op("pe", lambda e: e.matmul(pi.t[:, 0:gs], lhsT=wbd.t[:, d * 2 + 1, :], rhs=xcb.t[:, p0:p0 + gs], start=True, stop=True),
                         reads=[wbd.d, xcb.d], writes=[pi.d])
                    r = tmp.next()
                    ig = tmp.next()
                    c.op("act", lambda e: e.activation(out=r.t[:, 0:gs], in_=pr.t[:, 0:gs], func=AF.Sigmoid, bias=prm.t[:, d, 0:1], scale=1.0),
                         reads=[pr.d, prm.d], writes=[r.d])
                    c.op("act", lambda e: e.activation(out=ig.t[:, 0:gs], in_=pi.t[:, 0:gs], func=AF.Sigmoid, bias=prm.t[:, d, 1:2], scale=1.0),
                         reads=[pi.d, prm.d], writes=[ig.d])
                    c.op("act", lambda e: e.activation(out=a_t.t[:, p0:p0 + gs], in_=r.t[:, 0:gs], func=AF.Exp, scale=cst.t[:, d:d + 1]),
                         reads=[r.d, cst.d], writes=[a_t.d])
                    c.op("dve", lambda e: e.tensor_tensor(out=r.t[:, 0:gs], in0=a_t.t[:, p0:p0 + gs], in1=a_t.t[:, p0:p0 + gs], op=ALU.mult),
                         reads=[a_t.d], writes=[r.d])
                    c.op("dve", lambda e: e.tensor_scalar(out=r.t[:, 0:gs], in0=r.t[:, 0:gs], scalar1=-1.0, scalar2=1.0, op0=ALU.mult,
                                                          op1=ALU.add), reads=[r.d], writes=[r.d])
                    c.op("act", lambda e: e.activation(out=r.t[:, 0:gs], in_=r.t[:, 0:gs], func=AF.Sqrt, bias=tiny.t[:, 0:1], scale=1.0),
                         reads=[r.d, tiny.d], writes=[r.d])
                    c.op("dve", lambda e: e.tensor_tensor(out=r.t[:, 0:gs], in0=r.t[:, 0:gs], in1=ig.t[:, 0:gs], op=ALU.mult),
                         reads=[r.d, ig.d], writes=[r.d])
                    c.op("dve", lambda e: e.tensor_tensor(out=u_t.t[:, p0:p0 + gs], in0=r.t[:, 0:gs], in1=xc.t[:, p0:p0 + gs], op=ALU.mult),
                         reads=[r.d, xc.d], writes=[u_t.d])
                    yield
                if d == 0:
                    c.op("dve", lambda e: e.tensor_tensor_scan(out=hf.t[:, 0:TC], data0=a_t.t[:, 0:TC], data1=u_t.t[:, 0:TC], initial=0.0,
                                                               op0=ALU.mult, op1=ALU.add), reads=[a_t.d, u_t.d], writes=[hf.d])
                    c.op("dve", lambda e: e.tensor_tensor_scan(out=hf.t[:, 259:259 + TL], data0=a_t.t[:, 259:259 + TL],
                                                               data1=u_t.t[:, 259:259 + TL], initial=hf.t[:, TC - 1:TC],
                                                               op0=ALU.mult, op1=ALU.add), reads=[a_t.d, u_t.d, hf.d], writes=[hf.d])
                else:
                    c.op("dve", lambda e: e.tensor_tensor_scan(out=hb.t[:, 0:TC][:, ::-1], data0=a_t.t[:, 0:TC][:, ::-1],
                                                               data1=u_t.t[:, 0:TC][:, ::-1], initial=0.0,
                                                               op0=ALU.mult, op1=ALU.add), reads=[a_t.d, u_t.d], writes=[hb.d])
                    c.op("dve", lambda e: e.tensor_tensor_scan(out=hb.t[:, 259:259 + TL][:, ::-1], data0=a_t.t[:, 259:259 + TL][:, ::-1],
                                                               data1=u_t.t[:, 259:259 + TL][:, ::-1], initial=hb.t[:, 0:1],
                                                               op0=ALU.mult, op1=ALU.add), reads=[a_t.d, u_t.d, hb.d], writes=[hb.d])
            c.dma("pool", gbt.t[:, 0:TC], GBT[ch, 0:TC], writes=[gbt.d])
            c.dma("pool", gbt.t[:, 259:259 + TL], GBT[ch, TC:T], writes=[gbt.d])
            for (a0, n) in ((0, TC), (259, TL)):
                c.op("dve", lambda e: e.tensor_tensor(out=hf.t[:, a0:a0 + n], in0=hf.t[:, a0:a0 + n], in1=hb.t[:, a0:a0 + n], op=ALU.add),
                     reads=[hf.d, hb.d], writes=[hf.d])
                c.op("dve", lambda e: e.tensor_tensor(out=ob.t[:, a0:a0 + n], in0=hf.t[:, a0:a0 + n], in1=gbt.t[:, a0:a0 + n], op=ALU.mult),
                     reads=[hf.d, gbt.d], writes=[ob.d])
            c.dma("sp", BR[3, ch, 0:TC], ob.t[:, 0:TC], reads=[ob.d])
            c.dma("sp", BR[3, ch, TC:T], ob.t[:, 259:259 + TL], reads=[ob.d])
            yield

    def phase_conf(l, ph):
        cw = ph.sb([128, 4, 31], F32)
        cbias = ph.sb([128, 4], F32)
        lg = ph.sb([128, 4], F32)
        lb = ph.sb([128, 4], F32)
        for cc in range(4):
            ch = slice(cc * 128, (cc + 1) * 128)
            c.dma("pool", cw.t[:, cc, :].rearrange("p (j o) -> p j o", o=1), cv_w[l, :, ch].rearrange("j (p o) -> p j o", o=1), writes=[cw.d])
            c.dma("pool", cbias.t[:, cc:cc + 1], cv_b[l:l + 1, ch].rearrange("o (p q) -> p (o q)", q=1), writes=[cbias.d])
            c.dma("pool", lg.t[:, cc:cc + 1], cv_ln_g[l:l + 1, ch].rearrange("o (p q) -> p (o q)", q=1), writes=[lg.d])
            c.dma("pool", lb.t[:, cc:cc + 1], cv_ln_b[l:l + 1, ch].rearrange("o (p q) -> p (o q)", q=1), writes=[lb.d])
        epsb = ph.sb([128, 1], F32)
        c.op("pool", lambda e: e.memset(epsb.t[:], EPS), writes=[epsb.d])
        in_r = ph.sb([128, 542], F32, n=8)
        acc_r = ph.sb([128, 512], F32, n=8)
        sq_r = ph.sb([128, 512], F32, n=4)
        ps1 = ph.ps([128, 512], F32, n=1)
        ps2 = ph.ps([128, 512], F32, n=1)
        mean_r = ph.sb([128, 512], F32, n=2)
        rstd_r = ph.sb([128, 512], F32, n=2)
        ob_r = ph.sb([128, 512], BF16, n=4)
        for gi, (g0, gs) in enumerate(GROUPS):
            ins = []
            accs = []
            for cc in range(4):
                ch = slice(cc * 128, (cc + 1) * 128)
                it = in_r.next()
                if gi == 0:
                    c.op("pool", lambda e: e.memset(it.t[:], 0.0), writes=[it.d])
                    c.dma("pool", it.t[:, 15:15 + TC], UT[ch, 0:TC], writes=[it.d])
                else:
                    lo = g0 - 15
                    hi = g0 + gs + 15
                    if lo < TC or hi > T:
                        c.op("pool", lambda e: e.memset(it.t[:], 0.0), writes=[it.d])
                    lo_c = max(lo, TC)
                    hi_c = min(hi, T)
                    c.dma("pool", it.t[:, lo_c - lo:hi_c - lo], UT[ch, lo_c:hi_c], writes=[it.d])
                ins.append(it)
                accs.append(acc_r.next())
            for j in range(31):
                for cc in range(4):
                    it, ac = ins[cc], accs[cc]
                    if j == 0:
                        c.op("dve", lambda e: e.tensor_scalar(out=ac.t[:, 0:gs], in0=it.t[:, 0:gs], scalar1=cw.t[:, cc, 0:1],
                                                              scalar2=cbias.t[:, cc:cc + 1], op0=ALU.mult, op1=ALU.add),
                             reads=[it.d, cw.d, cbias.d], writes=[ac.d])
                    else:
                        c.op("dve", lambda e: e.scalar_tensor_tensor(out=ac.t[:, 0:gs], in0=it.t[:, j:j + gs], scalar=cw.t[:, cc, j:j + 1],
                                                                     in1=ac.t[:, 0:gs], op0=ALU.mult, op1=ALU.add),
                             reads=[it.d, cw.d, ac.d], writes=[ac.d])
                yield
            p1 = ps1.next()
            p2 = ps2.next()
            for cc in range(4):
                sq = sq_r.next()
                c.op("act", lambda e: e.activation(out=sq.t[:, 0:gs], in_=accs[cc].t[:, 0:gs], func=AF.Square), reads=[accs[cc].d], writes=[sq.d])
                c.op("pe", lambda e: e.matmul(p1.t[:, 0:gs], lhsT=ones32.t[:], rhs=accs[cc].t[:, 0:gs], start=(cc == 0), stop=(cc == 3)),
                     reads=[ones32.d, accs[cc].d], writes=[p1.d])
                c.op("pe", lambda e: e.matmul(p2.t[:, 0:gs], lhsT=ones32.t[:], rhs=sq.t[:, 0:gs], start=(cc == 0), stop=(cc == 3)),
                     reads=[ones32.d, sq.d], writes=[p2.d])
            mean = mean_r.next()
            rstd = rstd_r.next()
            c.op("act", lambda e: e.activation(out=mean.t[:, 0:gs], in_=p1.t[:, 0:gs], func=AF.Copy, scale=1.0 / 512), reads=[p1.d], writes=[mean.d])
            c.op("dve", lambda e: e.tensor_tensor(out=rstd.t[:, 0:gs], in0=mean.t[:, 0:gs], in1=mean.t[:, 0:gs], op=ALU.mult),
                 reads=[mean.d], writes=[rstd.d])
            c.op("dve", lambda e: e.scalar_tensor_tensor(out=rstd.t[:, 0:gs], in0=p2.t[:, 0:gs], scalar=1.0 / 512, in1=rstd.t[:, 0:gs],
                                                         op0=ALU.mult, op1=ALU.subtract), reads=[p2.d, rstd.d], writes=[rstd.d])
            c.op("act", lambda e: e.activation(out=rstd.t[:, 0:gs], in_=rstd.t[:, 0:gs], func=AF.Sqrt, bias=epsb.t[:, 0:1], scale=1.0),
                 reads=[rstd.d, epsb.d], writes=[rstd.d])
            c.op("dve", lambda e: e.reciprocal(out=rstd.t[:, 0:gs], in_=rstd.t[:, 0:gs]), reads=[rstd.d], writes=[rstd.d])
            for cc in range(4):
                ac = accs[cc]
                c.op("dve", lambda e: e.tensor_tensor(out=ac.t[:, 0:gs], in0=ac.t[:, 0:gs], in1=mean.t[:, 0:gs], op=ALU.subtract),
                     reads=[ac.d, mean.d], writes=[ac.d])
                c.op("dve", lambda e: e.tensor_tensor(out=ac.t[:, 0:gs], in0=ac.t[:, 0:gs], in1=rstd.t[:, 0:gs], op=ALU.mult),
                     reads=[ac.d, rstd.d], writes=[ac.d])
                o = ob_r.next()
                c.op("act", lambda e: e.activation(out=o.t[:, 0:gs], in_=ac.t[:, 0:gs], func=AF.Silu, bias=lb.t[:, cc:cc + 1],
                                                   scale=lg.t[:, cc:cc + 1]), reads=[ac.d, lb.d, lg.d], writes=[o.d])
                c.dma("sp", BR[1, cc * 128:(cc + 1) * 128, g0:g0 + gs], o.t[:, 0:gs], reads=[o.d])

    def phase_fnet(l, ph):
        AB = ph.sb([128, NT, 4, 256], BF16)
        ft_r = ph.sb([128, T], BF16, n=2)
        cs = ph.sb([128, 256], BF16)
        c.dma("pool", cs.t[:], ccsc[:, :], writes=[cs.d])
        pp = ph.ps([128, 512], F32, n=6)
        for g in range(4):
            ft = ft_r.next()
            c.dma("pool", ft.t[:], FT[g * 128:(g + 1) * 128, :], writes=[ft.d])
            for i in range(NT):
                if i % 4 == 0:
                    yield
                p = pp.next()
                c.op("pe", lambda e: e.matmul(p.t[:, 0:256], lhsT=ft.t[:, i * 128:(i + 1) * 128], rhs=cs.t[:], start=True, stop=True),
                     reads=[ft.d, cs.d], writes=[p.d])
                c.op("act" if g % 2 else "dve",
                     (lambda e: e.copy(out=AB.t[:, i, g, :], in_=p.t[:, 0:256])) if g % 2 else
                     (lambda e: e.tensor_copy(out=AB.t[:, i, g, :], in_=p.t[:, 0:256])), reads=[p.d], writes=[AB.d])
        cblk = ph.sb([128, 512], BF16, n=4)
        sblk = ph.sb([128, 512], BF16, n=4)
        ob_r = ph.sb([128, 512], BF16, n=4)
        for gi, (g0, gs) in enumerate(GROUPS):
            if gi == 0:
                tiles = [0, 1]
                cm, sm, col0, nrm = dft_c256, dft_s256, 0, 1.0 / math.sqrt(TC * 128)
            else:
                tiles = list(range(2, NT))
                cm, sm, col0, nrm = dft_c, dft_s, g0 - TC, 1.0 / math.sqrt(TL * 128)
            ps_ = [pp.next() for _ in range(4)]
            for si, i in enumerate(tiles):
                cb_ = cblk.next()
                sb_ = sblk.next()
                c.dma("pool", cb_.t[:, 0:gs], cm[si * 128:(si + 1) * 128, col0:col0 + gs], writes=[cb_.d])
                c.dma("pool", sb_.t[:, 0:gs], sm[si * 128:(si + 1) * 128, col0:col0 + gs], writes=[sb_.d])
                for g in range(4):
                    c.op("pe", lambda e: e.matmul(ps_[g].t[:, 0:gs], lhsT=AB.t[:, i, g, 0:128], rhs=cb_.t[:, 0:gs], start=(si == 0), stop=False),
                         reads=[AB.d, cb_.d], writes=[ps_[g].d])
                    c.op("pe", lambda e: e.matmul(ps_[g].t[:, 0:gs], lhsT=AB.t[:, i, g, 128:256], rhs=sb_.t[:, 0:gs], start=False,
                                                  stop=(si == len(tiles) - 1)), reads=[AB.d, sb_.d], writes=[ps_[g].d])
                yield
            for g in range(4):
                o = ob_r.next()
                c.op("act", lambda e: e.activation(out=o.t[:, 0:gs], in_=ps_[g].t[:, 0:gs], func=AF.Copy, scale=nrm), reads=[ps_[g].d], writes=[o.d])
                c.dma("sp", BR[2, g * 128:(g + 1) * 128, g0:g0 + gs], o.t[:, 0:gs], reads=[o.d])

    def post_norm_tile(i, src, src_reads, gate, lng, lnb, rr, x_r, t_r, final=False):
        xt = x_r.next()
        c.dma("pool", xt.t[:], X[i * 128:(i + 1) * 128, :], writes=[xt.d])
        tt = t_r.next()
        c.op("dve", lambda e: e.tensor_tensor(out=tt.t[:], in0=src, in1=gate.t[:], op=ALU.mult), reads=list(src_reads) + [gate.d], writes=[tt.d])
        c.op("dve", lambda e: e.scalar_tensor_tensor(out=tt.t[:], in0=xt.t[:], scalar=ALPHA, in1=tt.t[:], op0=ALU.mult, op1=ALU.add),
             reads=[xt.d, tt.d], writes=[tt.d])
        mv, rs = ln_stats(None, tt, rr)
        c.op("dve", lambda e: e.tensor_scalar(out=tt.t[:], in0=tt.t[:], scalar1=mv.t[:, 0:1], scalar2=rs.t[:, 0:1], op0=ALU.subtract,
                                              op1=ALU.mult), reads=[tt.d, mv.d, rs.d], writes=[tt.d])
        c.op("dve", lambda e: e.tensor_tensor(out=tt.t[:], in0=tt.t[:], in1=lng.t[:], op=ALU.mult), reads=[tt.d, lng.d], writes=[tt.d])
        c.op("dve", lambda e: e.tensor_tensor(out=tt.t[:], in0=tt.t[:], in1=lnb.t[:], op=ALU.add), reads=[tt.d, lnb.d], writes=[tt.d])
        c.dma("sp", X[i * 128:(i + 1) * 128, :], tt.t[:], reads=[tt.d])

    def load_pn_consts(ph, l, gate_off, g_ap, b_ap):
        gate = [ph.sb([128, D], F32) for _ in range(2)]
        for w in range(2):
            c.dma("pool", gate[w].t[:], MOD[l % 2, w, :, gate_off:gate_off + D], writes=[gate[w].d])
        lng = ph.sb([128, D], F32)
        lnb = ph.sb([128, D], F32)
        c.dma("pool", lng.t[:], bcast_rows(g_ap[l:l + 1, :]), writes=[lng.d])
        c.dma("pool", lnb.t[:], bcast_rows(b_ap[l:l + 1, :]), writes=[lnb.d])
        return gate, lng, lnb

    def phase_merge(l):
        ph = Phase(c)
        wb = ph.sb([128, 16, D], BF16)
        wo = ph.sb([128, 8, D], BF16)
        for k in range(4):
            c.dma("pool", wb.t[:, k * 4:(k + 1) * 4, :], w_branch[l, k].rearrange("(cc p) n -> p cc n", p=128), writes=[wb.d])
        c.dma("pool", wo.t[:], w_out[l].rearrange("(k p) n -> p k n", p=128), writes=[wo.d])
        gate, lng, lnb = load_pn_consts(ph, l, 2 * D, ln1_g, ln1_b)
        rr = ln_rings(ph)
        x_r = ph.sb([128, D], F32, n=2)
        t_r = ph.sb([128, D], F32, n=2)
        br_r = ph.sb([128, 16, 512], BF16, n=2)
        gt_r = ph.sb([128, 8, 512], BF16, n=4)
        mT_r = ph.sb([128, 8, 512], BF16, n=2)
        tmp_r = ph.sb([128, 512], BF16, n=6)
        pp = ph.ps([128, 512], F32, n=3)
        pacc = ph.ps([128, 512], F32, n=2)
        po = ph.ps([128, D], F32, n=1)
        deferred = []
        for gi, (g0, gs) in enumerate(GROUPS):
            br = br_r.next()
            for k in range(4):
                c.dma("pool", br.t[:, k * 4:(k + 1) * 4, 0:gs], BR[k, :, g0:g0 + gs].rearrange("(cc p) t -> p cc t", p=128), writes=[br.d])
            mT = mT_r.next()
            gt_bufs = {}
            tms = []
            for dch in range(8):
                pm_ = pacc.next()
                for k in range(4):
                    if dch == 0:
                        gt = gt_r.next()
                        c.dma("pool", gt.t[:, :, 0:gs],
                              GATES[k * D:(k + 1) * D, g0:g0 + gs].rearrange("(dc p) t -> p dc t", p=128), writes=[gt.d])
                        gt_bufs[k] = gt
                    gt = gt_bufs[k]
                    p = pp.next()
                    for cc in range(4):
                        c.op("pe", lambda e: e.matmul(p.t[:, 0:gs], lhsT=wb.t[:, k * 4 + cc, dch * 128:(dch + 1) * 128], rhs=br.t[:, k * 4 + cc, 0:gs],
                                                      start=(cc == 0), stop=(cc == 3)), reads=[wb.d, br.d], writes=[p.d])
                    tm = tmp_r.next()
                    c.op("dve", lambda e: e.tensor_tensor(out=tm.t[:, 0:gs], in0=p.t[:, 0:gs], in1=gt.t[:, dch, 0:gs], op=ALU.mult),
                         reads=[p.d, gt.d], writes=[tm.d])
                    tms.append((tm, k, dch, pm_))
                    if len(tms) > 1:
                        tm0, k0, dch0, pm0 = tms.pop(0)
                        c.op("pe", lambda e: e.matmul(pm0.t[:, 0:gs], lhsT=idb.t[:], rhs=tm0.t[:, 0:gs], start=(k0 == 0), stop=(k0 == 3)),
                             reads=[idb.d, tm0.d], writes=[pm0.d])
                        if k0 == 3:
                            c.op("act", lambda e: e.copy(out=mT.t[:, dch0, 0:gs], in_=pm0.t[:, 0:gs]), reads=[pm0.d], writes=[mT.d])
            while tms:
                tm0, k0, dch0, pm0 = tms.pop(0)
                c.op("pe", lambda e: e.matmul(pm0.t[:, 0:gs], lhsT=idb.t[:], rhs=tm0.t[:, 0:gs], start=(k0 == 0), stop=(k0 == 3)),
                     reads=[idb.d, tm0.d], writes=[pm0.d])
                if k0 == 3:
                    c.op("act", lambda e: e.copy(out=mT.t[:, dch0, 0:gs], in_=pm0.t[:, 0:gs]), reads=[pm0.d], writes=[mT.d])
            def finish(mT=mT, g0=g0, gs=gs):
                for sub in range(gs // 128):
                    i = g0 // 128 + sub
                    o = po.next()
                    for half in range(2):
                        for k in range(8):
                            c.op("pe", lambda e: e.matmul(o.t[:, half * 512:(half + 1) * 512], lhsT=mT.t[:, k, sub * 128:(sub + 1) * 128],
                                                          rhs=wo.t[:, k, half * 512:(half + 1) * 512], start=(k == 0), stop=(k == 7)),
                                 reads=[mT.d, wo.d], writes=[o.d])
                    post_norm_tile(i, o.t[:], [o.d], gate[1 if i < 2 else 0], lng, lnb, rr, x_r, t_r)

            if deferred:
                deferred.pop()()
            deferred.append(finish)
        while deferred:
            deferred.pop()()
        ph.close()

    def phase_moe(l):
        pm = Phase(c)
        AFFTM = pm.sb([128, NT, NEXP], F32)
        POS = pm.sb([128, NT, NEXP], F32)
        p12 = Phase(c)
        AFFT = p12.sb([NEXP, T], F32)
        phase_lnmod(l, 1, AFFT=AFFT, AFFTM=AFFTM)
        ph = Phase(c)
        lo = ph.sb([NEXP, 1], F32)
        hi = ph.sb([NEXP, 1], F32)
        mid = ph.sb([NEXP, 1], F32)
        cntb = ph.sb([NEXP, 1], F32)
        ge = ph.sb([NEXP, 1], F32)
        dlt = ph.sb([NEXP, 1], F32)
        scr = ph.sb([NEXP, TL], F32)
        mask = ph.sb([NEXP, T], F32)
        zer = ph.sb([NEXP, TL], F32)
        posm = ph.sb([NEXP, T], F32)
        c.op("pool", lambda e: e.memset(zer.t[:], 0.0), writes=[zer.d])
        for (a0, n, kk, base) in ((0, TC, 32, 512.0), (TC, TL, 512, 0.0)):
            c.op("dve", lambda e: e.memset(lo.t[:], 0.0), writes=[lo.d])
            c.op("dve", lambda e: e.memset(hi.t[:], 1.0), writes=[hi.d])
            for it in range(34):
                c.op("dve", lambda e: e.tensor_tensor(out=mid.t[:], in0=lo.t[:], in1=hi.t[:], op=ALU.add), reads=[lo.d, hi.d], writes=[mid.d])
                c.op("dve", lambda e: e.tensor_scalar(out=mid.t[:], in0=mid.t[:], scalar1=0.5, scalar2=None, op0=ALU.mult),
                     reads=[mid.d], writes=[mid.d])
                c.op("dve", lambda e: e.tensor_scalar(out=scr.t[:, 0:n], in0=AFFT.t[:, a0:a0 + n], scalar1=mid.t[:, 0:1], scalar2=None,
                                                      op0=ALU.is_ge, op1=ALU.add, accum_out=cntb.t[:]), reads=[AFFT.d, mid.d], writes=[scr.d, cntb.d])
                c.op("dve", lambda e: e.tensor_scalar(out=ge.t[:], in0=cntb.t[:], scalar1=float(kk) - 0.5, scalar2=None, op0=ALU.is_ge),
                     reads=[cntb.d], writes=[ge.d])
                c.op("dve", lambda e: e.tensor_tensor(out=dlt.t[:], in0=mid.t[:], in1=lo.t[:], op=ALU.subtract), reads=[mid.d, lo.d], writes=[dlt.d])
                c.op("dve", lambda e: e.tensor_tensor(out=dlt.t[:], in0=dlt.t[:], in1=ge.t[:], op=ALU.mult), reads=[dlt.d, ge.d], writes=[dlt.d])
                c.op("dve", lambda e: e.tensor_tensor(out=lo.t[:], in0=lo.t[:], in1=dlt.t[:], op=ALU.add), reads=[lo.d, dlt.d], writes=[lo.d])
                c.op("dve", lambda e: e.tensor_tensor(out=dlt.t[:], in0=hi.t[:], in1=mid.t[:], op=ALU.subtract), reads=[hi.d, mid.d], writes=[dlt.d])
                c.op("dve", lambda e: e.tensor_tensor(out=dlt.t[:], in0=dlt.t[:], in1=ge.t[:], op=ALU.mult), reads=[dlt.d, ge.d], writes=[dlt.d])
                c.op("dve", lambda e: e.tensor_tensor(out=hi.t[:], in0=mid.t[:], in1=dlt.t[:], op=ALU.add), reads=[mid.d, dlt.d], writes=[hi.d])
            c.op("dve", lambda e: e.tensor_scalar(out=mask.t[:, a0:a0 + n], in0=AFFT.t[:, a0:a0 + n], scalar1=lo.t[:, 0:1], scalar2=None,
                                                  op0=ALU.is_ge), reads=[AFFT.d, lo.d], writes=[mask.d])
            c.op("dve", lambda e: e.tensor_tensor_scan(out=posm.t[:, a0:a0 + n], data0=mask.t[:, a0:a0 + n], data1=zer.t[:, 0:n], initial=0.0,
                                                       op0=ALU.add, op1=ALU.add), reads=[mask.d, zer.d], writes=[posm.d])
            c.op("dve", lambda e: e.tensor_scalar(out=posm.t[:, a0:a0 + n], in0=posm.t[:, a0:a0 + n], scalar1=base, scalar2=None, op0=ALU.add),
                 reads=[posm.d], writes=[posm.d])
            c.op("dve", lambda e: e.tensor_tensor(out=posm.t[:, a0:a0 + n], in0=posm.t[:, a0:a0 + n], in1=mask.t[:, a0:a0 + n], op=ALU.mult),
                 reads=[posm.d, mask.d], writes=[posm.d])
            c.op("dve", lambda e: e.tensor_scalar(out=posm.t[:, a0:a0 + n], in0=posm.t[:, a0:a0 + n], scalar1=-1.0, scalar2=None, op0=ALU.add),
                 reads=[posm.d], writes=[posm.d])
        c.dma("sp", POSD.rearrange("i (e t) -> e i t", e=NEXP), posm.t[:].rearrange("e (i t) -> e i t", t=128), reads=[posm.d])
        ptp = ph.ps([128, NEXP], F32, n=2)
        for i in range(NT):
            p = ptp.next()
            c.op("pe", lambda e: e.transpose(out=p.t[:], in_=posm.t[:, i * 128:(i + 1) * 128], identity=idf.t[0:NEXP, 0:NEXP]),
                 reads=[posm.d, idf.d], writes=[p.d])
            c.op("act", lambda e: e.copy(out=POS.t[:, i, :], in_=p.t[:]), reads=[p.d], writes=[POS.d])
        ph.close()
        p12.close()
        WSLs = [pm.sb([128, 5], F32) for _ in range(NEXP)]
        IDXs = [pm.sb([128, 5], mybir.dt.int32) for _ in range(NEXP)]
        ph = Phase(c)
        ymoe_d = Dep()
        zt = ph.sb([128, D], F32)
        c.op("dve", lambda e: e.memset(zt.t[:], 0.0), writes=[zt.d])
        sc_state = {"prev": {}, "cur": {}, "ex": -1}

        def tok_add(dct):
            k_, v_ = c.last_tok
            if dct.get(k_, 0) < v_:
                dct[k_] = v_

        for i in range(NT):
            c.dma("sp", YMOE[i * 128:(i + 1) * 128, :], zt.t[:], reads=[zt.d])
            tok_add(sc_state["cur"])
        iota = ph.sb([128, 512], F32)
        c.op("pool", lambda e: e.iota(iota.t[:], pattern=[[1, 512]], base=0, channel_multiplier=0, allow_small_or_imprecise_dtypes=True),
             writes=[iota.d])
        iotac = ph.sb([128, 32], F32)
        c.op("pool", lambda e: e.iota(iotac.t[:], pattern=[[1, 32]], base=512, channel_multiplier=0, allow_small_or_imprecise_dtypes=True),
             writes=[iotac.d])
        wsp = ph.sb([128, 5, NT * NEXP], BF16)
        r1 = ph.sb([128, NT * NEXP], F32)
        r2 = ph.sb([128, NT * NEXP], F32)
        aff2 = AFFTM.t[:].rearrange("p i e -> p (i e)")
        c.op("dve", lambda e: e.tensor_copy(out=wsp.t[:, 0, :], in_=aff2), reads=[AFFTM.d], writes=[wsp.d])
        c.op("dve", lambda e: e.tensor_tensor(out=r1.t[:], in0=aff2, in1=wsp.t[:, 0, :], op=ALU.subtract), reads=[AFFTM.d, wsp.d], writes=[r1.d])
        c.op("dve", lambda e: e.tensor_copy(out=wsp.t[:, 1, :], in_=r1.t[:]), reads=[r1.d], writes=[wsp.d])
        c.op("dve", lambda e: e.tensor_tensor(out=r2.t[:], in0=r1.t[:], in1=wsp.t[:, 1, :], op=ALU.subtract), reads=[r1.d, wsp.d], writes=[r2.d])
        c.op("dve", lambda e: e.tensor_copy(out=wsp.t[:, 2, :], in_=r2.t[:]), reads=[r2.d], writes=[wsp.d])
        c.op("pool", lambda e: e.iota(r1.t[:], pattern=[[1, NT], [0, NEXP]], base=0, channel_multiplier=0, allow_small_or_imprecise_dtypes=True),
             reads=[r1.d], writes=[r1.d])
        c.op("dve", lambda e: e.tensor_copy(out=wsp.t[:, 3, :], in_=r1.t[:]), reads=[r1.d], writes=[wsp.d])
        c.op("pool", lambda e: e.iota(r2.t[:], pattern=[[0, NT * NEXP]], base=0, channel_multiplier=1, allow_small_or_imprecise_dtypes=True),
             reads=[r2.d], writes=[r2.d])
        c.op("dve", lambda e: e.tensor_copy(out=wsp.t[:, 4, :], in_=r2.t[:]), reads=[r2.d], writes=[wsp.d])
        sel = ph.sb([128, 32 * 512 + 64], BF16)
        wsl5_r = ph.sb([128, 5, 5], F32, n=2)
        idxf_r = ph.sb([128, 5], F32, n=2)
        pws = ph.ps([128, 512], F32, n=1)
        JC = [(0, 128), (128, 128), (256, 128), (384, 128), (512, 32)]
        for w5 in wsl5_r.bufs:
            c.op("dve", lambda e: e.memset(w5.t[:], 0.0), writes=[w5.d])

        def sel_ap(i):
            if i < 2:
                return sel.t[:, 32 * 512 + i * 32:32 * 512 + (i + 1) * 32]
            return sel.t[:, (i - 2) * 512:(i - 1) * 512]

        def emit_sel(ex, tiles):
            for i in tiles:
                src = iotac if i < 2 else iota
                c.op("dve", lambda e: e.tensor_scalar(out=sel_ap(i), in0=src.t[:], scalar1=POS.t[:, i, ex:ex + 1], scalar2=None, op0=ALU.is_equal),
                     reads=[src.d, POS.d], writes=[sel.d])

        def emit_pre(ex, with_sel=True):
            if with_sel:
                emit_sel(ex, range(NT))
            w5 = wsl5_r.next()
            for jc, (j0, jn) in enumerate(JC):
                p = pws.next()
                tl = [0, 1] if jc == 4 else list(range(2, NT))
                for ti, i in enumerate(tl):
                    sa = sel_ap(i)
                    lhs = sa[:, 0:32] if jc == 4 else sa[:, j0:j0 + jn]
                    c.op("pe", lambda e: e.matmul(p.t[0:jn, 0:5], lhsT=lhs, rhs=wsp.t[:, :, i * NEXP + ex], start=(ti == 0), stop=(ti == len(tl) - 1)),
                         reads=[sel.d, wsp.d], writes=[p.d])
                c.op("act", lambda e: e.copy(out=w5.t[0:jn, jc, :], in_=p.t[0:jn, 0:5]), reads=[p.d], writes=[w5.d])
            wl, ix = WSLs[ex], IDXs[ex]
            c.op("dve", lambda e: e.tensor_tensor(out=wl.t[:], in0=w5.t[:, :, 2], in1=w5.t[:, :, 1], op=ALU.add), reads=[w5.d], writes=[wl.d])
            c.op("dve", lambda e: e.tensor_tensor(out=wl.t[:], in0=wl.t[:], in1=w5.t[:, :, 0], op=ALU.add), reads=[wl.d, w5.d], writes=[wl.d])
            xf = idxf_r.next()
            c.op("dve", lambda e: e.scalar_tensor_tensor(out=xf.t[:], in0=w5.t[:, :, 3], scalar=128.0, in1=w5.t[:, :, 4], op0=ALU.mult, op1=ALU.add),
                 reads=[w5.d], writes=[xf.d])
            c.op("dve", lambda e: e.tensor_copy(out=ix.t[:], in_=xf.t[:]), reads=[xf.d], writes=[ix.d])

        emit_pre(0)
        emit_pre(1)
        xg_r = ph.sb([128, 5, D], BF16, n=2)
        xeT = ph.sb([128, 8, NSLOT], BF16)
        heT = ph.sb([128, NFC, NSLOT], BF16)
        wgu_r = ph.sb([128, 2, 8, 512], BF16, n=2)
        wd_r = ph.sb([128, NFC, 512], BF16, n=4)
        ptx = ph.ps([128, 1024], BF16, n=1)
        pg_ = ph.ps([128, 512], F32, n=2)
        pgc = ph.ps([128, 32], F32, n=1)
        pup = ph.ps([128, 512], F32, n=2)
        pupc = ph.ps([128, 32], F32, n=1)
        sg_r = ph.sb([128, NSLOT], F32, n=2)
        ye_r = ph.sb([128, D], BF16, n=10)

        def emit_gather(ex):
            xg = xg_r.next()
            for jc, (j0, jn) in enumerate(JC):
                c.idma(xg.t[0:jn, jc, :], H2[:, :], IDXs[ex].t[0:jn, jc:jc + 1], gather=True, reads=[IDXs[ex].d], writes=[xg.d])
            return xg

        pending = []
        xg_next = emit_gather(0)
        for ex in range(NEXP):
            xg = xg_next
            wds = []
            for half in range(2):
                wd = wd_r.next()
                c.dma("pool", wd.t[:], w_e_down[l, ex, :, half * 512:(half + 1) * 512].rearrange("(f p) n -> p f n", p=128), writes=[wd.d])
                wds.append(wd)
            if ex + 1 < NEXP:
                xg_next = emit_gather(ex + 1)
            for dch in range(8):
                p = ptx.next()
                for jc, (j0, jn) in enumerate(JC):
                    c.op("pe", lambda e: e.transpose(out=p.t[:, j0:j0 + jn], in_=xg.t[0:jn, jc, dch * 128:(dch + 1) * 128], identity=idb.t[0:jn, 0:jn]),
                         reads=[xg.d, idb.d], writes=[p.d])
                c.op("act", lambda e: e.copy(out=xeT.t[:, dch, :], in_=p.t[:, 0:NSLOT]), reads=[p.d], writes=[xeT.d])
            for fc in range(NFC):
                if fc % 4 == 0:
                    wgu = wgu_r.next()
                    nb = min(512, EFF - fc * 128)
                    c.dma("pool", wgu.t[:, 0, :, 0:nb], w_e_gate[l, ex, :, fc * 128:fc * 128 + nb].rearrange("(k p) n -> p k n", p=128), writes=[wgu.d])
                    c.dma("pool", wgu.t[:, 1, :, 0:nb], w_e_up[l, ex, :, fc * 128:fc * 128 + nb].rearrange("(k p) n -> p k n", p=128), writes=[wgu.d])
                if fc == 8:
                    for fn_ in pending:
                        fn_()
                    pending = []
                fo = (fc % 4) * 128
                if ex + 2 < NEXP:
                    emit_sel(ex + 2, range(fc * 4, min(NT, fc * 4 + 4)))
                pga, pgb, pua, pub = pg_.next(), pgc.next(), pup.next(), pupc.next()
                for (pa, pb, wi) in ((pga, pgb, 0), (pua, pub, 1)):
                    for k in range(8):
                        c.op("pe", lambda e: e.matmul(pa.t[:], lhsT=wgu.t[:, wi, k, fo:fo + 128], rhs=xeT.t[:, k, 0:512], start=(k == 0), stop=(k == 7)),
                             reads=[wgu.d, xeT.d], writes=[pa.d])
                    for k in range(8):
                        c.op("pe", lambda e: e.matmul(pb.t[:], lhsT=wgu.t[:, wi, k, fo:fo + 128], rhs=xeT.t[:, k, 512:544], start=(k == 0), stop=(k == 7)),
                             reads=[wgu.d, xeT.d], writes=[pb.d])
                sg = sg_r.next()
                c.op("act", lambda e: e.activation(out=sg.t[:, 0:512], in_=pga.t[:], func=AF.Silu), reads=[pga.d], writes=[sg.d])
                c.op("act", lambda e: e.activation(out=sg.t[:, 512:544], in_=pgb.t[:], func=AF.Silu), reads=[pgb.d], writes=[sg.d])
                c.op("dve", lambda e: e.tensor_tensor(out=heT.t[:, fc, 0:512], in0=pua.t[:], in1=sg.t[:, 0:512], op=ALU.mult),
                     reads=[pua.d, sg.d], writes=[heT.d])
                c.op("dve", lambda e: e.tensor_tensor(out=heT.t[:, fc, 512:544], in0=pub.t[:], in1=sg.t[:, 512:544], op=ALU.mult),
                     reads=[pub.d, sg.d], writes=[heT.d])
            if ex + 2 < NEXP:
                emit_pre(ex + 2, with_sel=False)
            for jc, (j0, jn) in enumerate(JC):
                ye = ye_r.next()
                for half in range(2):
                    wd = wds[half]
                    p = pg_.next() if half == 0 else pup.next()
                    for fc in range(NFC):
                        c.op("pe", lambda e: e.matmul(p.t[0:jn, :], lhsT=heT.t[:, fc, j0:j0 + jn], rhs=wd.t[:, fc, :],
                                                      start=(fc == 0), stop=(fc == NFC - 1)), reads=[heT.d, wd.d], writes=[p.d])
                    c.op("act", lambda e: e.activation(out=ye.t[0:jn, half * 512:(half + 1) * 512], in_=p.t[0:jn, :], func=AF.Copy,
                                                       scale=WSLs[ex].t[0:jn, jc:jc + 1]), reads=[p.d, WSLs[ex].d], writes=[ye.d])

                def scat(ye=ye, jn=jn, ex=ex, jc=jc):
                    if sc_state["ex"] != ex:
                        sc_state["prev"], sc_state["cur"], sc_state["ex"] = sc_state["cur"], {}, ex
                    dtmp = Dep()
                    dtmp.r = dict(sc_state["prev"])
                    c.idma(YMOE[:, :], ye.t[0:jn, :], IDXs[ex].t[0:jn, jc:jc + 1], gather=False, reads=[ye.d, IDXs[ex].d], writes=[dtmp], add=True)
                    tok_add(sc_state["cur"])
                pending.append(scat)
        for fn_ in pending:
            fn_()
        ph.close()
        pm.close()
        def phase_m5(l, ph):
            gate, lng, lnb = load_pn_consts(ph, l, 5 * D, ln2_g, ln2_b)
            rr = ln_rings(ph)
            x_r = ph.sb([128, D], F32, n=3)
            t_r = ph.sb([128, D], F32, n=3)
            ym_r = ph.sb([128, D], F32, n=3)
            for i in range(NT):
                ym = ym_r.next()
                c.dma("pool", ym.t[:], YMOE[i * 128:(i + 1) * 128, :], writes=[ym.d])
                post_norm_tile(i, ym.t[:], [ym.d], gate[1 if i < 2 else 0], lng, lnb, rr, x_r, t_r)
                yield

        specs = [(phase_m5, NT, l)]
        if l + 1 < n_layers:
            specs.append((phase_mod, 12, l + 1))
        interleave(specs)

    def interleave(specs):
        phs = [Phase(c) for _ in specs]
        act = [[fn(la, ph), tot, 0] for (fn, tot, la), ph in zip(specs, phs)]
        while act:
            a = min(act, key=lambda a: a[2] / a[1])
            try:
                next(a[0])
                a[2] += 1
            except StopIteration:
                act.remove(a)
        c.barrier()
        for ph in reversed(phs):
            ph.es.close()

    interleave([(phase_mod, 12, 0)])
    for l in range(n_layers):
        if stop_after == "mod":
            break
        ph = Phase(c)
        HT = ph.sb([128, 8, T], BF16)
        phase_lnmod(l, 0, HT=HT)
        phase_win(l, HT)
        ph.close()
        if stop_after == "win":
            break
        phase_qkv(l)
        if stop_after == "qkv":
            break
        interleave([(phase_attn, 8 * (2 + 8 * NT), l), (phase_conf, 9 * 31, l)])
        if stop_after == "lru":
            break
        interleave([(phase_lru, 4 * 19, l), (phase_fnet, NT + 2 + 8 * 32, l)])
        if stop_after == "fnet":
            break
        phase_merge(l)
        if stop_after == "merge":
            break
        phase_moe(l)

    c.barrier()
    for i in range(8):
        c.dma("sp", out_ap[512 * i:512 * (i + 1), :], X[TC + 512 * i:TC + 512 * (i + 1), :])
    c.barrier()
    glob.es.close()
    c.close()
    dbg = {"X": X, "BR": BR, "MOD": MOD, "CKVN": CKVN, "CQN": CQN, "KT": KT, "QT": QT, "VV": VV, "XBT": XBT, "UT": UT, "FT": FT,
           "GBT": GBT, "GATES": GATES, "H2": H2, "POSD": POSD, "YE": YE, "YMOE": YMOE}
    return nc, c, dbg


_CONST = {}


def host_consts():
    if _CONST:
        return _CONST
    bf = ml_dtypes.bfloat16
    inv = (10000.0 ** (-np.arange(8, dtype=np.float32) / 8)).astype(np.float32)
    tt = np.arange(TL)
    row = (tt // 64).astype(np.float32)
    col = (tt % 64).astype(np.float32)
    ang_r = row[None, :] * inv[:, None]
    ang_c = col[None, :] * inv[:, None]
    cosf = np.ones((96, T), np.float32)
    sinf = np.zeros((96, T), np.float32)
    for part, ang in ((0, ang_r), (1, ang_c)):
        for half in range(2):
            r0 = 64 + part * 16 + half * 8
            cosf[r0:r0 + 8, TC:] = np.cos(ang)
            sinf[r0:r0 + 8, TC:] = np.sin(ang)
    _CONST["rope_cos"] = cosf
    _CONST["rope_sin"] = sinf

    def dft(n):
        k = np.arange(n, dtype=np.int64)
        m = (k[:, None] * k[None, :]) % n
        a = 2.0 * np.pi * m.astype(np.float64) / n
        return np.cos(a), np.sin(a)

    cL, sL = dft(TL)
    _CONST["dft_c"] = cL.astype(np.float32).astype(bf)
    _CONST["dft_s"] = (-sL).astype(np.float32).astype(bf)
    c2, s2 = dft(TC)
    _CONST["dft_c256"] = c2.astype(np.float32).astype(bf)
    _CONST["dft_s256"] = (-s2).astype(np.float32).astype(bf)
    cc, sc = dft(128)
    _CONST["ccsc"] = np.concatenate([cc, sc], axis=1).astype(np.float32).astype(bf)
    return _CONST


_PROG = {}


def kernel(**inputs):
    if "nc" not in _PROG:
        _PROG["nc"] = build_program()[0]
    nc = _PROG["nc"]
    cst = host_consts()
    f32 = lambda a: np.ascontiguousarray(np.asarray(a, dtype=np.float32))
    shared = {}
    for k in ("ada_w", "ada_b", "w_in", "q_norm_g", "kv_norm_g", "cv_w", "cv_b", "cv_ln_g", "cv_ln_b", "lru_conv_w", "lru_conv_b",
              "lru_wa", "lru_ba", "lru_wx", "lru_bx", "lru_lambda", "w_branch", "w_out", "ln1_g", "ln1_b", "w_router",
              "w_e_gate", "w_e_up", "w_e_down", "ln2_g", "ln2_b"):
        shared[k] = f32(inputs[k])
    shared["w_uq"] = f32(inputs["w_uq"]).reshape(DEPTH, 384, 768)
    shared["w_ukv"] = f32(inputs["w_ukv"]).reshape(DEPTH, 256, 1024)
    shared.update(cst)
    x = f32(inputs["x"])
    ctx = f32(inputs["ctx"])
    cc = f32(inputs["c"])
    c_ctx = f32(inputs["c_ctx"])
    in_maps = []
    for core in range(8):
        b = core % 4
        m = dict(shared)
        m["x"] = x[b]
        m["ctx"] = ctx[b]
        m["cvec"] = np.ascontiguousarray(np.stack([cc[b], c_ctx], axis=0))
        in_maps.append(m)
    res = run_bass_kernel_spmd(nc, in_maps, core_ids=list(range(8)))
    out = np.stack([np.asarray(res.results[b]["out"], dtype=np.float32) for b in range(4)], axis=0)
    return out
```
